# Optimizing a Trainium2 kernel written in Bass

```python
import math
import jax
import jax.numpy as jnp
from jax import lax
import numpy as np

D_MODEL = 1024
BATCH = 8
SEQ = 4096
DEPTH = 1

HEAD_DIM = 64
ROPE_THETA = 10000.0
NORM_EPS = 1e-6
NEG_INF = -1e30
FORCE_SCORE = 1e9

DIFF_HEADS = 4
DIFF_VDIM = 2 * HEAD_DIM
DIFF_Q_BLOCK = 128

NSA_HEADS = 8
NSA_KV_GROUPS = 2
NSA_HPG = NSA_HEADS // NSA_KV_GROUPS
CMP_LEN = 32
CMP_STRIDE = 16
CMP_HIDDEN = 4 * HEAD_DIM
SEL_LEN = 64
SEL_TOPK = 16
WINDOW = 512
NSA_Q_BLOCK = 64

D_FF = int(math.ceil(8 * D_MODEL / 3 / 256)) * 256

DIFF_QK = DIFF_HEADS * 2 * HEAD_DIM
DIFF_V = DIFF_HEADS * DIFF_VDIM
NSA_Q = NSA_HEADS * HEAD_DIM
NSA_KV = NSA_KV_GROUPS * HEAD_DIM
NSA_GATES = 3 * NSA_HEADS
MERGE_GATES = 2 * D_MODEL
IN_SPLITS = (DIFF_QK, DIFF_QK, DIFF_V, NSA_Q, NSA_KV, NSA_KV, NSA_KV, NSA_KV, NSA_KV, NSA_KV, NSA_GATES, MERGE_GATES)
IN_DIM = sum(IN_SPLITS)
A_WIDTH = DIFF_V
B_WIDTH = NSA_HEADS * HEAD_DIM

kernel_name = "hybrid_diffattn_nsa_block"


def lambda_init(layer):
    return 0.8 - 0.6 * math.exp(-0.3 * layer)


def rms_norm(x, g):
    xf = x.astype(jnp.float32)
    y = xf * lax.rsqrt(jnp.mean(xf * xf, axis=-1, keepdims=True) + NORM_EPS)
    return (y * g.astype(jnp.float32)).astype(x.dtype)


def rope_tables(positions, dim):
    inv_freq = 1.0 / (ROPE_THETA ** (jnp.arange(0, dim, 2, dtype=jnp.float32) / dim))
    ang = positions.astype(jnp.float32)[..., None] * inv_freq
    return jnp.cos(ang), jnp.sin(ang)


def apply_rope(x, cos, sin):
    half = x.shape[-1] // 2
    xf = x.astype(jnp.float32)
    x1, x2 = xf[..., :half], xf[..., half:]
    return jnp.concatenate([x1 * cos - x2 * sin, x2 * cos + x1 * sin], axis=-1).astype(x.dtype)


def differential_attention(q, k, v, lam):
    B, S, H, _, dh = q.shape
    nblk = S // DIFF_Q_BLOCK
    scale = dh ** -0.5
    kpos = jnp.arange(S)

    def one_block(i):
        s0 = i * DIFF_Q_BLOCK
        qb = lax.dynamic_slice_in_dim(q, s0, DIFF_Q_BLOCK, axis=1)
        s = jnp.einsum('bqhcd,bkhcd->bhcqk', qb, k).astype(jnp.float32) * scale
        qpos = s0 + jnp.arange(DIFF_Q_BLOCK)
        causal = kpos[None, :] <= qpos[:, None]
        p = jax.nn.softmax(jnp.where(causal, s, NEG_INF), axis=-1)
        a = p[:, :, 0] - lam * p[:, :, 1]
        return jnp.einsum('bhqk,bkhe->bqhe', a.astype(v.dtype), v)

    o = lax.map(one_block, jnp.arange(nblk))
    return o.transpose(1, 0, 2, 3, 4).reshape(B, S, H, v.shape[-1])


def compress(x, pe, w1, w2):
    B, S, G, dh = x.shape
    nc = (S - CMP_LEN) // CMP_STRIDE + 1
    idx = np.arange(nc)[:, None] * CMP_STRIDE + np.arange(CMP_LEN)[None, :]
    blk = x[:, idx] + pe[None, None, :, None, :]
    blk = blk.transpose(0, 1, 3, 2, 4).reshape(B, nc, G, CMP_LEN * dh)
    return jax.nn.silu(blk @ w1) @ w2


def nsa_attention(q, kc, vc, ks, vs, kw, vw):
    B, S, G, hpg, dh = q.shape
    nc = kc.shape[1]
    nsb = S // SEL_LEN
    topk = min(SEL_TOPK, nsb)
    nblk = S // NSA_Q_BLOCK
    scale = dh ** -0.5
    cmp_end = np.arange(nc) * CMP_STRIDE + CMP_LEN - 1
    ci = np.arange(nc)[:, None] * CMP_STRIDE
    sj = np.arange(nsb)[None, :] * SEL_LEN
    overlap = jnp.asarray(((ci < sj + SEL_LEN) & (ci + CMP_LEN > sj)).astype(np.float32))
    kb = ks.reshape(B, nsb, SEL_LEN, G, dh).transpose(0, 3, 1, 2, 4)
    vb = vs.reshape(B, nsb, SEL_LEN, G, dh).transpose(0, 3, 1, 2, 4)
    kw_pad = jnp.pad(kw, ((0, 0), (WINDOW, 0), (0, 0), (0, 0)))
    vw_pad = jnp.pad(vw, ((0, 0), (WINDOW, 0), (0, 0), (0, 0)))
    b_idx = jnp.arange(B)[:, None, None, None]
    g_idx = jnp.arange(G)[None, None, :, None]
    blk_ids = jnp.arange(nsb)
    sel_off = jnp.arange(SEL_LEN)

    def one_block(i):
        s0 = i * NSA_Q_BLOCK
        qpos = s0 + jnp.arange(NSA_Q_BLOCK)
        qb = lax.dynamic_slice_in_dim(q, s0, NSA_Q_BLOCK, axis=1)
        sc = jnp.einsum('bqghd,bcgd->bqghc', qb, kc).astype(jnp.float32) * scale
        cm = (cmp_end[None, :] <= qpos[:, None])[None, :, None, None, :]
        pc = jax.nn.softmax(jnp.where(cm, sc, NEG_INF), axis=-1) * cm
        o_c = jnp.einsum('bqghc,bcgd->bqghd', pc.astype(vc.dtype), vc)
        imp = jnp.einsum('bqgc,cn->bqgn', pc.sum(axis=3), overlap)
        qblk = qpos // SEL_LEN
        forced = (blk_ids[None, :] == 0) | (blk_ids[None, :] == qblk[:, None]) | (blk_ids[None, :] == qblk[:, None] - 1)
        causal_blk = blk_ids[None, :] <= qblk[:, None]
        imp = jnp.where(forced[None, :, None, :], FORCE_SCORE, imp)
        imp = jnp.where(causal_blk[None, :, None, :], imp, NEG_INF)
        _, idx = lax.top_k(imp, topk)
        ksel = kb[b_idx, g_idx, idx]
        vsel = vb[b_idx, g_idx, idx]
        ss = jnp.einsum('bqghd,bqgnld->bqghnl', qb, ksel).astype(jnp.float32) * scale
        kp = idx[..., None] * SEL_LEN + sel_off
        smask = (kp <= qpos[None, :, None, None, None])[:, :, :, None]
        ss = jnp.where(smask, ss, NEG_INF).reshape(B, NSA_Q_BLOCK, G, hpg, topk * SEL_LEN)
        ps = jax.nn.softmax(ss, axis=-1).reshape(B, NSA_Q_BLOCK, G, hpg, topk, SEL_LEN)
        o_s = jnp.einsum('bqghnl,bqgnld->bqghd', ps.astype(vsel.dtype), vsel)
        kwb = lax.dynamic_slice_in_dim(kw_pad, s0, NSA_Q_BLOCK + WINDOW, axis=1)
        vwb = lax.dynamic_slice_in_dim(vw_pad, s0, NSA_Q_BLOCK + WINDOW, axis=1)
        kpos = s0 - WINDOW + jnp.arange(NSA_Q_BLOCK + WINDOW)
        dist = qpos[:, None] - kpos[None, :]
        wmask = ((dist >= 0) & (dist < WINDOW) & (kpos[None, :] >= 0))[None, :, None, None, :]
        sw = jnp.einsum('bqghd,bkgd->bqghk', qb, kwb).astype(jnp.float32) * scale
        pw = jax.nn.softmax(jnp.where(wmask, sw, NEG_INF), axis=-1)
        o_w = jnp.einsum('bqghk,bkgd->bqghd', pw.astype(vwb.dtype), vwb)
        return o_c, o_s, o_w

    o_c, o_s, o_w = lax.map(one_block, jnp.arange(nblk))

    def to_seq(o):
        return o.transpose(1, 0, 2, 3, 4, 5).reshape(B, S, G * hpg, dh)

    return to_seq(o_c), to_seq(o_s), to_seq(o_w)


def hybrid_mixer(u, cos, sin, layer, w_in, diff_lambda, diff_subln_g, cmp_pe_k, cmp_pe_v,
                 cmp_k_w1, cmp_k_w2, cmp_v_w1, cmp_v_w2, w_branch_a, w_branch_b, w_out):
    B, S, _ = u.shape
    G, dh = NSA_KV_GROUPS, HEAD_DIM
    split_at = np.cumsum(IN_SPLITS)[:-1].tolist()
    (dq, dk, dv, nq, kc_, vc_, ks_, vs_, kw_, vw_, ngate, mgate) = jnp.split(u @ w_in, split_at, axis=-1)
    c4, s4 = cos[:, :, None, None, :], sin[:, :, None, None, :]
    c3, s3 = cos[:, :, None, :], sin[:, :, None, :]
    dq = apply_rope(dq.reshape(B, S, DIFF_HEADS, 2, dh), c4, s4)
    dk = apply_rope(dk.reshape(B, S, DIFF_HEADS, 2, dh), c4, s4)
    dv = dv.reshape(B, S, DIFF_HEADS, DIFF_VDIM)
    lam_init = lambda_init(layer)
    lf = diff_lambda.astype(jnp.float32)
    lam = jnp.exp(jnp.sum(lf[0] * lf[1])) - jnp.exp(jnp.sum(lf[2] * lf[3])) + lam_init
    y_a = differential_attention(dq, dk, dv, lam)
    y_a = rms_norm(y_a, diff_subln_g) * (1.0 - lam_init)
    y_a = y_a.reshape(B, S, A_WIDTH) @ w_branch_a
    nq = apply_rope(nq.reshape(B, S, NSA_HEADS, dh), c3, s3).reshape(B, S, G, NSA_HPG, dh)

    def kv(t):
        return t.reshape(B, S, G, dh)

    k_cmp = compress(apply_rope(kv(kc_), c3, s3), cmp_pe_k, cmp_k_w1, cmp_k_w2)
    v_cmp = compress(kv(vc_), cmp_pe_v, cmp_v_w1, cmp_v_w2)
    k_sel = apply_rope(kv(ks_), c3, s3)
    k_win = apply_rope(kv(kw_), c3, s3)
    o_c, o_s, o_w = nsa_attention(nq, k_cmp, v_cmp, k_sel, kv(vs_), k_win, kv(vw_))
    g = jax.nn.sigmoid(ngate.astype(jnp.float32)).reshape(B, S, 3, NSA_HEADS, 1).astype(u.dtype)
    y_b = g[:, :, 0] * o_c + g[:, :, 1] * o_s + g[:, :, 2] * o_w
    y_b = y_b.reshape(B, S, B_WIDTH) @ w_branch_b
    mg = jax.nn.sigmoid(mgate.astype(jnp.float32)).reshape(B, S, 2, D_MODEL).astype(u.dtype)
    return (mg[:, :, 0] * y_a + mg[:, :, 1] * y_b) @ w_out


def swiglu(u, w_gate, w_up, w_down):
    return (jax.nn.silu(u @ w_gate) * (u @ w_up)) @ w_down


def setup_inputs(seed: int = 0) -> dict:
    key = jax.random.key(seed)
    ks = jax.random.split(key, 22)

    def nrm(k, shape, scale):
        return jax.random.normal(k, shape, jnp.float32) * scale

    return {
        "x": nrm(ks[0], (BATCH, SEQ, D_MODEL), 1.0),
        "positions": jnp.tile(jnp.arange(SEQ, dtype=jnp.int32)[None, :], (BATCH, 1)),
        "attn_norm_g": 1.0 + nrm(ks[1], (DEPTH, D_MODEL), 0.02),
        "w_in": nrm(ks[2], (DEPTH, D_MODEL, IN_DIM), D_MODEL ** -0.5),
        "diff_lambda": nrm(ks[3], (DEPTH, 4, HEAD_DIM), 0.1),
        "diff_subln_g": 1.0 + nrm(ks[4], (DEPTH, DIFF_VDIM), 0.02),
        "cmp_pe_k": nrm(ks[5], (DEPTH, CMP_LEN, HEAD_DIM), 0.1),
        "cmp_pe_v": nrm(ks[6], (DEPTH, CMP_LEN, HEAD_DIM), 0.1),
        "cmp_k_w1": nrm(ks[7], (DEPTH, CMP_LEN * HEAD_DIM, CMP_HIDDEN), (CMP_LEN * HEAD_DIM) ** -0.5),
        "cmp_k_w2": nrm(ks[8], (DEPTH, CMP_HIDDEN, HEAD_DIM), CMP_HIDDEN ** -0.5),
        "cmp_v_w1": nrm(ks[9], (DEPTH, CMP_LEN * HEAD_DIM, CMP_HIDDEN), (CMP_LEN * HEAD_DIM) ** -0.5),
        "cmp_v_w2": nrm(ks[10], (DEPTH, CMP_HIDDEN, HEAD_DIM), CMP_HIDDEN ** -0.5),
        "w_branch_a": nrm(ks[11], (DEPTH, A_WIDTH, D_MODEL), A_WIDTH ** -0.5),
        "w_branch_b": nrm(ks[12], (DEPTH, B_WIDTH, D_MODEL), B_WIDTH ** -0.5),
        "w_out": nrm(ks[13], (DEPTH, D_MODEL, D_MODEL), D_MODEL ** -0.5),
        "ffn_norm_g": 1.0 + nrm(ks[14], (DEPTH, D_MODEL), 0.02),
        "w_gate": nrm(ks[15], (DEPTH, D_MODEL, D_FF), D_MODEL ** -0.5),
        "w_up": nrm(ks[16], (DEPTH, D_MODEL, D_FF), D_MODEL ** -0.5),
        "w_down": nrm(ks[17], (DEPTH, D_FF, D_MODEL), D_FF ** -0.5),
        "final_norm_g": 1.0 + nrm(ks[18], (D_MODEL,), 0.02),
    }


def reference(x, positions, attn_norm_g, w_in, diff_lambda, diff_subln_g, cmp_pe_k, cmp_pe_v,
              cmp_k_w1, cmp_k_w2, cmp_v_w1, cmp_v_w2, w_branch_a, w_branch_b, w_out,
              ffn_norm_g, w_gate, w_up, w_down, final_norm_g):
    cos, sin = rope_tables(positions, HEAD_DIM)
    h = x
    for layer in range(DEPTH):
        u = rms_norm(h, attn_norm_g[layer])
        h = h + hybrid_mixer(u, cos, sin, layer, w_in[layer], diff_lambda[layer], diff_subln_g[layer],
                             cmp_pe_k[layer], cmp_pe_v[layer], cmp_k_w1[layer], cmp_k_w2[layer],
                             cmp_v_w1[layer], cmp_v_w2[layer], w_branch_a[layer], w_branch_b[layer],
                             w_out[layer])
        u = rms_norm(h, ffn_norm_g[layer])
        h = h + swiglu(u, w_gate[layer], w_up[layer], w_down[layer])
    return rms_norm(h, final_norm_g)
```

```python
import math
import numpy as np
import concourse.bass as bass
import concourse.mybir as mybir
from concourse.bass_utils import run_bass_kernel_spmd

F32 = mybir.dt.float32
BF16 = mybir.dt.bfloat16
I32 = mybir.dt.int32
AF = mybir.ActivationFunctionType
ALU = mybir.AluOpType
AX = mybir.AxisListType


class Res:
    __slots__ = ("w", "r", "name", "excl")

    def __init__(self, name="", excl=False):
        self.w = None
        self.r = {}
        self.name = name
        self.excl = excl


class Sched:
    ENG = ("pe", "dve", "act", "pool", "sp")
    NDSEM = 12

    def __init__(self, nc, stack):
        self.nc = nc
        self.h = {"pe": nc.tensor, "dve": nc.vector, "act": nc.scalar, "pool": nc.gpsimd, "sp": nc.sync}
        self.sems = {}
        self.cnt = {}
        self.ops = {e: [] for e in self.ENG}
        self.waited = {e: {} for e in self.ENG}
        for e in self.ENG:
            self.sems[e] = stack.enter_context(nc.semaphore("s_" + e))
            self.cnt[e] = 0
        self.dsem = {}
        self.dsem_cnt = {}
        self.dsem_rr = {}
        for q in ("sp", "pool", "act"):
            self.dsem[q] = [stack.enter_context(nc.semaphore(f"d_{q}{i}")) for i in range(self.NDSEM)]
            self.dsem_cnt[q] = [0] * self.NDSEM
            self.dsem_rr[q] = 0
        self.nops = 0
        self.bar = {e: [] for e in self.ENG}

    def _collect(self, eng, reads, writes, extra=()):
        own = ("e", eng)
        waits = {}

        def need(tok, allow_own):
            if tok is None:
                return
            k, v = tok
            if k == own and not allow_own:
                return
            if waits.get(k, 0) < v:
                waits[k] = v

        for r in reads:
            need(r.w, True)
        own_ok = (eng != "pe")
        for r in writes:
            need(r.w, own_ok)
            for k, v in r.r.items():
                need((k, v), own_ok)
        for t in extra:
            need(t, True)
        for t in self.bar[eng]:
            need(t, True)
        self.bar[eng] = []
        wd = self.waited[eng]
        out = []
        for k, v in waits.items():
            if wd.get(k, 0) >= v:
                continue
            wd[k] = v
            out.append((k, v))
        return out

    def _mark(self, tok, reads, writes):
        k, v = tok
        for r in reads:
            if r.r.get(k, 0) < v:
                r.r[k] = v
        for r in writes:
            r.w = tok
            r.r = {}

    def op(self, eng, fn, reads=(), writes=()):
        if any(r.excl for r in reads):
            writes = list(writes) + [r for r in reads if r.excl]
            reads = [r for r in reads if not r.excl]
        waits = self._collect(eng, reads, writes)
        self.cnt[eng] += 1
        tok = (("e", eng), self.cnt[eng])
        self.ops[eng].append((waits, fn, ("e", eng), 1))
        self._mark(tok, reads, writes)
        self.nops += 1
        return tok

    def dma(self, q, fn, reads=(), writes=()):
        i = self.dsem_rr[q]
        self.dsem_rr[q] = (i + 1) % self.NDSEM
        key = ("d", q, i)
        prev = self.dsem_cnt[q][i]
        extra = [(key, prev)] if prev > 0 else []
        waits = self._collect(q, reads, writes, extra)
        self.dsem_cnt[q][i] = prev + 16
        tok = (key, prev + 16)
        self.ops[q].append((waits, fn, key, 16))
        self._mark(tok, reads, writes)
        self.nops += 1
        return tok

    def all_tokens(self):
        toks = [(("e", e), self.cnt[e]) for e in self.ENG if self.cnt[e] > 0]
        for q in self.dsem_cnt:
            for i, v in enumerate(self.dsem_cnt[q]):
                if v > 0:
                    toks.append((("d", q, i), v))
        return toks

    def barrier(self):
        toks = self.all_tokens()
        for e in self.ENG:
            self.bar[e] = list(toks)

    def final_all(self):
        self.ops["sp"].append((self.all_tokens(), None, None, 0))

    def _sem(self, key):
        if key[0] == "e":
            return self.sems[key[1]]
        return self.dsem[key[1]][key[2]]

    def final_wait(self, eng, toks):
        waits = []
        for k, v in toks:
            waits.append((k, v))
        self.ops[eng].append((waits, None, None, 0))

    def emit(self):
        nc = self.nc
        with nc.Block() as block:
            def mk(e):
                def body(h):
                    for waits, fn, key, inc in self.ops[e]:
                        for k, v in waits:
                            h.wait_ge(self._sem(k), v)
                        if fn is not None:
                            ins = fn(h)
                            ins.then_inc(self._sem(key), inc)
                return body
            block.tensor(mk("pe"))
            block.vector(mk("dve"))
            block.scalar(mk("act"))
            block.gpsimd(mk("pool"))
            block.sync(mk("sp"))


D = 1024
HD = 64
DFF = 2816
NEG = -30000.0
EPS = 1e-6
INDIM = 4888
C_DQ, C_DK, C_DV, C_NQ = 0, 512, 1024, 1536
C_NSA0 = 1536
C_MG = 2840


def host_consts(S):
    c = {}
    c["ident"] = np.eye(128, dtype=np.float32)
    kk = np.arange(128)[:, None]
    q = np.arange(512)[None, :]
    m = np.zeros((128, 8, 512), np.float32)
    for j in range(4):
        m[:, j, :] = np.where(128 * j + kk <= q, 1.0, 0.0)
        m[:, 4 + j, :] = np.where(128 * j + kk > q, 1.0, 0.0)
    c["maskcw"] = m
    cc = np.arange(256)[:, None]
    qq = np.arange(S)[None, :]
    cm = np.where((16 * cc + 31 <= qq) & (cc <= 254), 0.0, NEG).astype(np.float32)
    c["cmask"] = np.ascontiguousarray(cm.reshape(2, 128, S).transpose(1, 0, 2))
    nkt = S // 128
    E = np.zeros((64, nkt, 128), np.float32)
    for kt in range(nkt):
        for k2 in range(128):
            n = 2 * kt + k2 // 64
            if n < 64:
                E[n, kt, k2] = 1.0
    c["E"] = E
    ci = np.arange(256)[:, None] * 16
    sj = np.arange(64)[None, :] * 64
    ov = ((ci < sj + 64) & (ci + 32 > sj)).astype(np.float32)
    ov[255] = 0.0
    c["ov"] = np.ascontiguousarray(ov.reshape(2, 128, 64).transpose(1, 0, 2))
    qb = (np.arange(S) // 64)[:, None]
    n = np.arange(64)[None, :]
    forced = (n == 0) | (n == qb) | (n == qb - 1)
    bt = np.where(forced, 100.0 + n, 0.0)
    bt = np.where(n > qb, -1000.0, bt).astype(np.float32)
    c["btab"] = bt
    c["invf"] = (1.0 / (10000.0 ** (np.arange(0, 64, 2, dtype=np.float32) / 64))).astype(np.float32).reshape(1, 32)
    return c


class KB:
    def __init__(self, nc, S_, st):
        self.nc = nc
        self.S = S_
        self.st = st

    def mm(self, out, lhsT, rhs, start, stop, reads, writes):
        return self.S.op("pe", lambda h: h.matmul(out, lhsT, rhs, start=start, stop=stop, skip_group_check=True),
                         reads=reads, writes=writes)

    def tr(self, out, in_, ident, reads, writes):
        return self.S.op("pe", lambda h: h.transpose(out, in_, ident), reads=reads, writes=writes)

    def act(self, out, in_, func, reads, writes, scale=None, bias=None, accum=None):
        kw = {}
        if scale is not None:
            kw["scale"] = scale
        if bias is not None:
            kw["bias"] = bias
        if accum is not None:
            kw["accum_out"] = accum
        return self.S.op("act", lambda h: h.activation(out=out, in_=in_, func=func, **kw), reads=reads, writes=writes)

    def ts(self, eng, out, in0, s1, s2, op0, op1, reads, writes):
        if op1 is None:
            return self.S.op(eng, lambda h: h.tensor_scalar(out=out, in0=in0, scalar1=s1, scalar2=None, op0=op0),
                             reads=reads, writes=writes)
        return self.S.op(eng, lambda h: h.tensor_scalar(out=out, in0=in0, scalar1=s1, scalar2=s2, op0=op0, op1=op1),
                         reads=reads, writes=writes)

    def tt(self, eng, out, in0, in1, op, reads, writes):
        return self.S.op(eng, lambda h: h.tensor_tensor(out=out, in0=in0, in1=in1, op=op), reads=reads, writes=writes)

    def stt(self, out, in0, scalar, in1, op0, op1, reads, writes, accum=None):
        if accum is None:
            return self.S.op("dve", lambda h: h.scalar_tensor_tensor(out=out, in0=in0, scalar=scalar, in1=in1, op0=op0, op1=op1),
                             reads=reads, writes=writes)
        return self.S.op("dve", lambda h: h.scalar_tensor_tensor(out=out, in0=in0, scalar=scalar, in1=in1, op0=op0, op1=op1,
                                                                  accum_out=accum), reads=reads, writes=writes)

    def cp(self, eng, out, in_, reads, writes):
        if eng == "act":
            return self.S.op("act", lambda h: h.copy(out=out, in_=in_), reads=reads, writes=writes)
        return self.S.op(eng, lambda h: h.tensor_copy(out=out, in_=in_), reads=reads, writes=writes)

    def recip(self, out, in_, reads, writes):
        return self.S.op("dve", lambda h: h.reciprocal(out=out, in_=in_), reads=reads, writes=writes)

    def memset(self, eng, ap, val, writes):
        return self.S.op(eng, lambda h: h.memset(ap, val), writes=writes)

    def dma(self, q, out, in_, reads, writes):
        return self.S.dma(q, lambda h: h.dma_start(out=out, in_=in_), reads=reads, writes=writes)


def build_program(SEQ, debug=False, stop_after=None):
    from contextlib import ExitStack
    nc = bass.Bass("TRN2", target_bir_lowering=False)
    NT = SEQ // 128
    NG = SEQ // 512
    ext_in = lambda n, shp, dt=F32: nc.dram_tensor(n, shp, dt, kind="ExternalInput").ap()
    x_d = ext_in("x", [SEQ, D])
    pos_d = ext_in("pos", [SEQ, 1], I32)
    attn_g_d = ext_in("attn_norm_g", [1, D])
    w_in_d = ext_in("w_in", [D, INDIM])
    lam_d = ext_in("diff_lambda", [1, 256])
    subg_d = ext_in("diff_subln_g", [1, 128])
    pek_d = ext_in("cmp_pe_k", [32, 64])
    pev_d = ext_in("cmp_pe_v", [32, 64])
    w1k_d = ext_in("cmp_k_w1", [2048, 256])
    w2k_d = ext_in("cmp_k_w2", [256, 64])
    w1v_d = ext_in("cmp_v_w1", [2048, 256])
    w2v_d = ext_in("cmp_v_w2", [256, 64])
    wa_d = ext_in("w_branch_a", [512, D])
    wb_d = ext_in("w_branch_b", [512, D])
    wout_d = ext_in("w_out", [D, D])
    ffn_g_d = ext_in("ffn_norm_g", [1, D])
    wg_d = ext_in("w_gate", [D, DFF])
    wu_d = ext_in("w_up", [D, DFF])
    wd_d = ext_in("w_down", [DFF, D])
    fin_g_d = ext_in("final_norm_g", [1, D])
    ident_d = ext_in("ident", [128, 128])
    maskcw_d = ext_in("maskcw", [128, 8, 512])
    cmask_d = ext_in("cmask", [128, 2, SEQ])
    E_d = ext_in("E", [64, NT, 128])
    ov_d = ext_in("ov", [128, 2, 64])
    btab_d = ext_in("btab", [SEQ, 64])
    invf_d = ext_in("invf", [1, 32])
    y_d = nc.dram_tensor("y", [SEQ, D], F32, kind="ExternalOutput").ap()
    skind = "ExternalOutput" if debug else "Internal"
    scr = lambda n, shp, dt: nc.dram_tensor(n, shp, dt, kind=skind).ap()
    xTn_s = scr("xTn_s", [NG, 128, 8, 512], BF16)
    yaT_s = scr("yaT_s", [NG, 128, 4, 512], BF16)
    ybT_s = scr("ybT_s", [NG, 128, 4, 512], BF16)
    h_s = scr("h_s", [SEQ, D], F32)
    wgu_s = scr("wgu_s", [22, 128, 2, 8, 128], BF16)

    with ExitStack() as st:
        S = Sched(nc, st)
        K = KB(nc, S, st)

        def _finish():
            S.final_all()
            with nc.allow_non_contiguous_dma(reason="small strided constant loads"):
                S.emit()
            return nc
        sb = lambda n, shp, dt: st.enter_context(nc.sbuf_tensor(n, shp, dt))
        banks = [st.enter_context(nc.psum_tensor(f"bank{i}", [128, 512], F32)) for i in range(8)]
        bres = [Res(f"bank{i}", excl=True) for i in range(8)]
        bk = lambda i: banks[i][:]
        bkb = lambda i: banks[i][:].bitcast(BF16)
        ident_bf = sb("ident_bf", [128, 128], BF16)
        ident_f = sb("ident_f", [128, 128], F32)
        cos_t = sb("cos_t", [128, NT, 32], F32)
        sin_t = sb("sin_t", [128, NT, 32], F32)
        g_attn = sb("g_attn", [128, 8], F32)
        g_ffn = sb("g_ffn", [128, 8], F32)
        gfin_b = sb("gfin_b", [128, D], F32)
        g08 = sb("g08", [128, 128], F32)
        neglam = sb("neglam", [128, 1], F32)
        maskcw = sb("maskcw_sb", [128, 8, 512], BF16)
        gate = sb("gate", [128, NT, 24], F32)
        R_const = Res("const")
        R_gate = Res("gate")
        ARENA_B = 148 * 1024
        arena = sb("arena", [128, ARENA_B // 2], BF16)

        class Carver:
            def __init__(self):
                self.off = 0

            def take(self, nbytes_pp, dt, shape_free, parts=128):
                assert self.off % 4 == 0
                n2 = (nbytes_pp + 3) // 4 * 4
                assert self.off + n2 <= ARENA_B, ("arena overflow", self.off + n2)
                v = arena[0:parts, self.off // 2:(self.off + nbytes_pp) // 2]
                self.off += n2
                if dt == F32:
                    v = v.bitcast(F32)
                return v

            def t(self, dt, *free, parts=128):
                n = 1
                for f in free:
                    n *= f
                esz = 4 if dt == F32 else 2
                v = self.take(n * esz, dt, free, parts)
                if len(free) == 2:
                    v = v.rearrange("p (a b) -> p a b", a=free[0])
                elif len(free) == 3:
                    v = v.rearrange("p (a b c) -> p a b c", a=free[0], b=free[1])
                elif len(free) == 4:
                    v = v.rearrange("p (a b c d) -> p a b c d", a=free[0], b=free[1], c=free[2])
                return v

        cv = Carver()
        K.dma("pool", ident_bf[:], ident_d, [], [R_const])
        K.dma("sp", ident_f[:], ident_d, [], [R_const])
        K.dma("pool", maskcw[:], maskcw_d, [], [R_const])
        K.dma("sp", g_attn[:], attn_g_d.rearrange("o (k p) -> p (o k)", p=128), [], [R_const])
        K.dma("sp", g_ffn[:], ffn_g_d.rearrange("o (k p) -> p (o k)", p=128), [], [R_const])
        K.dma("sp", gfin_b[:], fin_g_d.partition_broadcast(128), [], [R_const])
        K.dma("sp", g08[:], subg_d.partition_broadcast(128), [], [R_const])
        K.ts("dve", g08[:], g08[:], 0.8, None, ALU.mult, None, [R_const], [R_const])
        pos_i = cv.t(I32 if False else F32, NT)
        pos_i32 = pos_i.bitcast(I32)
        pos_f = cv.t(F32, NT)
        invf_b = cv.t(F32, 32)
        ang = cv.t(F32, NT, 32)
        tmpa = cv.t(F32, NT, 32)
        tmpi = cv.t(F32, NT, 32)
        tmpi_i = tmpi.bitcast(I32)
        R_rope = Res("rope")
        K.dma("sp", pos_i32, pos_d.rearrange("(t p) o -> p (t o)", p=128), [], [R_rope])
        K.dma("sp", invf_b, invf_d.partition_broadcast(128), [], [R_rope])
        K.cp("dve", pos_f, pos_i32, [R_rope], [R_rope])
        K.tt("dve", ang, pos_f.unsqueeze(2).broadcast_to([128, NT, 32]), invf_b.unsqueeze(1).broadcast_to([128, NT, 32]),
             ALU.mult, [R_rope], [R_rope])
        TWO_PI = 2.0 * math.pi
        for (dst, shift) in ((sin_t, 0.0), (cos_t, math.pi / 2)):
            K.ts("dve", tmpa, ang, shift, 1.0 / TWO_PI, ALU.add, ALU.mult, [R_rope], [R_rope])
            K.cp("dve", tmpi_i, tmpa, [R_rope], [R_rope])
            K.cp("dve", tmpa, tmpi_i, [R_rope], [R_rope])
            K.ts("dve", tmpa, tmpa, -TWO_PI, shift, ALU.mult, ALU.add, [R_rope], [R_rope])
            K.tt("dve", tmpa, tmpa, ang, ALU.add, [R_rope], [R_rope])
            K.ts("dve", tmpi, tmpa, math.pi, -TWO_PI, ALU.is_gt, ALU.mult, [R_rope], [R_rope])
            K.tt("dve", tmpa, tmpa, tmpi, ALU.add, [R_rope], [R_rope])
            K.ts("dve", tmpi, tmpa, -math.pi, TWO_PI, ALU.is_lt, ALU.mult, [R_rope], [R_rope])
            K.tt("dve", tmpa, tmpa, tmpi, ALU.add, [R_rope], [R_rope])
            K.ts("dve", tmpa, tmpa, math.pi, -math.pi, ALU.min, ALU.max, [R_rope], [R_rope])
            K.act(dst[:], tmpa, AF.Sin, [R_rope], [R_const])
        lam_sb = cv.t(F32, 256, parts=1)
        lam_j = cv.t(F32, 64, parts=1)
        lam_s = cv.t(F32, 4, parts=1)
        ones_row = cv.t(F32, 128, parts=1)
        R_lam = Res("lam")
        K.dma("sp", lam_sb, lam_d, [], [R_lam])
        K.memset("dve", ones_row, 1.0, [R_lam])
        K.stt(lam_j, lam_sb[:, 0:64], 1.0, lam_sb[:, 64:128], ALU.mult, ALU.mult, [R_lam], [R_lam], accum=lam_s[:, 0:1])
        K.stt(lam_j, lam_sb[:, 128:192], 1.0, lam_sb[:, 192:256], ALU.mult, ALU.mult, [R_lam], [R_lam], accum=lam_s[:, 1:2])
        K.act(lam_s[:, 2:4], lam_s[:, 0:2], AF.Exp, [R_lam], [R_lam])
        K.tt("dve", lam_s[:, 0:1], lam_s[:, 3:4], lam_s[:, 2:3], ALU.subtract, [R_lam], [R_lam])
        K.ts("dve", lam_s[:, 0:1], lam_s[:, 0:1], -0.2, None, ALU.add, None, [R_lam], [R_lam])
        K.mm(bk(0)[:, 0:1], ones_row, lam_s[:, 0:1], True, True, [R_lam], [bres[0]])
        K.cp("dve", neglam[:], bk(0)[:, 0:1], [bres[0]], [R_const])
        S.barrier()
        if stop_after == "P0":
            return _finish()

        cv = Carver()
        xt = [cv.t(F32, D) for _ in range(2)]
        xs = [cv.t(F32, D) for _ in range(2)]
        junk = cv.t(F32, D)
        stg = [cv.t(BF16, 8, 512) for _ in range(2)]
        st1 = [cv.t(F32, 4) for _ in range(2)]
        xt_r = [Res(), Res()]
        xs_r = [Res(), Res()]
        stg_r = [Res(), Res()]
        st1_r = [Res(), Res()]
        junk_r = Res()
        R_xTn = Res("xTn_s")
        _save_off = cv.off
        cv.off = ARENA_B - 2 * 4096
        wst = [cv.t(BF16, 2, 8, 128) for _ in range(2)]
        cv.off = _save_off
        wst_r = [Res("wst0"), Res("wst1")]
        R_wgu = Res("wgu_s")
        for j in range(22):
            b = j % 2
            K.dma("pool", wst[b][:, 0], wg_d[:, j * 128:(j + 1) * 128].rearrange("(k p) c -> p k c", p=128), [], [wst_r[b]])
            K.dma("pool", wst[b][:, 1], wu_d[:, j * 128:(j + 1) * 128].rearrange("(k p) c -> p k c", p=128), [], [wst_r[b]])
            K.dma("pool", wgu_s[j], wst[b], [wst_r[b]], [R_wgu])

        def nt_A(src_tile_ap, t, b, xt_ap, xt_res, load=True):
            if load:
                K.dma("sp", xt_ap, src_tile_ap, [], [xt_res])
            K.act(junk, xt_ap, AF.Square, [xt_res], [junk_r], accum=st1[b][:, 0:1])
            K.act(st1[b][:, 1:2], st1[b][:, 0:1], AF.Ln, [junk_r, R_const], [st1_r[b]], scale=1.0 / D, bias=eps_ap)
            K.act(st1[b][:, 2:3], st1[b][:, 1:2], AF.Exp, [st1_r[b]], [st1_r[b]], scale=-0.5)
            K.ts("dve", xs[b], xt_ap, st1[b][:, 2:3], None, ALU.mult, None, [xt_res, st1_r[b]], [xs_r[b]])

        def nt_B(t, b, gcol, stage, stage_r, pb):
            for half in range(2):
                pbank = pb[half]
                for kq in range(4):
                    k = half * 4 + kq
                    K.tr(bk(pbank)[:, kq * 128:(kq + 1) * 128], xs[b][:, k * 128:(k + 1) * 128], ident_f[:],
                         [xs_r[b], R_const], [bres[pbank]])
                K.tt("dve", stage[:, half * 4:half * 4 + 4, (t % 4) * 128:(t % 4 + 1) * 128],
                     bk(pbank).rearrange("p (a b) -> p a b", a=4),
                     gcol[:, half * 4:half * 4 + 4].unsqueeze(2).broadcast_to([128, 4, 128]), ALU.mult,
                     [bres[pbank], R_const], [stage_r])

        eps_t = sb("eps_t", [128, 1], F32)
        K.memset("dve", eps_t[:], EPS, [R_const])
        eps_ap = eps_t[:]
        for tt_ in range(NT + 1):
            if tt_ < NT:
                nt_A(x_d[tt_ * 128:(tt_ + 1) * 128, :], tt_, tt_ % 2, xt[tt_ % 2], xt_r[tt_ % 2])
            if tt_ >= 1:
                t = tt_ - 1
                b = t % 2
                g = t // 4
                sgb = g % 2
                nt_B(t, b, g_attn, stg[sgb], stg_r[sgb], (2 * b, 2 * b + 1))
                if t % 4 == 3:
                    K.dma("sp", xTn_s[g], stg[sgb], [stg_r[sgb]], [R_xTn])
        S.barrier()
        if stop_after == "P1":
            return _finish()

        class AttnPipe:
            def __init__(self, st_banks, pbufs, pres, look=1):
                self.stb = st_banks
                self.pb = pbufs
                self.pr = pres
                self.look = look
                self.i = 0
                self.pend = []
                self.deferred = []

            def defer(self, fn, nsteps):
                self.deferred.append([nsteps, fn])

            def _tick(self):
                ready = [d for d in self.deferred if d[0] <= 1]
                self.deferred = [[d[0] - 1, d[1]] for d in self.deferred if d[0] > 1]
                for d in ready:
                    d[1]()

            def step(self, units):
                ent = []
                for (qk, nrows, pv, post) in units:
                    i = self.i
                    self.i += 1
                    bnk = self.stb[i % len(self.stb)]
                    pi = i % len(self.pb)
                    qk(bnk)
                    ent.append((bnk, pi, nrows, pv, post))
                for ui, (bnk, pi, nrows, pv, post) in enumerate(ent):
                    elo, ehi = getattr(units[ui][0], "exp_cols", (0, 512))
                    K.act(self.pb[pi][0:nrows, elo:ehi], bk(bnk)[0:nrows, elo:ehi], AF.Exp, [bres[bnk]], [self.pr[pi]], scale=0.125)
                    mk = getattr(units[ui][0], "mask01", None)
                    if mk is not None:
                        lo, hi = units[ui][0].mask_cols
                        K.tt("pool" if ui == 0 else "dve", self.pb[pi][0:nrows, lo:hi], self.pb[pi][0:nrows, lo:hi], mk[:, lo:hi],
                             ALU.mult, [self.pr[pi], R_const], [self.pr[pi]])
                self.pend.append(ent)
                if len(self.pend) > self.look:
                    self._flush1()
                self._tick()

            def unit(self, qk, nrows, pv, post=None):
                self.step([(qk, nrows, pv, post)])

            def _flush1(self):
                ent = self.pend.pop(0)
                for (bnk, pi, nrows, pv, post) in ent:
                    pv(self.pb[pi], self.pr[pi])
                for (bnk, pi, nrows, pv, post) in ent:
                    if post is not None:
                        post()

            def flush(self):
                while self.pend:
                    self._flush1()
                while self.deferred:
                    self._tick()

        cv = Carver()
        xg = [cv.t(BF16, 8, 512) for _ in range(2)]
        xg_r = [Res(), Res()]
        qkT = cv.t(BF16, 2, SEQ)
        dv_aug = cv.t(BF16, NT, 130)
        W_hs = [cv.t(BF16, 8, 384) for _ in range(2)]
        qk_tok = [cv.t(BF16, 256) for _ in range(2)]
        rtmp = [cv.t(F32, 4, 128) for _ in range(2)]
        pbufs = [cv.t(BF16, 512) for _ in range(4)]
        tb_ = [cv.t(F32, 2, 4, 128) for _ in range(2)]
        ya4 = [cv.t(F32, 4, 128) for _ in range(2)]
        yj = cv.t(F32, 128)
        yan4 = [cv.t(BF16, 4, 128) for _ in range(2)]
        sst = [cv.t(F32, 12) for _ in range(2)]
        rz = [cv.t(F32, 2, 4) for _ in range(2)]
        mhalf = cv.t(F32, 4)
        tb_r = [Res(), Res()]
        gcount = [0]
        yaT_st = [cv.t(BF16, 512) for _ in range(2)]
        R_qkT, R_dv = Res(), Res()
        R_Whs = [Res(), Res()]
        qk_tok_r = [Res(), Res()]
        rtmp_r = [Res(), Res()]
        pres = [Res(), Res(), Res(), Res()]
        t0_r, yj_r = Res(), Res()
        ya_r = [Res(), Res()]
        yan_r = [Res(), Res()]
        sst_r = [Res(), Res()]
        rz_r = [Res(), Res()]
        yaT_st_r = [Res(), Res()]
        R_yaT = Res("yaT_s")
        K.memset("dve", dv_aug[:, :, 128:129], 1.0, [R_dv])
        K.memset("dve", mhalf, -0.5, [R_const])

        def rope(eng, out_v, in_v, t, nh, tmp, tmp_r, in_res, out_res):
            shp = [128, nh, 32]
            cb = cos_t[:, t, :].unsqueeze(1).broadcast_to(shp)
            sbb = sin_t[:, t, :].unsqueeze(1).broadcast_to(shp)
            x1 = in_v[:, :, 0, :]
            x2 = in_v[:, :, 1, :]
            tv = [tmp[:, i, 0:nh * 32].rearrange("p (h d) -> p h d", h=nh) for i in range(4)]
            K.tt(eng, tv[0], x1, cb, ALU.mult, in_res + [R_const], [tmp_r])
            K.tt(eng, tv[1], x2, sbb, ALU.mult, in_res + [R_const], [tmp_r])
            K.tt(eng, tv[2], x2, cb, ALU.mult, in_res + [R_const], [tmp_r])
            K.tt(eng, tv[3], x1, sbb, ALU.mult, in_res + [R_const], [tmp_r])
            K.tt(eng, out_v[:, :, 0, :], tv[0], tv[1], ALU.subtract, [tmp_r], out_res)
            K.tt(eng, out_v[:, :, 1, :], tv[2], tv[3], ALU.add, [tmp_r], out_res)

        for hd in range(4):
            for hn in ([0, 1] if hd == 0 else [hd + 1]):
                if hn < 4:
                    for i, c0 in enumerate((C_DQ + hn * 128, C_DK + hn * 128, C_DV + hn * 128)):
                        K.dma("pool", W_hs[hn % 2][:, :, i * 128:(i + 1) * 128],
                              w_in_d[:, c0:c0 + 128].rearrange("(k p) c -> p k c", p=128), [], [R_Whs[hn % 2]])
            W_h = W_hs[hd % 2]
            R_Wh = R_Whs[hd % 2]
            def projA(t):
                g = t // 4
                gb = g % 2
                b = t % 2
                if t % 4 == 0:
                    K.dma("sp", xg[gb], xTn_s[g], [R_xTn], [xg_r[gb]])
                pbank = b
                for k in range(8):
                    K.mm(bk(pbank)[:, 0:384], xg[gb][:, k, (t % 4) * 128:(t % 4 + 1) * 128], W_h[:, k, :],
                         k == 0, k == 7, [xg_r[gb], R_Wh], [bres[pbank]])
                pv = bk(pbank)[:, 0:256].rearrange("p (h c d) -> p h c d", h=4, c=2)
                ov_ = qk_tok[b].rearrange("p (h c d) -> p h c d", h=4, c=2)
                rope("dve", ov_, pv, t, 4, rtmp[b], rtmp_r[b], [bres[pbank]], [qk_tok_r[b]])
                K.cp("act", dv_aug[:, t, 0:128], bk(pbank)[:, 256:384], [bres[pbank]], [R_dv])

            def projB(t):
                b = t % 2
                tb = 2 + b
                for i in range(2):
                    K.tr(bkb(tb)[:, i * 128:(i + 1) * 128], qk_tok[b][:, i * 128:(i + 1) * 128], ident_bf[:],
                         [qk_tok_r[b], R_const], [bres[tb]])
                K.cp("act", qkT[:, :, t * 128:(t + 1) * 128], bkb(tb)[:, 0:256].rearrange("p (a b) -> p a b", a=2),
                     [bres[tb]], [R_qkT])

            for tt_ in range(NT + 1):
                if tt_ < NT:
                    projA(tt_)
                if tt_ >= 1:
                    projB(tt_ - 1)
            if stop_after == "P2a":
                return _finish()
            pipe = AttnPipe([4, 5, 6, 7], pbufs, pres)
            for C in range(NG):
                nk = 4 * C + 4
                for kt in range(nk):
                    diag = kt - 4 * C
                    units = []
                    for s in range(2):
                        ob = (0, 1) if s == 0 else (2, 3)

                        def qk(bnk, kt=kt, s=s, C=C, diag=diag):
                            K.mm(bk(bnk), qkT[64 * s:64 * s + 64, 1, kt * 128:(kt + 1) * 128],
                                 qkT[64 * s:64 * s + 64, 0, C * 512:(C + 1) * 512], True, True, [R_qkT], [bres[bnk]])
                        if diag >= 0:
                            qk.mask01 = maskcw[:, diag, :]
                            qk.mask_cols = (128 * diag, 128 * diag + 128)
                            qk.exp_cols = (128 * diag, 512)

                        def pv(P, Pr, kt=kt, ob=ob, diag=diag, nk=nk):
                            for qs in range(4):
                                if diag >= 0 and qs < diag:
                                    continue
                                bnk = ob[qs // 2]
                                c0 = (qs % 2) * 129
                                first = (kt == 0 and qs % 2 == 0)
                                K.mm(bk(bnk)[:, c0:c0 + 129], P[:, qs * 128:(qs + 1) * 128], dv_aug[:, kt, 0:129],
                                     first, kt == nk - 1, [Pr, R_dv], [bres[bnk]])

                        post = None
                        if kt == nk - 1:
                            def post(s=s, C=C, ob=ob, hd=hd):
                                gi = gcount[0] % 2
                                for half in range(2):
                                    bnk = ob[half]
                                    K.recip(rz[gi][:, s, 2 * half:2 * half + 2], bk(bnk)[:, 128:258:129], [bres[bnk]], [rz_r[gi]])
                                for qs in range(4):
                                    bnk = ob[qs // 2]
                                    c0 = (qs % 2) * 129
                                    K.ts("dve", tb_[gi][:, s, qs, :], bk(bnk)[:, c0:c0 + 128], rz[gi][:, s, qs:qs + 1], None,
                                         ALU.mult, None, [bres[bnk], rz_r[gi]], [tb_r[gi]])
                                if s == 1:
                                    gcount[0] += 1

                                    def post_b1(gi=gi):
                                        for qs in range(4):
                                            K.stt(ya4[gi][:, qs, :], tb_[gi][:, 1, qs, :], neglam[:], tb_[gi][:, 0, qs, :], ALU.mult, ALU.add,
                                                  [tb_r[gi], R_const], [ya_r[gi]])
                                            K.stt(yj, ya4[gi][:, qs, :], 1.0, ya4[gi][:, qs, :], ALU.mult, ALU.mult, [ya_r[gi]], [yj_r],
                                                  accum=sst[gi][:, qs:qs + 1])
                                        K.ts("pool", sst[gi][:, 4:8], sst[gi][:, 0:4], 1.0 / 128, EPS, ALU.mult, ALU.add, [yj_r], [sst_r[gi]])
                                        K.tt("pool", sst[gi][:, 8:12], sst[gi][:, 4:8], mhalf[:, 0:4], ALU.pow, [sst_r[gi], R_const], [sst_r[gi]])
                                        for qs in range(4):
                                            K.stt(yan4[gi][:, qs, :], ya4[gi][:, qs, :], sst[gi][:, 8 + qs:9 + qs], g08[:], ALU.mult, ALU.mult,
                                                  [ya_r[gi], sst_r[gi], R_const], [yan_r[gi]])

                                    def post_b2(gi=gi, C=C, hd=hd):
                                        for qs in range(4):
                                            K.tr(bkb(7)[:, qs * 128:(qs + 1) * 128], yan4[gi][:, qs, :], ident_bf[:], [yan_r[gi], R_const], [bres[7]])
                                        K.cp("dve", yaT_st[gi], bkb(7)[:, 0:512], [bres[7]], [yaT_st_r[gi]])
                                        K.dma("sp", yaT_s[C, :, hd, :], yaT_st[gi], [yaT_st_r[gi]], [R_yaT])
                                    pipe.defer(post_b1, 2)
                                    pipe.defer(post_b2, 5)
                        units.append((qk, 128, pv, post))
                    pipe.step(units)
            pipe.flush()
        S.barrier()
        if stop_after == "P2":
            return _finish()

        cv = Carver()
        nqT = cv.t(BF16, 4, SEQ)
        KT = cv.t(BF16, 4, SEQ)
        vs_aug = cv.t(BF16, NT, 2, 66)
        vw_aug = cv.t(BF16, NT, 2, 66)
        R_nqT, R_KT, R_vs, R_vw = Res(), Res(), Res(), Res()
        mark = cv.off
        xg = [cv.t(BF16, 8, 512) for _ in range(2)]
        xg_r = [Res(), Res()]
        Wn = cv.t(BF16, 8, 1304)
        R_Wn = Res()
        nq_tok = [cv.t(BF16, 512) for _ in range(2)]
        k_tok = [cv.t(BF16, 4, 128) for _ in range(2)]
        rtmpn = [cv.t(F32, 4, 256) for _ in range(2)]
        rtmpk = [cv.t(F32, 4, 192) for _ in range(2)]
        rtmpk_r = [Res(), Res()]
        gtmp = [cv.t(F32, 24) for _ in range(2)]
        nq_tok_r, k_tok_r, rtmpn_r, gtmp_r = [Res(), Res()], [Res(), Res()], [Res(), Res()], [Res(), Res()]
        for (d0, n_, c0) in ((0, 640, 1536), (640, 128, 2304), (768, 128, 2560), (896, 128, 2176), (1024, 128, 2432), (1152, 152, 2688)):
            K.dma("pool", Wn[:, :, d0:d0 + n_], w_in_d[:, c0:c0 + n_].rearrange("(k p) c -> p k c", p=128), [], [R_Wn])
        K.memset("dve", vs_aug[:, :, :, 64:65], 1.0, [R_vs])
        K.memset("dve", vw_aug[:, :, :, 64:65], 1.0, [R_vw])
        def p3A(t):
            g = t // 4
            gb = g % 2
            b = t % 2
            if t % 4 == 0:
                K.dma("sp", xg[gb], xTn_s[g], [R_xTn], [xg_r[gb]])
            pb3 = (0, 1, 2) if b == 0 else (3, 4, 5)
            for bi, (c0, cn) in enumerate(((0, 512), (512, 512), (1024, 280))):
                for k in range(8):
                    K.mm(bk(pb3[bi])[:, 0:cn], xg[gb][:, k, (t % 4) * 128:(t % 4 + 1) * 128], Wn[:, k, c0:c0 + cn],
                         k == 0, k == 7, [xg_r[gb], R_Wn], [bres[pb3[bi]]])
            A, Bk, Ck = pb3
            shp = [128, 2, 4, 32]
            cb4 = cos_t[:, t, :].unsqueeze(1).unsqueeze(1).broadcast_to(shp)
            sb4 = sin_t[:, t, :].unsqueeze(1).unsqueeze(1).broadcast_to(shp)
            pin = bk(A).rearrange("p (g j c d) -> p g j c d", g=2, j=4, c=2)
            pout = nq_tok[b].rearrange("p (j g c d) -> p g j c d", j=4, g=2, c=2)
            tv = [rtmpn[b][:, i, 0:256].rearrange("p (g j d) -> p g j d", g=2, j=4) for i in range(4)]
            x1, x2 = pin[:, :, :, 0, :], pin[:, :, :, 1, :]
            rr = [bres[A], R_const]
            K.tt("dve", tv[0], x1, cb4, ALU.mult, rr, [rtmpn_r[b]])
            K.tt("dve", tv[1], x2, sb4, ALU.mult, rr, [rtmpn_r[b]])
            K.tt("dve", tv[2], x2, cb4, ALU.mult, rr, [rtmpn_r[b]])
            K.tt("dve", tv[3], x1, sb4, ALU.mult, rr, [rtmpn_r[b]])
            K.tt("dve", pout[:, :, :, 0, :], tv[0], tv[1], ALU.subtract, [rtmpn_r[b]], [nq_tok_r[b]])
            K.tt("dve", pout[:, :, :, 1, :], tv[2], tv[3], ALU.add, [rtmpn_r[b]], [nq_tok_r[b]])
            kin = bk(Bk)[:, 0:384].rearrange("p (h c d) -> p h c d", h=6, c=2)
            rope("dve", k_tok[b][:, 0:3, :].rearrange("p s (h c d) -> p (s h) c d", h=2, c=2), kin, t, 6,
                 rtmpk[b], rtmpk_r[b], [bres[Bk]], [k_tok_r[b]])
            K.cp("act", k_tok[b][:, 3, :], bk(Bk)[:, 384:512], [bres[Bk]], [k_tok_r[b]])
            K.cp("act", vs_aug[:, t, :, 0:64], bk(Ck)[:, 0:128].rearrange("p (g d) -> p g d", g=2), [bres[Ck]], [R_vs])
            K.cp("act", vw_aug[:, t, :, 0:64], bk(Ck)[:, 128:256].rearrange("p (g d) -> p g d", g=2), [bres[Ck]], [R_vw])
            K.act(gtmp[b], bk(Ck)[:, 256:280], AF.Exp, [bres[Ck]], [gtmp_r[b]], scale=-1.0)
            K.ts("dve", gtmp[b], gtmp[b], 1.0, None, ALU.add, None, [gtmp_r[b]], [gtmp_r[b]])
            K.recip(gate[:, t, :], gtmp[b], [gtmp_r[b]], [R_gate])

        def p3B(t):
            b = t % 2
            tb = 6 + b
            for i in range(4):
                K.tr(bkb(tb)[:, i * 128:(i + 1) * 128], nq_tok[b][:, i * 128:(i + 1) * 128], ident_bf[:],
                     [nq_tok_r[b], R_const], [bres[tb]])
            for i in range(4):
                K.tr(bkb(tb)[:, 512 + i * 128:512 + (i + 1) * 128], k_tok[b][:, i, :], ident_bf[:],
                     [k_tok_r[b], R_const], [bres[tb]])
            K.cp("act", nqT[:, :, t * 128:(t + 1) * 128], bkb(tb)[:, 0:512].rearrange("p (a b) -> p a b", a=4), [bres[tb]], [R_nqT])
            K.cp("act", KT[:, :, t * 128:(t + 1) * 128], bkb(tb)[:, 512:1024].rearrange("p (a b) -> p a b", a=4), [bres[tb]], [R_KT])

        for tt_ in range(NT + 1):
            if tt_ < NT:
                p3A(tt_)
            if tt_ >= 1:
                p3B(tt_ - 1)
        S.barrier()
        if stop_after == "P3a":
            return _finish()

        cv.off = mark
        kcmpT = cv.t(BF16, 256)
        vc_aug = cv.t(BF16, 2, 2, 130)
        mark = cv.off
        w1 = [cv.t(BF16, 32, 256) for _ in range(2)]
        w2kd = cv.t(BF16, 2, 128)
        w2v = cv.t(BF16, 2, 64)
        peT = cv.t(BF16, 2, 32)
        cb_sb = cv.t(F32, 2, 2)
        hid_sb = cv.t(BF16, 2, 2, 2, 256)
        R_w1, R_w2, R_pe, R_cb, R_hid, R_kcmp, R_vca = Res(), Res(), Res(), Res(), Res(), Res(), Res()
        for kv, wd_ in enumerate((w1k_d, w1v_d)):
            for half in range(2):
                K.dma("pool", w1[kv][64 * half:64 * half + 64], wd_.rearrange("(l d) h -> d l h", d=64), [], [R_w1])
        for half in range(2):
            K.dma("pool", w2kd[:, :, 64 * half:64 * half + 64], w2k_d.rearrange("(c p) d -> p c d", p=128), [], [R_w2])
        K.dma("pool", w2v, w2v_d.rearrange("(c p) d -> p c d", p=128), [], [R_w2])
        with nc.allow_non_contiguous_dma(reason="tiny pe transpose load"):
            for kv, pd in enumerate((pek_d, pev_d)):
                for half in range(2):
                    K.dma("pool", peT[64 * half:64 * half + 64, kv, :], pd.rearrange("l d -> d l"), [], [R_pe])
        K.memset("dve", vc_aug[:, :, :, 64:65], 1.0, [R_vca])
        for g in range(2):
            K.dma("pool", vc_aug[:, g, :, 66:130], ov_d, [], [R_vca])
        for kv in range(2):
            for hc in range(2):
                col = kv * 2 + hc
                for l in range(32):
                    K.mm(bk(0)[:, col:col + 1], w1[kv][0:64, l, hc * 128:(hc + 1) * 128], peT[0:64, kv, l:l + 1],
                         l == 0 and col == 0, l == 31, [R_w1, R_pe], [bres[0]])
        K.cp("dve", cb_sb.rearrange("p a b -> p (a b)"), bk(0)[:, 0:4], [bres[0]], [R_cb])
        NCB = (SEQ - 32) // 16 + 1
        ui = 0
        for kv in range(2):
            for g in range(2):
                for hc in range(2):
                    bnk = 1 + ui % 3
                    ui += 1
                    for l in range(32):
                        K.mm(bk(bnk)[:, 0:NCB], w1[kv][64 * g:64 * g + 64, l, hc * 128:(hc + 1) * 128],
                             KT[64 * g:64 * g + 64, 3 * kv, l:l + 16 * (NCB - 1) + 1:16], l == 0, l == 31, [R_w1, R_KT], [bres[bnk]])
                    K.act(hid_sb[:, kv, g, hc, 0:NCB], bk(bnk)[:, 0:NCB], AF.Silu, [bres[bnk], R_cb], [R_hid],
                          bias=cb_sb[:, kv, hc:hc + 1])
        for g in range(2):
            for hc in range(2):
                K.mm(bk(4)[:, 0:NCB], w2kd[:, hc, :], hid_sb[:, 0, g, hc, 0:NCB], hc == 0, hc == 1, [R_w2, R_hid], [bres[4]])
            K.cp("dve", kcmpT[64 * g:64 * g + 64, 0:NCB], bk(4)[64 * g:64 * g + 64, 0:NCB], [bres[4]], [R_kcmp])
            for ct in range(2):
                n = min(128, NCB - ct * 128)
                if n <= 0:
                    continue
                for hc in range(2):
                    K.mm(bk(5)[0:n, ct * 64:ct * 64 + 64], hid_sb[:, 1, g, hc, ct * 128:ct * 128 + n], w2v[:, hc, :],
                         hc == 0 and ct == 0, hc == 1, [R_hid, R_w2], [bres[5]])
            for ct in range(2):
                n = min(128, NCB - ct * 128)
                if n <= 0:
                    continue
                K.cp("dve", vc_aug[0:n, g, ct, 0:64], bk(5)[0:n, ct * 64:ct * 64 + 64], [bres[5]], [R_vca])
        S.barrier()
        if stop_after == "P3b":
            return _finish()

        cv.off = mark
        E_sb = cv.t(BF16, NT, 128)
        pbufs = [cv.t(BF16, 512) for _ in range(4)]
        pres = [Res(), Res(), Res(), Res()]
        acc = cv.t(F32, 4, 512)
        imp = cv.t(F32, 4, 2, 64)
        imp2 = cv.t(F32, 64)
        selb = cv.t(F32, 64)
        selb_bf = cv.t(BF16, 4, 2, 64)
        R_selbf = Res()
        m8 = cv.t(F32, 16)
        selbT = cv.t(BF16, 512)
        btab = [cv.t(F32, 4, 64) for _ in range(2)]
        cmk = [cv.t(BF16, 2, 512) for _ in range(2)]
        rzn = [cv.t(F32, 8) for _ in range(2)]
        yb_bf = cv.t(BF16, 4, 512)
        ybT_st = [cv.t(BF16, 4, 512) for _ in range(2)]
        R_E, R_acc, R_imp, R_sel, R_selbT, R_ybbf = Res(), Res(), Res(), Res(), Res(), Res()
        btab_r, cmk_r, rzn_r, ybT_st_r = [Res(), Res()], [Res(), Res()], [Res(), Res()], [Res(), Res()]
        R_ybT = Res("ybT_s")
        K.dma("pool", E_sb[0:64], E_d, [], [R_E])
        K.dma("pool", E_sb[64:128], E_d, [], [R_E])
        hh_list = [(hh, hh // 4, hh % 4) for hh in range(8)]
        ecount = [0]

        def evac_branch(bnk, ncols, hh, br, C, with_imp):
            e = ecount[0] % 2
            ecount[0] += 1
            Ov = bk(bnk)[:, 0:4 * ncols].rearrange("p (a b) -> p a b", a=4)
            g = hh // 4
            K.ts("dve", rzn[e][:, 0:4], Ov[:, :, 64], 1e-30, None, ALU.add, None, [bres[bnk]], [rzn_r[e]])
            K.recip(rzn[e][:, 0:4], rzn[e][:, 0:4], [rzn_r[e]], [rzn_r[e]])
            K.tt("dve", rzn[e][:, 4:8], rzn[e][:, 0:4], gate[:, 4 * C:4 * C + 4, br * 8 + hh], ALU.mult,
                 [rzn_r[e], R_gate], [rzn_r[e]])
            for qs in range(4):
                dst = acc[:, qs, hh * 64:(hh + 1) * 64]
                if br == 0:
                    K.ts("dve", dst, Ov[:, qs, 0:64], rzn[e][:, 4 + qs:5 + qs], None, ALU.mult, None,
                         [bres[bnk], rzn_r[e]], [R_acc])
                else:
                    K.stt(dst, Ov[:, qs, 0:64], rzn[e][:, 4 + qs:5 + qs], dst, ALU.mult, ALU.add,
                          [bres[bnk], rzn_r[e], R_acc], [R_acc])
                if with_imp:
                    K.stt(imp[:, qs, g, :], Ov[:, qs, 65:129], rzn[e][:, qs:qs + 1], imp[:, qs, g, :], ALU.mult, ALU.add,
                          [bres[bnk], rzn_r[e], R_imp], [R_imp])

        pipe = AttnPipe([0, 1, 2, 3], pbufs, pres)
        obank_i = [0]
        pairs = [((j, 0, j), (4 + j, 1, j)) for j in range(4)]
        for C in range(NG):
            cbi = C % 2
            for Cn in ([0, 1] if C == 0 else [C + 1]):
                if Cn < NG:
                    K.dma("sp", btab[Cn % 2], btab_d[Cn * 512:(Cn + 1) * 512, :].rearrange("(a p) n -> p a n", p=128), [], [btab_r[Cn % 2]])
                    K.dma("pool", cmk[Cn % 2], cmask_d[:, :, Cn * 512:(Cn + 1) * 512], [], [cmk_r[Cn % 2]])
            pipe.flush()
            for g in range(2):
                K.cp("dve", imp[:, :, g, :], btab[cbi], [btab_r[cbi]], [R_imp])
            ncts = [ct for ct in range(2) if (ct == 0 or 32 * C + 30 >= 128) and NCB - ct * 128 > 0]
            for pair in pairs:
                for ui_, ct in enumerate(ncts):
                    n = min(128, NCB - ct * 128)
                    units = []
                    for pi_, (hh, g, j) in enumerate(pair):
                        oa = 4 + 2 * pi_
                        ib = oa + 1

                        def qk(bnk, ct=ct, n=n, g=g, j=j, C=C, cbi=cbi):
                            K.mm(bk(bnk)[0:n, :], kcmpT[64 * g:64 * g + 64, ct * 128:ct * 128 + n],
                                 nqT[64 * g:64 * g + 64, j, C * 512:(C + 1) * 512], True, False, [R_kcmp, R_nqT], [bres[bnk]])
                            K.mm(bk(bnk)[0:n, :], ident_bf[:, 0:n], cmk[cbi][:, ct, :], False, True, [R_const, cmk_r[cbi]], [bres[bnk]])

                        def pv(P, Pr, ct=ct, n=n, g=g, oa=oa, ib=ib, first=(ui_ == 0), last=(ui_ == len(ncts) - 1)):
                            for qs in range(4):
                                K.mm(bk(oa)[:, qs * 65:qs * 65 + 65], P[0:n, qs * 128:(qs + 1) * 128], vc_aug[0:n, g, ct, 0:65],
                                     first and qs == 0, last, [Pr, R_vca], [bres[oa]])
                                K.mm(bk(ib)[:, qs * 64:qs * 64 + 64], P[0:n, qs * 128:(qs + 1) * 128], vc_aug[0:n, g, ct, 66:130],
                                     first and qs == 0, last, [Pr, R_vca], [bres[ib]])
                        post = None
                        if ui_ == len(ncts) - 1:
                            def post(oa=oa, ib=ib, hh=hh, C=C, g=g):
                                e = ecount[0] % 2
                                evac_branch(oa, 65, hh, 0, C, False)
                                Iv = bk(ib)[:, 0:256].rearrange("p (a b) -> p a b", a=4)
                                for qs in range(4):
                                    K.stt(imp[:, qs, g, :], Iv[:, qs, :], rzn[e][:, qs:qs + 1], imp[:, qs, g, :], ALU.mult, ALU.add,
                                          [bres[ib], rzn_r[e], R_imp], [R_imp])
                        units.append((qk, n, pv, post))
                    pipe.step(units)
            pipe.flush()
            for qs in range(4):
                for g in range(2):
                    iv = imp[:, qs, g, :]
                    K.S.op("dve", lambda h, iv=iv: h.max(out=m8[:, 0:8], in_=iv), reads=[R_imp], writes=[R_sel])
                    K.S.op("dve", lambda h, iv=iv: h.match_replace(out=imp2, in_to_replace=m8[:, 0:8], in_values=iv, imm_value=-1e9),
                           reads=[R_imp, R_sel], writes=[R_sel])
                    K.S.op("dve", lambda h: h.max(out=m8[:, 8:16], in_=imp2), reads=[R_sel], writes=[R_sel])
                    K.ts("dve", selb, iv, m8[:, 15:16], None, ALU.is_ge, None, [R_imp, R_sel], [R_sel])
                    K.ts("dve", selb_bf[:, qs, g, :], selb, -1.0, -NEG, ALU.add, ALU.mult, [R_sel], [R_selbf])

            def sel_transposes():
                for qs in range(4):
                    K.tr(bkb(0)[:, qs * 128:(qs + 1) * 128], selb_bf[:, qs].rearrange("p g n -> p (g n)"), ident_bf[:],
                         [R_selbf, R_const], [bres[0]])
                K.cp("dve", selbT, bkb(0)[:, 0:512], [bres[0]], [R_selbT])
            for br in (2, 1):
                if br == 1:
                    pipe.flush()
                    sel_transposes()
                    kts = list(range(4 * C + 4))
                else:
                    kts = [kt for kt in range(4 * C - 4, 4 * C + 4) if kt >= 0]
                Vt = vs_aug if br == 1 else vw_aug
                Rv = R_vs if br == 1 else R_vw
                slot = 1 if br == 1 else 2
                for pair in pairs:
                    ob0 = 4 + 2 * (obank_i[0] % 2)
                    obank_i[0] += 1
                    for ui_, kt in enumerate(kts):
                        off = kt - 4 * C
                        units = []
                        for pi_, (hh, g, j) in enumerate(pair):
                            oa = ob0 + pi_

                            def qk(bnk, kt=kt, off=off, g=g, j=j, C=C, br=br, slot=slot):
                                K.mm(bk(bnk), KT[64 * g:64 * g + 64, slot, kt * 128:(kt + 1) * 128],
                                     nqT[64 * g:64 * g + 64, j, C * 512:(C + 1) * 512], True, br == 2, [R_KT, R_nqT], [bres[bnk]])
                                if br == 1:
                                    K.mm(bk(bnk), E_sb[64 * g:64 * g + 64, kt, :], selbT[64 * g:64 * g + 64, :], False, True,
                                         [R_E, R_selbT], [bres[bnk]])
                            qlo, qhi = 0, 4
                            if off >= 0:
                                qk.mask01 = maskcw[:, off, :]
                                qk.mask_cols = (128 * off, 128 * off + 128)
                                qk.exp_cols = (128 * off, 512)
                                qlo = off
                            elif br == 2:
                                jw = off + 4
                                qk.mask01 = maskcw[:, 4 + jw, :]
                                qk.mask_cols = (128 * jw, 128 * jw + 128)
                                qk.exp_cols = (0, 128 * jw + 128)
                                qhi = jw + 1

                            def pv(P, Pr, kt=kt, g=g, oa=oa, Vt=Vt, Rv=Rv, first=(ui_ == 0), last=(ui_ == len(kts) - 1), qlo=qlo, qhi=qhi):
                                for qs in range(qlo, qhi):
                                    K.mm(bk(oa)[:, qs * 65:qs * 65 + 65], P[:, qs * 128:(qs + 1) * 128], Vt[:, kt, g, 0:65],
                                         first and qs == 0, last, [Pr, Rv], [bres[oa]])
                            post = None
                            if ui_ == len(kts) - 1:
                                def post(oa=oa, hh=hh, C=C, br=br):
                                    evac_branch(oa, 65, hh, br, C, False)
                            units.append((qk, 128, pv, post))
                        pipe.step(units)
            pipe.flush()
            K.cp("dve", yb_bf, acc, [R_acc], [R_ybbf])
            for qs in range(4):
                for fc in range(4):
                    K.tr(bkb(1)[:, fc * 128:(fc + 1) * 128], yb_bf[:, qs, fc * 128:(fc + 1) * 128], ident_bf[:],
                         [R_ybbf, R_const], [bres[1]])
                K.cp("dve", ybT_st[cbi][:, :, qs * 128:(qs + 1) * 128], bkb(1)[:, 0:512].rearrange("p (a b) -> p a b", a=4),
                     [bres[1]], [ybT_st_r[cbi]])
            K.dma("pool", ybT_s[C], ybT_st[cbi], [ybT_st_r[cbi]], [R_ybT])
        S.barrier()
        if stop_after == "P3c":
            return _finish()

        cv = Carver()
        Wmg = cv.t(BF16, 8, 2048)
        Wa = cv.t(BF16, 4, D)
        Wb = cv.t(BF16, 4, D)
        Wo = cv.t(BF16, 8, D)
        xg = [cv.t(BF16, 8, 512) for _ in range(2)]
        yag = [cv.t(BF16, 4, 512) for _ in range(2)]
        ybg = [cv.t(BF16, 4, 512) for _ in range(2)]
        xt = [cv.t(F32, D) for _ in range(2)]
        mT = cv.t(BF16, 8, 512)
        sg = [cv.t(F32, 2, 512) for _ in range(2)]
        mm_ = [cv.t(F32, 2, 512) for _ in range(2)]
        ho = [cv.t(F32, D) for _ in range(2)]
        R_W4, R_mT, R_h = Res(), Res(), Res("h_s")
        xg_r, yag_r, ybg_r, xt_r, sg_r, mm_r, ho_r = ([Res(), Res()] for _ in range(7))
        for kq in range(4):
            K.dma("pool", Wmg[:, 2 * kq:2 * kq + 2, :], w_in_d[kq * 256:(kq + 1) * 256, C_MG:C_MG + 2048].rearrange("(k p) c -> p k c", p=128),
                  [], [R_W4])
        K.dma("pool", Wa, wa_d.rearrange("(k p) c -> p k c", p=128), [], [R_W4])
        K.dma("pool", Wb, wb_d.rearrange("(k p) c -> p k c", p=128), [], [R_W4])
        for kq in range(2):
            K.dma("pool", Wo[:, 4 * kq:4 * kq + 4, :], wout_d[kq * 512:(kq + 1) * 512, :].rearrange("(k p) c -> p k c", p=128), [], [R_W4])
        for G in range(NG):
            gb = G % 2
            for Gn in ([0, 1] if G == 0 else [G + 1]):
                if Gn < NG:
                    K.dma("sp", xg[Gn % 2], xTn_s[Gn], [R_xTn], [xg_r[Gn % 2]])
                    K.dma("sp", yag[Gn % 2], yaT_s[Gn], [R_yaT], [yag_r[Gn % 2]])
                    K.dma("sp", ybg[Gn % 2], ybT_s[Gn], [R_ybT], [ybg_r[Gn % 2]])
            for dc in range(8):
                e = dc % 2
                pb4 = (0, 1, 2, 3) if e == 0 else (4, 5, 6, 7)
                for i in range(2):
                    for k in range(8):
                        K.mm(bk(pb4[i]), Wmg[:, k, i * 1024 + dc * 128:i * 1024 + (dc + 1) * 128], xg[gb][:, k, :],
                             k == 0, k == 7, [R_W4, xg_r[gb]], [bres[pb4[i]]])
                for i, (Wx, yg, yr) in enumerate(((Wa, yag, yag_r), (Wb, ybg, ybg_r))):
                    for k in range(4):
                        K.mm(bk(pb4[2 + i]), Wx[:, k, dc * 128:(dc + 1) * 128], yg[gb][:, k, :], k == 0, k == 3,
                             [R_W4, yr[gb]], [bres[pb4[2 + i]]])
                for i in range(2):
                    K.act(sg[e][:, i, :], bk(pb4[i]), AF.Sigmoid, [bres[pb4[i]]], [sg_r[e]])
                for i in range(2):
                    K.tt("dve", mm_[e][:, i, :], sg[e][:, i, :], bk(pb4[2 + i]), ALU.mult, [sg_r[e], bres[pb4[2 + i]]], [mm_r[e]])
                K.tt("dve", mT[:, dc, :], mm_[e][:, 0, :], mm_[e][:, 1, :], ALU.add, [mm_r[e]], [R_mT])
            for qs in range(4):
                t = G * 4 + qs
                b = qs % 2
                K.dma("sp", xt[b], x_d[t * 128:(t + 1) * 128, :], [], [xt_r[b]])
                for n2 in range(2):
                    bnk = 2 * b + n2
                    for dc in range(8):
                        K.mm(bk(bnk), mT[:, dc, qs * 128:(qs + 1) * 128], Wo[:, dc, n2 * 512:(n2 + 1) * 512], dc == 0, dc == 7,
                             [R_mT, R_W4], [bres[bnk]])
                    K.tt("dve", ho[b][:, n2 * 512:(n2 + 1) * 512], bk(bnk), xt[b][:, n2 * 512:(n2 + 1) * 512], ALU.add,
                         [bres[bnk], xt_r[b]], [ho_r[b]])
                K.dma("pool", h_s[t * 128:(t + 1) * 128, :], ho[b], [ho_r[b]], [R_h])
        S.barrier()
        if stop_after == "P4":
            return _finish()

        cv = Carver()
        Wd = cv.t(BF16, 22, D)
        hgs = [cv.t(F32, 4, D) for _ in range(2)]
        xs = [cv.t(F32, D) for _ in range(2)]
        junk = cv.t(F32, D)
        hT = cv.t(BF16, 8, 512)
        actT = cv.t(BF16, 22, 512)
        wr = [cv.t(BF16, 2, 8, 128) for _ in range(4)]
        sgl = [cv.t(F32, 512) for _ in range(2)]
        st1 = [cv.t(F32, 4) for _ in range(2)]
        yo = [cv.t(F32, D) for _ in range(2)]
        R_Wd, R_hT, R_act, R_y = Res(), Res(), Res(), Res("y")
        R_hgs = [Res(), Res()]
        xs_r, wr_r, sgl_r, st1_r, yo_r = [Res(), Res()], [Res() for _ in range(4)], [Res(), Res()], [Res(), Res()], [Res(), Res()]
        junk_r = Res()
        for kq in range(2):
            K.dma("pool", Wd[:, 11 * kq:11 * kq + 11, :], wd_d[kq * 1408:(kq + 1) * 1408, :].rearrange("(k p) c -> p k c", p=128), [], [R_Wd])
        wi = 0
        for G in range(NG):
            for Gn in ([0, 1] if G == 0 else [G + 1]):
                if Gn < NG:
                    K.dma("sp", hgs[Gn % 2], h_s[Gn * 512:(Gn + 1) * 512, :].rearrange("(a p) c -> p a c", p=128), [R_h], [R_hgs[Gn % 2]])
            hg = hgs[G % 2]
            R_hg = R_hgs[G % 2]
            for qs in range(4):
                b = qs % 2
                nt_A(None, qs, b, hg[:, qs, :], R_hg, load=False)
                nt_B(qs, b, g_ffn, hT, R_hT, (2 * b, 2 * b + 1))
            for j in range(22):
                w = wi % 4
                wi += 1
                K.dma("sp", wr[w], wgu_s[j], [R_wgu], [wr_r[w]])
                e = j % 2
                bg, bu = (4, 5) if e == 0 else (6, 7)
                for i, bnk in enumerate((bg, bu)):
                    for k in range(8):
                        K.mm(bk(bnk), wr[w][:, i, k, :], hT[:, k, :], k == 0, k == 7, [wr_r[w], R_hT], [bres[bnk]])
                K.act(sgl[e], bk(bg), AF.Silu, [bres[bg]], [sgl_r[e]])
                K.tt("dve", actT[:, j, :], sgl[e], bk(bu), ALU.mult, [sgl_r[e], bres[bu]], [R_act])
            for qs in range(4):
                t = G * 4 + qs
                b = qs % 2
                for n2 in range(2):
                    bnk = 2 * b + n2
                    for j in range(22):
                        K.mm(bk(bnk), actT[:, j, qs * 128:(qs + 1) * 128], Wd[:, j, n2 * 512:(n2 + 1) * 512], j == 0, j == 21,
                             [R_act, R_Wd], [bres[bnk]])
                    K.tt("dve", yo[b][:, n2 * 512:(n2 + 1) * 512], bk(bnk), hg[:, qs, n2 * 512:(n2 + 1) * 512], ALU.add,
                         [bres[bnk], R_hg], [yo_r[b]])
                K.act(junk, yo[b], AF.Square, [yo_r[b]], [junk_r], accum=st1[b][:, 0:1])
                K.act(st1[b][:, 1:2], st1[b][:, 0:1], AF.Ln, [junk_r, R_const], [st1_r[b]], scale=1.0 / D, bias=eps_ap)
                K.act(st1[b][:, 2:3], st1[b][:, 1:2], AF.Exp, [st1_r[b]], [st1_r[b]], scale=-0.5)
                K.stt(yo[b], yo[b], st1[b][:, 2:3], gfin_b[:], ALU.mult, ALU.mult, [yo_r[b], st1_r[b], R_const], [yo_r[b]])
                K.dma("pool", y_d[t * 128:(t + 1) * 128, :], yo[b], [yo_r[b]], [R_y])
        return _finish()


_PROG = {}


def kernel(**inputs):
    x = np.ascontiguousarray(np.asarray(inputs["x"], dtype=np.float32))
    B, SEQ, _ = x.shape
    pos = np.asarray(inputs["positions"]).astype(np.int32)
    f = lambda k, shp: np.ascontiguousarray(np.asarray(inputs[k], dtype=np.float32).reshape(shp))
    common = {
        "attn_norm_g": f("attn_norm_g", (1, D)), "w_in": f("w_in", (D, INDIM)), "diff_lambda": f("diff_lambda", (1, 256)),
        "diff_subln_g": f("diff_subln_g", (1, 128)), "cmp_pe_k": f("cmp_pe_k", (32, 64)), "cmp_pe_v": f("cmp_pe_v", (32, 64)),
        "cmp_k_w1": f("cmp_k_w1", (2048, 256)), "cmp_k_w2": f("cmp_k_w2", (256, 64)),
        "cmp_v_w1": f("cmp_v_w1", (2048, 256)), "cmp_v_w2": f("cmp_v_w2", (256, 64)),
        "w_branch_a": f("w_branch_a", (512, D)), "w_branch_b": f("w_branch_b", (512, D)), "w_out": f("w_out", (D, D)),
        "ffn_norm_g": f("ffn_norm_g", (1, D)), "w_gate": f("w_gate", (D, DFF)), "w_up": f("w_up", (D, DFF)),
        "w_down": f("w_down", (DFF, D)), "final_norm_g": f("final_norm_g", (1, D)),
    }
    common.update(host_consts(SEQ))
    if SEQ not in _PROG:
        _PROG[SEQ] = build_program(SEQ)
    nc = _PROG[SEQ]
    in_maps = [dict(common, x=x[b], pos=np.ascontiguousarray(pos[b].reshape(SEQ, 1))) for b in range(B)]
    res = run_bass_kernel_spmd(nc, in_maps, core_ids=list(range(B)))
    return np.stack([np.asarray(r["y"], dtype=np.float32) for r in res.results], axis=0)
```

```python
import math
import numpy as np
import concourse.bass as bass
import concourse.mybir as mybir
from concourse.bass_utils import run_bass_kernel_spmd

F32 = mybir.dt.float32
BF16 = mybir.dt.bfloat16
I32 = mybir.dt.int32
AF = mybir.ActivationFunctionType
ALU = mybir.AluOpType
AX = mybir.AxisListType


class Res:
    __slots__ = ("w", "r", "name", "excl")

    def __init__(self, name="", excl=False):
        self.w = None
        self.r = {}
        self.name = name
        self.excl = excl


class Sched:
    ENG = ("pe", "dve", "act", "pool", "sp")
    NDSEM = 12

    def __init__(self, nc, stack):
        self.nc = nc
        self.h = {"pe": nc.tensor, "dve": nc.vector, "act": nc.scalar, "pool": nc.gpsimd, "sp": nc.sync}
        self.sems = {}
        self.cnt = {}
        self.ops = {e: [] for e in self.ENG}
        self.waited = {e: {} for e in self.ENG}
        for e in self.ENG:
            self.sems[e] = stack.enter_context(nc.semaphore("s_" + e))
            self.cnt[e] = 0
        self.dsem = {}
        self.dsem_cnt = {}
        self.dsem_rr = {}
        for q in ("sp", "pool", "act"):
            self.dsem[q] = [stack.enter_context(nc.semaphore(f"d_{q}{i}")) for i in range(self.NDSEM)]
            self.dsem_cnt[q] = [0] * self.NDSEM
            self.dsem_rr[q] = 0
        self.nops = 0
        self.bar = {e: [] for e in self.ENG}

    def _collect(self, eng, reads, writes, extra=()):
        own = ("e", eng)
        waits = {}

        def need(tok, allow_own):
            if tok is None:
                return
            k, v = tok
            if k == own and not allow_own:
                return
            if waits.get(k, 0) < v:
                waits[k] = v

        for r in reads:
            need(r.w, True)
        own_ok = (eng != "pe")
        for r in writes:
            need(r.w, own_ok)
            for k, v in r.r.items():
                need((k, v), own_ok)
        for t in extra:
            need(t, True)
        for t in self.bar[eng]:
            need(t, True)
        self.bar[eng] = []
        wd = self.waited[eng]
        out = []
        for k, v in waits.items():
            if wd.get(k, 0) >= v:
                continue
            wd[k] = v
            out.append((k, v))
        return out

    def _mark(self, tok, reads, writes):
        k, v = tok
        for r in reads:
            if r.r.get(k, 0) < v:
                r.r[k] = v
        for r in writes:
            r.w = tok
            r.r = {}

    def op(self, eng, fn, reads=(), writes=()):
        if any(r.excl for r in reads):
            writes = list(writes) + [r for r in reads if r.excl]
            reads = [r for r in reads if not r.excl]
        waits = self._collect(eng, reads, writes)
        self.cnt[eng] += 1
        tok = (("e", eng), self.cnt[eng])
        self.ops[eng].append((waits, fn, ("e", eng), 1))
        self._mark(tok, reads, writes)
        self.nops += 1
        return tok

    def dma(self, q, fn, reads=(), writes=()):
        i = self.dsem_rr[q]
        self.dsem_rr[q] = (i + 1) % self.NDSEM
        key = ("d", q, i)
        prev = self.dsem_cnt[q][i]
        extra = [(key, prev)] if prev > 0 else []
        waits = self._collect(q, reads, writes, extra)
        self.dsem_cnt[q][i] = prev + 16
        tok = (key, prev + 16)
        self.ops[q].append((waits, fn, key, 16))
        self._mark(tok, reads, writes)
        self.nops += 1
        return tok

    def all_tokens(self):
        toks = [(("e", e), self.cnt[e]) for e in self.ENG if self.cnt[e] > 0]
        for q in self.dsem_cnt:
            for i, v in enumerate(self.dsem_cnt[q]):
                if v > 0:
                    toks.append((("d", q, i), v))
        return toks

    def barrier(self):
        toks = self.all_tokens()
        for e in self.ENG:
            self.bar[e] = list(toks)

    def final_all(self):
        self.ops["sp"].append((self.all_tokens(), None, None, 0))

    def _sem(self, key):
        if key[0] == "e":
            return self.sems[key[1]]
        return self.dsem[key[1]][key[2]]

    def final_wait(self, eng, toks):
        waits = []
        for k, v in toks:
            waits.append((k, v))
        self.ops[eng].append((waits, None, None, 0))

    def emit(self):
        nc = self.nc
        with nc.Block() as block:
            def mk(e):
                def body(h):
                    for waits, fn, key, inc in self.ops[e]:
                        for k, v in waits:
                            h.wait_ge(self._sem(k), v)
                        if fn is not None:
                            ins = fn(h)
                            ins.then_inc(self._sem(key), inc)
                return body
            block.tensor(mk("pe"))
            block.vector(mk("dve"))
            block.scalar(mk("act"))
            block.gpsimd(mk("pool"))
            block.sync(mk("sp"))


D = 1024
HD = 64
DFF = 2816
NEG = -30000.0
EPS = 1e-6
INDIM = 4888
C_DQ, C_DK, C_DV, C_NQ = 0, 512, 1024, 1536
C_NSA0 = 1536
C_MG = 2840


def host_consts(S):
    c = {}
    c["ident"] = np.eye(128, dtype=np.float32)
    kk = np.arange(128)[:, None]
    q = np.arange(512)[None, :]
    m = np.zeros((128, 8, 512), np.float32)
    for j in range(4):
        m[:, j, :] = np.where(128 * j + kk <= q, 1.0, 0.0)
        m[:, 4 + j, :] = np.where(128 * j + kk > q, 1.0, 0.0)
    c["maskcw"] = m
    cc = np.arange(256)[:, None]
    qq = np.arange(S)[None, :]
    cm = np.where((16 * cc + 31 <= qq) & (cc <= 254), 0.0, NEG).astype(np.float32)
    c["cmask"] = np.ascontiguousarray(cm.reshape(2, 128, S).transpose(1, 0, 2))
    nkt = S // 128
    E = np.zeros((64, nkt, 128), np.float32)
    for kt in range(nkt):
        for k2 in range(128):
            n = 2 * kt + k2 // 64
            if n < 64:
                E[n, kt, k2] = 1.0
    c["E"] = E
    ci = np.arange(256)[:, None] * 16
    sj = np.arange(64)[None, :] * 64
    ov = ((ci < sj + 64) & (ci + 32 > sj)).astype(np.float32)
    ov[255] = 0.0
    c["ov"] = np.ascontiguousarray(ov.reshape(2, 128, 64).transpose(1, 0, 2))
    qb = (np.arange(S) // 64)[:, None]
    n = np.arange(64)[None, :]
    forced = (n == 0) | (n == qb) | (n == qb - 1)
    bt = np.where(forced, 100.0 + n, 0.0)
    bt = np.where(n > qb, -1000.0, bt).astype(np.float32)
    c["btab"] = bt
    c["invf"] = (1.0 / (10000.0 ** (np.arange(0, 64, 2, dtype=np.float32) / 64))).astype(np.float32).reshape(1, 32)
    return c


class KB:
    def __init__(self, nc, S_, st):
        self.nc = nc
        self.S = S_
        self.st = st

    def mm(self, out, lhsT, rhs, start, stop, reads, writes):
        return self.S.op("pe", lambda h: h.matmul(out, lhsT, rhs, start=start, stop=stop, skip_group_check=True),
                         reads=reads, writes=writes)

    def tr(self, out, in_, ident, reads, writes):
        return self.S.op("pe", lambda h: h.transpose(out, in_, ident), reads=reads, writes=writes)

    def act(self, out, in_, func, reads, writes, scale=None, bias=None, accum=None):
        kw = {}
        if scale is not None:
            kw["scale"] = scale
        if bias is not None:
            kw["bias"] = bias
        if accum is not None:
            kw["accum_out"] = accum
        return self.S.op("act", lambda h: h.activation(out=out, in_=in_, func=func, **kw), reads=reads, writes=writes)

    def ts(self, eng, out, in0, s1, s2, op0, op1, reads, writes):
        if op1 is None:
            return self.S.op(eng, lambda h: h.tensor_scalar(out=out, in0=in0, scalar1=s1, scalar2=None, op0=op0),
                             reads=reads, writes=writes)
        return self.S.op(eng, lambda h: h.tensor_scalar(out=out, in0=in0, scalar1=s1, scalar2=s2, op0=op0, op1=op1),
                         reads=reads, writes=writes)

    def tt(self, eng, out, in0, in1, op, reads, writes):
        return self.S.op(eng, lambda h: h.tensor_tensor(out=out, in0=in0, in1=in1, op=op), reads=reads, writes=writes)

    def stt(self, out, in0, scalar, in1, op0, op1, reads, writes, accum=None):
        if accum is None:
            return self.S.op("dve", lambda h: h.scalar_tensor_tensor(out=out, in0=in0, scalar=scalar, in1=in1, op0=op0, op1=op1),
                             reads=reads, writes=writes)
        return self.S.op("dve", lambda h: h.scalar_tensor_tensor(out=out, in0=in0, scalar=scalar, in1=in1, op0=op0, op1=op1,
                                                                  accum_out=accum), reads=reads, writes=writes)

    def cp(self, eng, out, in_, reads, writes):
        if eng == "act":
            return self.S.op("act", lambda h: h.copy(out=out, in_=in_), reads=reads, writes=writes)
        return self.S.op(eng, lambda h: h.tensor_copy(out=out, in_=in_), reads=reads, writes=writes)

    def recip(self, out, in_, reads, writes):
        return self.S.op("dve", lambda h: h.reciprocal(out=out, in_=in_), reads=reads, writes=writes)

    def memset(self, eng, ap, val, writes):
        return self.S.op(eng, lambda h: h.memset(ap, val), writes=writes)

    def dma(self, q, out, in_, reads, writes):
        return self.S.dma(q, lambda h: h.dma_start(out=out, in_=in_), reads=reads, writes=writes)


def build_program(SEQ, debug=False, stop_after=None):
    from contextlib import ExitStack
    nc = bass.Bass("TRN2", target_bir_lowering=False)
    NT = SEQ // 128
    NG = SEQ // 512
    ext_in = lambda n, shp, dt=F32: nc.dram_tensor(n, shp, dt, kind="ExternalInput").ap()
    x_d = ext_in("x", [SEQ, D])
    pos_d = ext_in("pos", [SEQ, 1], I32)
    attn_g_d = ext_in("attn_norm_g", [1, D])
    w_in_d = ext_in("w_in", [D, INDIM])
    lam_d = ext_in("diff_lambda", [1, 256])
    subg_d = ext_in("diff_subln_g", [1, 128])
    pek_d = ext_in("cmp_pe_k", [32, 64])
    pev_d = ext_in("cmp_pe_v", [32, 64])
    w1k_d = ext_in("cmp_k_w1", [2048, 256])
    w2k_d = ext_in("cmp_k_w2", [256, 64])
    w1v_d = ext_in("cmp_v_w1", [2048, 256])
    w2v_d = ext_in("cmp_v_w2", [256, 64])
    wa_d = ext_in("w_branch_a", [512, D])
    wb_d = ext_in("w_branch_b", [512, D])
    wout_d = ext_in("w_out", [D, D])
    ffn_g_d = ext_in("ffn_norm_g", [1, D])
    wg_d = ext_in("w_gate", [D, DFF])
    wu_d = ext_in("w_up", [D, DFF])
    wd_d = ext_in("w_down", [DFF, D])
    fin_g_d = ext_in("final_norm_g", [1, D])
    ident_d = ext_in("ident", [128, 128])
    maskcw_d = ext_in("maskcw", [128, 8, 512])
    cmask_d = ext_in("cmask", [128, 2, SEQ])
    E_d = ext_in("E", [64, NT, 128])
    ov_d = ext_in("ov", [128, 2, 64])
    btab_d = ext_in("btab", [SEQ, 64])
    invf_d = ext_in("invf", [1, 32])
    y_d = nc.dram_tensor("y", [SEQ, D], F32, kind="ExternalOutput").ap()
    skind = "ExternalOutput" if debug else "Internal"
    scr = lambda n, shp, dt: nc.dram_tensor(n, shp, dt, kind=skind).ap()
    xTn_s = scr("xTn_s", [NG, 128, 8, 512], BF16)
    yaT_s = scr("yaT_s", [NG, 128, 4, 512], BF16)
    ybT_s = scr("ybT_s", [NG, 128, 4, 512], BF16)
    h_s = scr("h_s", [SEQ, D], F32)
    wgu_s = scr("wgu_s", [22, 128, 2, 8, 128], BF16)

    with ExitStack() as st:
        S = Sched(nc, st)
        K = KB(nc, S, st)

        def _finish():
            S.final_all()
            with nc.allow_non_contiguous_dma(reason="small strided constant loads"):
                S.emit()
            return nc
        sb = lambda n, shp, dt: st.enter_context(nc.sbuf_tensor(n, shp, dt))
        banks = [st.enter_context(nc.psum_tensor(f"bank{i}", [128, 512], F32)) for i in range(8)]
        bres = [Res(f"bank{i}", excl=True) for i in range(8)]
        bk = lambda i: banks[i][:]
        bkb = lambda i: banks[i][:].bitcast(BF16)
        ident_bf = sb("ident_bf", [128, 128], BF16)
        ident_f = sb("ident_f", [128, 128], F32)
        cos_t = sb("cos_t", [128, NT, 32], F32)
        sin_t = sb("sin_t", [128, NT, 32], F32)
        g_attn = sb("g_attn", [128, 8], F32)
        g_ffn = sb("g_ffn", [128, 8], F32)
        gfin_b = sb("gfin_b", [128, D], F32)
        g08 = sb("g08", [128, 128], F32)
        neglam = sb("neglam", [128, 1], F32)
        maskcw = sb("maskcw_sb", [128, 8, 512], BF16)
        gate = sb("gate", [128, NT, 24], F32)
        R_const = Res("const")
        R_gate = Res("gate")
        ARENA_B = 148 * 1024
        arena = sb("arena", [128, ARENA_B // 2], BF16)

        class Carver:
            def __init__(self):
                self.off = 0

            def take(self, nbytes_pp, dt, shape_free, parts=128):
                assert self.off % 4 == 0
                n2 = (nbytes_pp + 3) // 4 * 4
                assert self.off + n2 <= ARENA_B, ("arena overflow", self.off + n2)
                v = arena[0:parts, self.off // 2:(self.off + nbytes_pp) // 2]
                self.off += n2
                if dt == F32:
                    v = v.bitcast(F32)
                return v

            def t(self, dt, *free, parts=128):
                n = 1
                for f in free:
                    n *= f
                esz = 4 if dt == F32 else 2
                v = self.take(n * esz, dt, free, parts)
                if len(free) == 2:
                    v = v.rearrange("p (a b) -> p a b", a=free[0])
                elif len(free) == 3:
                    v = v.rearrange("p (a b c) -> p a b c", a=free[0], b=free[1])
                elif len(free) == 4:
                    v = v.rearrange("p (a b c d) -> p a b c d", a=free[0], b=free[1], c=free[2])
                return v

        cv = Carver()
        K.dma("pool", ident_bf[:], ident_d, [], [R_const])
        K.dma("sp", ident_f[:], ident_d, [], [R_const])
        K.dma("pool", maskcw[:], maskcw_d, [], [R_const])
        K.dma("sp", g_attn[:], attn_g_d.rearrange("o (k p) -> p (o k)", p=128), [], [R_const])
        K.dma("sp", g_ffn[:], ffn_g_d.rearrange("o (k p) -> p (o k)", p=128), [], [R_const])
        K.dma("sp", gfin_b[:], fin_g_d.partition_broadcast(128), [], [R_const])
        K.dma("sp", g08[:], subg_d.partition_broadcast(128), [], [R_const])
        K.ts("dve", g08[:], g08[:], 0.8, None, ALU.mult, None, [R_const], [R_const])
        pos_i = cv.t(I32 if False else F32, NT)
        pos_i32 = pos_i.bitcast(I32)
        pos_f = cv.t(F32, NT)
        invf_b = cv.t(F32, 32)
        ang = cv.t(F32, NT, 32)
        tmpa = cv.t(F32, NT, 32)
        tmpi = cv.t(F32, NT, 32)
        tmpi_i = tmpi.bitcast(I32)
        R_rope = Res("rope")
        K.dma("sp", pos_i32, pos_d.rearrange("(t p) o -> p (t o)", p=128), [], [R_rope])
        K.dma("sp", invf_b, invf_d.partition_broadcast(128), [], [R_rope])
        K.cp("dve", pos_f, pos_i32, [R_rope], [R_rope])
        K.tt("dve", ang, pos_f.unsqueeze(2).broadcast_to([128, NT, 32]), invf_b.unsqueeze(1).broadcast_to([128, NT, 32]),
             ALU.mult, [R_rope], [R_rope])
        TWO_PI = 2.0 * math.pi
        for (dst, shift) in ((sin_t, 0.0), (cos_t, math.pi / 2)):
            K.ts("dve", tmpa, ang, shift, 1.0 / TWO_PI, ALU.add, ALU.mult, [R_rope], [R_rope])
            K.cp("dve", tmpi_i, tmpa, [R_rope], [R_rope])
            K.cp("dve", tmpa, tmpi_i, [R_rope], [R_rope])
            K.ts("dve", tmpa, tmpa, -TWO_PI, shift, ALU.mult, ALU.add, [R_rope], [R_rope])
            K.tt("dve", tmpa, tmpa, ang, ALU.add, [R_rope], [R_rope])
            K.ts("dve", tmpi, tmpa, math.pi, -TWO_PI, ALU.is_gt, ALU.mult, [R_rope], [R_rope])
            K.tt("dve", tmpa, tmpa, tmpi, ALU.add, [R_rope], [R_rope])
            K.ts("dve", tmpi, tmpa, -math.pi, TWO_PI, ALU.is_lt, ALU.mult, [R_rope], [R_rope])
            K.tt("dve", tmpa, tmpa, tmpi, ALU.add, [R_rope], [R_rope])
            K.ts("dve", tmpa, tmpa, math.pi, -math.pi, ALU.min, ALU.max, [R_rope], [R_rope])
            K.act(dst[:], tmpa, AF.Sin, [R_rope], [R_const])
        lam_sb = cv.t(F32, 256, parts=1)
        lam_j = cv.t(F32, 64, parts=1)
        lam_s = cv.t(F32, 4, parts=1)
        ones_row = cv.t(F32, 128, parts=1)
        R_lam = Res("lam")
        K.dma("sp", lam_sb, lam_d, [], [R_lam])
        K.memset("dve", ones_row, 1.0, [R_lam])
        K.stt(lam_j, lam_sb[:, 0:64], 1.0, lam_sb[:, 64:128], ALU.mult, ALU.mult, [R_lam], [R_lam], accum=lam_s[:, 0:1])
        K.stt(lam_j, lam_sb[:, 128:192], 1.0, lam_sb[:, 192:256], ALU.mult, ALU.mult, [R_lam], [R_lam], accum=lam_s[:, 1:2])
        K.act(lam_s[:, 2:4], lam_s[:, 0:2], AF.Exp, [R_lam], [R_lam])
        K.tt("dve", lam_s[:, 0:1], lam_s[:, 3:4], lam_s[:, 2:3], ALU.subtract, [R_lam], [R_lam])
        K.ts("dve", lam_s[:, 0:1], lam_s[:, 0:1], -0.2, None, ALU.add, None, [R_lam], [R_lam])
        K.mm(bk(0)[:, 0:1], ones_row, lam_s[:, 0:1], True, True, [R_lam], [bres[0]])
        K.cp("dve", neglam[:], bk(0)[:, 0:1], [bres[0]], [R_const])
        wst = [cv.t(BF16, 2, 8, 128) for _ in range(2)]
        wst_r = [Res("wst0"), Res("wst1")]
        R_wgu = Res("wgu_s")
        for j in range(22):
            b = j % 2
            K.dma("pool", wst[b][:, 0], wg_d[:, j * 128:(j + 1) * 128].rearrange("(k p) c -> p k c", p=128), [], [wst_r[b]])
            K.dma("pool", wst[b][:, 1], wu_d[:, j * 128:(j + 1) * 128].rearrange("(k p) c -> p k c", p=128), [], [wst_r[b]])
            K.dma("sp", wgu_s[j], wst[b], [wst_r[b]], [R_wgu])
        S.barrier()
        if stop_after == "P0":
            return _finish()

        cv = Carver()
        xt = [cv.t(F32, D) for _ in range(2)]
        xs = [cv.t(F32, D) for _ in range(2)]
        junk = cv.t(F32, D)
        stg = [cv.t(BF16, 8, 512) for _ in range(2)]
        st1 = [cv.t(F32, 4) for _ in range(2)]
        xt_r = [Res(), Res()]
        xs_r = [Res(), Res()]
        stg_r = [Res(), Res()]
        st1_r = [Res(), Res()]
        junk_r = Res()
        R_xTn = Res("xTn_s")

        def nt_A(src_tile_ap, t, b, xt_ap, xt_res, load=True):
            if load:
                K.dma("sp", xt_ap, src_tile_ap, [], [xt_res])
            K.act(junk, xt_ap, AF.Square, [xt_res], [junk_r], accum=st1[b][:, 0:1])
            K.act(st1[b][:, 1:2], st1[b][:, 0:1], AF.Ln, [junk_r, R_const], [st1_r[b]], scale=1.0 / D, bias=eps_ap)
            K.act(st1[b][:, 2:3], st1[b][:, 1:2], AF.Exp, [st1_r[b]], [st1_r[b]], scale=-0.5)
            K.ts("dve", xs[b], xt_ap, st1[b][:, 2:3], None, ALU.mult, None, [xt_res, st1_r[b]], [xs_r[b]])

        def nt_B(t, b, gcol, stage, stage_r, pb):
            for half in range(2):
                pbank = pb[half]
                for kq in range(4):
                    k = half * 4 + kq
                    K.tr(bk(pbank)[:, kq * 128:(kq + 1) * 128], xs[b][:, k * 128:(k + 1) * 128], ident_f[:],
                         [xs_r[b], R_const], [bres[pbank]])
                K.tt("dve", stage[:, half * 4:half * 4 + 4, (t % 4) * 128:(t % 4 + 1) * 128],
                     bk(pbank).rearrange("p (a b) -> p a b", a=4),
                     gcol[:, half * 4:half * 4 + 4].unsqueeze(2).broadcast_to([128, 4, 128]), ALU.mult,
                     [bres[pbank], R_const], [stage_r])

        eps_t = sb("eps_t", [128, 1], F32)
        K.memset("dve", eps_t[:], EPS, [R_const])
        eps_ap = eps_t[:]
        for tt_ in range(NT + 1):
            if tt_ < NT:
                nt_A(x_d[tt_ * 128:(tt_ + 1) * 128, :], tt_, tt_ % 2, xt[tt_ % 2], xt_r[tt_ % 2])
            if tt_ >= 1:
                t = tt_ - 1
                b = t % 2
                g = t // 4
                sgb = g % 2
                nt_B(t, b, g_attn, stg[sgb], stg_r[sgb], (2 * b, 2 * b + 1))
                if t % 4 == 3:
                    K.dma("sp", xTn_s[g], stg[sgb], [stg_r[sgb]], [R_xTn])
        S.barrier()
        if stop_after == "P1":
            return _finish()

        class AttnPipe:
            def __init__(self, st_banks, pbufs, pres, look=1):
                self.stb = st_banks
                self.pb = pbufs
                self.pr = pres
                self.look = look
                self.i = 0
                self.pend = []
                self.deferred = []

            def defer(self, fn, nsteps):
                self.deferred.append([nsteps, fn])

            def _tick(self):
                ready = [d for d in self.deferred if d[0] <= 1]
                self.deferred = [[d[0] - 1, d[1]] for d in self.deferred if d[0] > 1]
                for d in ready:
                    d[1]()

            def step(self, units):
                ent = []
                for (qk, nrows, pv, post) in units:
                    i = self.i
                    self.i += 1
                    bnk = self.stb[i % len(self.stb)]
                    pi = i % len(self.pb)
                    qk(bnk)
                    ent.append((bnk, pi, nrows, pv, post))
                for ui, (bnk, pi, nrows, pv, post) in enumerate(ent):
                    elo, ehi = getattr(units[ui][0], "exp_cols", (0, 512))
                    K.act(self.pb[pi][0:nrows, elo:ehi], bk(bnk)[0:nrows, elo:ehi], AF.Exp, [bres[bnk]], [self.pr[pi]], scale=0.125)
                    mk = getattr(units[ui][0], "mask01", None)
                    if mk is not None:
                        lo, hi = units[ui][0].mask_cols
                        K.tt("pool" if ui == 0 else "dve", self.pb[pi][0:nrows, lo:hi], self.pb[pi][0:nrows, lo:hi], mk[:, lo:hi],
                             ALU.mult, [self.pr[pi], R_const], [self.pr[pi]])
                self.pend.append(ent)
                if len(self.pend) > self.look:
                    self._flush1()
                self._tick()

            def unit(self, qk, nrows, pv, post=None):
                self.step([(qk, nrows, pv, post)])

            def _flush1(self):
                ent = self.pend.pop(0)
                for (bnk, pi, nrows, pv, post) in ent:
                    pv(self.pb[pi], self.pr[pi])
                for (bnk, pi, nrows, pv, post) in ent:
                    if post is not None:
                        post()

            def flush(self):
                while self.pend:
                    self._flush1()
                while self.deferred:
                    self._tick()

        cv = Carver()
        xg = [cv.t(BF16, 8, 512) for _ in range(2)]
        xg_r = [Res(), Res()]
        qkT = cv.t(BF16, 2, SEQ)
        dv_aug = cv.t(BF16, NT, 130)
        W_hs = [cv.t(BF16, 8, 384) for _ in range(2)]
        qk_tok = [cv.t(BF16, 256) for _ in range(2)]
        rtmp = [cv.t(F32, 4, 128) for _ in range(2)]
        pbufs = [cv.t(BF16, 512) for _ in range(6)]
        tb_ = [cv.t(F32, 2, 4, 128) for _ in range(2)]
        ya4 = [cv.t(F32, 4, 128) for _ in range(2)]
        yj = cv.t(F32, 128)
        yan4 = [cv.t(BF16, 4, 128) for _ in range(2)]
        sst = [cv.t(F32, 12) for _ in range(2)]
        rz = [cv.t(F32, 2, 4) for _ in range(2)]
        mhalf = cv.t(F32, 4)
        tb_r = [Res(), Res()]
        gcount = [0]
        yaT_st = [cv.t(BF16, 512) for _ in range(2)]
        R_qkT, R_dv = Res(), Res()
        R_Whs = [Res(), Res()]
        qk_tok_r = [Res(), Res()]
        rtmp_r = [Res(), Res()]
        pres = [Res() for _ in range(6)]
        t0_r, yj_r = Res(), Res()
        ya_r = [Res(), Res()]
        yan_r = [Res(), Res()]
        sst_r = [Res(), Res()]
        rz_r = [Res(), Res()]
        yaT_st_r = [Res(), Res()]
        R_yaT = Res("yaT_s")
        K.memset("dve", dv_aug[:, :, 128:129], 1.0, [R_dv])
        K.memset("dve", mhalf, -0.5, [R_const])

        def rope(eng, out_v, in_v, t, nh, tmp, tmp_r, in_res, out_res):
            shp = [128, nh, 32]
            cb = cos_t[:, t, :].unsqueeze(1).broadcast_to(shp)
            sbb = sin_t[:, t, :].unsqueeze(1).broadcast_to(shp)
            x1 = in_v[:, :, 0, :]
            x2 = in_v[:, :, 1, :]
            tv = [tmp[:, i, 0:nh * 32].rearrange("p (h d) -> p h d", h=nh) for i in range(4)]
            K.tt(eng, tv[0], x1, cb, ALU.mult, in_res + [R_const], [tmp_r])
            K.tt(eng, tv[1], x2, sbb, ALU.mult, in_res + [R_const], [tmp_r])
            K.tt(eng, tv[2], x2, cb, ALU.mult, in_res + [R_const], [tmp_r])
            K.tt(eng, tv[3], x1, sbb, ALU.mult, in_res + [R_const], [tmp_r])
            K.tt(eng, out_v[:, :, 0, :], tv[0], tv[1], ALU.subtract, [tmp_r], out_res)
            K.tt(eng, out_v[:, :, 1, :], tv[2], tv[3], ALU.add, [tmp_r], out_res)

        for hd in range(4):
            for hn in ([0, 1] if hd == 0 else [hd + 1]):
                if hn < 4:
                    for i, c0 in enumerate((C_DQ + hn * 128, C_DK + hn * 128, C_DV + hn * 128)):
                        K.dma("pool", W_hs[hn % 2][:, :, i * 128:(i + 1) * 128],
                              w_in_d[:, c0:c0 + 128].rearrange("(k p) c -> p k c", p=128), [], [R_Whs[hn % 2]])
            W_h = W_hs[hd % 2]
            R_Wh = R_Whs[hd % 2]
            def projA(t):
                g = t // 4
                gb = g % 2
                b = t % 2
                if t % 4 == 0:
                    K.dma("sp", xg[gb], xTn_s[g], [R_xTn], [xg_r[gb]])
                pbank = b
                for k in range(8):
                    K.mm(bk(pbank)[:, 0:384], xg[gb][:, k, (t % 4) * 128:(t % 4 + 1) * 128], W_h[:, k, :],
                         k == 0, k == 7, [xg_r[gb], R_Wh], [bres[pbank]])
                pv = bk(pbank)[:, 0:256].rearrange("p (h c d) -> p h c d", h=4, c=2)
                ov_ = qk_tok[b].rearrange("p (h c d) -> p h c d", h=4, c=2)
                rope("dve", ov_, pv, t, 4, rtmp[b], rtmp_r[b], [bres[pbank]], [qk_tok_r[b]])
                K.cp("act", dv_aug[:, t, 0:128], bk(pbank)[:, 256:384], [bres[pbank]], [R_dv])

            def projB(t):
                b = t % 2
                tb = 2 + b
                for i in range(2):
                    K.tr(bkb(tb)[:, i * 128:(i + 1) * 128], qk_tok[b][:, i * 128:(i + 1) * 128], ident_bf[:],
                         [qk_tok_r[b], R_const], [bres[tb]])
                K.cp("act", qkT[:, :, t * 128:(t + 1) * 128], bkb(tb)[:, 0:256].rearrange("p (a b) -> p a b", a=2),
                     [bres[tb]], [R_qkT])

            for tt_ in range(NT + 1):
                if tt_ < NT:
                    projA(tt_)
                if tt_ >= 1:
                    projB(tt_ - 1)
            if stop_after == "P2a":
                return _finish()
            pipe = AttnPipe([4, 5, 6, 7], pbufs, pres, look=2)
            for C in range(NG):
                nk = 4 * C + 4
                for kt in range(nk):
                    diag = kt - 4 * C
                    units = []
                    for s in range(2):
                        ob = (0, 1) if s == 0 else (2, 3)

                        def qk(bnk, kt=kt, s=s, C=C, diag=diag):
                            K.mm(bk(bnk), qkT[64 * s:64 * s + 64, 1, kt * 128:(kt + 1) * 128],
                                 qkT[64 * s:64 * s + 64, 0, C * 512:(C + 1) * 512], True, True, [R_qkT], [bres[bnk]])
                        if diag >= 0:
                            qk.mask01 = maskcw[:, diag, :]
                            qk.mask_cols = (128 * diag, 128 * diag + 128)
                            qk.exp_cols = (128 * diag, 512)

                        def pv(P, Pr, kt=kt, ob=ob, diag=diag, nk=nk):
                            for qs in range(4):
                                if diag >= 0 and qs < diag:
                                    continue
                                bnk = ob[qs // 2]
                                c0 = (qs % 2) * 129
                                first = (kt == 0 and qs % 2 == 0)
                                K.mm(bk(bnk)[:, c0:c0 + 129], P[:, qs * 128:(qs + 1) * 128], dv_aug[:, kt, 0:129],
                                     first, kt == nk - 1, [Pr, R_dv], [bres[bnk]])

                        post = None
                        if kt == nk - 1:
                            def post(s=s, C=C, ob=ob, hd=hd):
                                gi = gcount[0] % 2
                                for half in range(2):
                                    bnk = ob[half]
                                    K.recip(rz[gi][:, s, 2 * half:2 * half + 2], bk(bnk)[:, 128:258:129], [bres[bnk]], [rz_r[gi]])
                                for qs in range(4):
                                    bnk = ob[qs // 2]
                                    c0 = (qs % 2) * 129
                                    K.ts("dve", tb_[gi][:, s, qs, :], bk(bnk)[:, c0:c0 + 128], rz[gi][:, s, qs:qs + 1], None,
                                         ALU.mult, None, [bres[bnk], rz_r[gi]], [tb_r[gi]])
                                if s == 1:
                                    gcount[0] += 1

                                    def post_b1(gi=gi):
                                        for qs in range(4):
                                            K.stt(ya4[gi][:, qs, :], tb_[gi][:, 1, qs, :], neglam[:], tb_[gi][:, 0, qs, :], ALU.mult, ALU.add,
                                                  [tb_r[gi], R_const], [ya_r[gi]])
                                            K.stt(yj, ya4[gi][:, qs, :], 1.0, ya4[gi][:, qs, :], ALU.mult, ALU.mult, [ya_r[gi]], [yj_r],
                                                  accum=sst[gi][:, qs:qs + 1])
                                        K.ts("pool", sst[gi][:, 4:8], sst[gi][:, 0:4], 1.0 / 128, EPS, ALU.mult, ALU.add, [yj_r], [sst_r[gi]])
                                        K.tt("pool", sst[gi][:, 8:12], sst[gi][:, 4:8], mhalf[:, 0:4], ALU.pow, [sst_r[gi], R_const], [sst_r[gi]])
                                        for qs in range(4):
                                            K.stt(yan4[gi][:, qs, :], ya4[gi][:, qs, :], sst[gi][:, 8 + qs:9 + qs], g08[:], ALU.mult, ALU.mult,
                                                  [ya_r[gi], sst_r[gi], R_const], [yan_r[gi]])

                                    def post_b2(gi=gi, C=C, hd=hd):
                                        for qs in range(4):
                                            K.tr(bkb(7)[:, qs * 128:(qs + 1) * 128], yan4[gi][:, qs, :], ident_bf[:], [yan_r[gi], R_const], [bres[7]])
                                        K.cp("dve", yaT_st[gi], bkb(7)[:, 0:512], [bres[7]], [yaT_st_r[gi]])
                                        K.dma("sp", yaT_s[C, :, hd, :], yaT_st[gi], [yaT_st_r[gi]], [R_yaT])
                                    pipe.defer(post_b1, 2)
                                    pipe.defer(post_b2, 5)
                        units.append((qk, 128, pv, post))
                    pipe.step(units)
            pipe.flush()
        S.barrier()
        if stop_after == "P2":
            return _finish()

        cv = Carver()
        nqT = cv.t(BF16, 4, SEQ)
        KT = cv.t(BF16, 4, SEQ)
        vs_aug = cv.t(BF16, NT, 2, 66)
        vw_aug = cv.t(BF16, NT, 2, 66)
        R_nqT, R_KT, R_vs, R_vw = Res(), Res(), Res(), Res()
        mark = cv.off
        xg = [cv.t(BF16, 8, 512) for _ in range(2)]
        xg_r = [Res(), Res()]
        Wn = cv.t(BF16, 8, 1304)
        R_Wn = Res()
        nq_tok = [cv.t(BF16, 512) for _ in range(2)]
        k_tok = [cv.t(BF16, 4, 128) for _ in range(2)]
        rtmpn = [cv.t(F32, 4, 256) for _ in range(2)]
        rtmpk = [cv.t(F32, 4, 192) for _ in range(2)]
        rtmpk_r = [Res(), Res()]
        gtmp = [cv.t(F32, 24) for _ in range(2)]
        nq_tok_r, k_tok_r, rtmpn_r, gtmp_r = [Res(), Res()], [Res(), Res()], [Res(), Res()], [Res(), Res()]
        for (d0, n_, c0) in ((0, 640, 1536), (640, 128, 2304), (768, 128, 2560), (896, 128, 2176), (1024, 128, 2432), (1152, 152, 2688)):
            K.dma("pool", Wn[:, :, d0:d0 + n_], w_in_d[:, c0:c0 + n_].rearrange("(k p) c -> p k c", p=128), [], [R_Wn])
        K.memset("dve", vs_aug[:, :, :, 64:65], 1.0, [R_vs])
        K.memset("dve", vw_aug[:, :, :, 64:65], 1.0, [R_vw])
        def p3A(t):
            g = t // 4
            gb = g % 2
            b = t % 2
            if t % 4 == 0:
                K.dma("sp", xg[gb], xTn_s[g], [R_xTn], [xg_r[gb]])
            pb3 = (0, 1, 2) if b == 0 else (3, 4, 5)
            for bi, (c0, cn) in enumerate(((0, 512), (512, 512), (1024, 280))):
                for k in range(8):
                    K.mm(bk(pb3[bi])[:, 0:cn], xg[gb][:, k, (t % 4) * 128:(t % 4 + 1) * 128], Wn[:, k, c0:c0 + cn],
                         k == 0, k == 7, [xg_r[gb], R_Wn], [bres[pb3[bi]]])
            A, Bk, Ck = pb3
            shp = [128, 2, 4, 32]
            cb4 = cos_t[:, t, :].unsqueeze(1).unsqueeze(1).broadcast_to(shp)
            sb4 = sin_t[:, t, :].unsqueeze(1).unsqueeze(1).broadcast_to(shp)
            pin = bk(A).rearrange("p (g j c d) -> p g j c d", g=2, j=4, c=2)
            pout = nq_tok[b].rearrange("p (j g c d) -> p g j c d", j=4, g=2, c=2)
            tv = [rtmpn[b][:, i, 0:256].rearrange("p (g j d) -> p g j d", g=2, j=4) for i in range(4)]
            x1, x2 = pin[:, :, :, 0, :], pin[:, :, :, 1, :]
            rr = [bres[A], R_const]
            K.tt("dve", tv[0], x1, cb4, ALU.mult, rr, [rtmpn_r[b]])
            K.tt("dve", tv[1], x2, sb4, ALU.mult, rr, [rtmpn_r[b]])
            K.tt("dve", tv[2], x2, cb4, ALU.mult, rr, [rtmpn_r[b]])
            K.tt("dve", tv[3], x1, sb4, ALU.mult, rr, [rtmpn_r[b]])
            K.tt("dve", pout[:, :, :, 0, :], tv[0], tv[1], ALU.subtract, [rtmpn_r[b]], [nq_tok_r[b]])
            K.tt("dve", pout[:, :, :, 1, :], tv[2], tv[3], ALU.add, [rtmpn_r[b]], [nq_tok_r[b]])
            kin = bk(Bk)[:, 0:384].rearrange("p (h c d) -> p h c d", h=6, c=2)
            rope("dve", k_tok[b][:, 0:3, :].rearrange("p s (h c d) -> p (s h) c d", h=2, c=2), kin, t, 6,
                 rtmpk[b], rtmpk_r[b], [bres[Bk]], [k_tok_r[b]])
            K.cp("act", k_tok[b][:, 3, :], bk(Bk)[:, 384:512], [bres[Bk]], [k_tok_r[b]])
            K.cp("act", vs_aug[:, t, :, 0:64], bk(Ck)[:, 0:128].rearrange("p (g d) -> p g d", g=2), [bres[Ck]], [R_vs])
            K.cp("act", vw_aug[:, t, :, 0:64], bk(Ck)[:, 128:256].rearrange("p (g d) -> p g d", g=2), [bres[Ck]], [R_vw])
            K.act(gtmp[b], bk(Ck)[:, 256:280], AF.Exp, [bres[Ck]], [gtmp_r[b]], scale=-1.0)
            K.ts("dve", gtmp[b], gtmp[b], 1.0, None, ALU.add, None, [gtmp_r[b]], [gtmp_r[b]])
            K.recip(gate[:, t, :], gtmp[b], [gtmp_r[b]], [R_gate])

        def p3B(t):
            b = t % 2
            tb = 6 + b
            for i in range(4):
                K.tr(bkb(tb)[:, i * 128:(i + 1) * 128], nq_tok[b][:, i * 128:(i + 1) * 128], ident_bf[:],
                     [nq_tok_r[b], R_const], [bres[tb]])
            for i in range(4):
                K.tr(bkb(tb)[:, 512 + i * 128:512 + (i + 1) * 128], k_tok[b][:, i, :], ident_bf[:],
                     [k_tok_r[b], R_const], [bres[tb]])
            K.cp("act", nqT[:, :, t * 128:(t + 1) * 128], bkb(tb)[:, 0:512].rearrange("p (a b) -> p a b", a=4), [bres[tb]], [R_nqT])
            K.cp("act", KT[:, :, t * 128:(t + 1) * 128], bkb(tb)[:, 512:1024].rearrange("p (a b) -> p a b", a=4), [bres[tb]], [R_KT])

        for tt_ in range(NT + 1):
            if tt_ < NT:
                p3A(tt_)
            if tt_ >= 1:
                p3B(tt_ - 1)
        S.barrier()
        if stop_after == "P3a":
            return _finish()

        cv.off = mark
        kcmpT = cv.t(BF16, 256)
        vc_aug = cv.t(BF16, 2, 2, 130)
        mark = cv.off
        w1 = [cv.t(BF16, 32, 256) for _ in range(2)]
        w2kd = cv.t(BF16, 2, 128)
        w2v = cv.t(BF16, 2, 64)
        peT = cv.t(BF16, 2, 32)
        cb_sb = cv.t(F32, 2, 2)
        hid_sb = cv.t(BF16, 2, 2, 2, 256)
        R_w1, R_w2, R_pe, R_cb, R_hid, R_kcmp, R_vca = Res(), Res(), Res(), Res(), Res(), Res(), Res()
        for kv, wd_ in enumerate((w1k_d, w1v_d)):
            for half in range(2):
                K.dma("pool", w1[kv][64 * half:64 * half + 64], wd_.rearrange("(l d) h -> d l h", d=64), [], [R_w1])
        for half in range(2):
            K.dma("pool", w2kd[:, :, 64 * half:64 * half + 64], w2k_d.rearrange("(c p) d -> p c d", p=128), [], [R_w2])
        K.dma("pool", w2v, w2v_d.rearrange("(c p) d -> p c d", p=128), [], [R_w2])
        with nc.allow_non_contiguous_dma(reason="tiny pe transpose load"):
            for kv, pd in enumerate((pek_d, pev_d)):
                for half in range(2):
                    K.dma("pool", peT[64 * half:64 * half + 64, kv, :], pd.rearrange("l d -> d l"), [], [R_pe])
        K.memset("dve", vc_aug[:, :, :, 64:65], 1.0, [R_vca])
        for g in range(2):
            K.dma("pool", vc_aug[:, g, :, 66:130], ov_d, [], [R_vca])
        for kv in range(2):
            for hc in range(2):
                col = kv * 2 + hc
                for l in range(32):
                    K.mm(bk(0)[:, col:col + 1], w1[kv][0:64, l, hc * 128:(hc + 1) * 128], peT[0:64, kv, l:l + 1],
                         l == 0 and col == 0, l == 31, [R_w1, R_pe], [bres[0]])
        K.cp("dve", cb_sb.rearrange("p a b -> p (a b)"), bk(0)[:, 0:4], [bres[0]], [R_cb])
        NCB = (SEQ - 32) // 16 + 1
        ui = 0
        for kv in range(2):
            for g in range(2):
                for hc in range(2):
                    bnk = 1 + ui % 3
                    ui += 1
                    for l in range(32):
                        K.mm(bk(bnk)[:, 0:NCB], w1[kv][64 * g:64 * g + 64, l, hc * 128:(hc + 1) * 128],
                             KT[64 * g:64 * g + 64, 3 * kv, l:l + 16 * (NCB - 1) + 1:16], l == 0, l == 31, [R_w1, R_KT], [bres[bnk]])
                    K.act(hid_sb[:, kv, g, hc, 0:NCB], bk(bnk)[:, 0:NCB], AF.Silu, [bres[bnk], R_cb], [R_hid],
                          bias=cb_sb[:, kv, hc:hc + 1])
        for g in range(2):
            for hc in range(2):
                K.mm(bk(4)[:, 0:NCB], w2kd[:, hc, :], hid_sb[:, 0, g, hc, 0:NCB], hc == 0, hc == 1, [R_w2, R_hid], [bres[4]])
            K.cp("dve", kcmpT[64 * g:64 * g + 64, 0:NCB], bk(4)[64 * g:64 * g + 64, 0:NCB], [bres[4]], [R_kcmp])
            for ct in range(2):
                n = min(128, NCB - ct * 128)
                if n <= 0:
                    continue
                for hc in range(2):
                    K.mm(bk(5)[0:n, ct * 64:ct * 64 + 64], hid_sb[:, 1, g, hc, ct * 128:ct * 128 + n], w2v[:, hc, :],
                         hc == 0 and ct == 0, hc == 1, [R_hid, R_w2], [bres[5]])
            for ct in range(2):
                n = min(128, NCB - ct * 128)
                if n <= 0:
                    continue
                K.cp("dve", vc_aug[0:n, g, ct, 0:64], bk(5)[0:n, ct * 64:ct * 64 + 64], [bres[5]], [R_vca])
        S.barrier()
        if stop_after == "P3b":
            return _finish()

        cv.off = mark
        E_sb = cv.t(BF16, NT, 128)
        pbufs = [cv.t(BF16, 512) for _ in range(6)]
        pres = [Res() for _ in range(6)]
        acc = cv.t(F32, 4, 512)
        imp = cv.t(F32, 4, 2, 64)
        imp2 = cv.t(F32, 64)
        selb = cv.t(F32, 64)
        selb_bf = cv.t(BF16, 4, 2, 64)
        R_selbf = Res()
        m8 = cv.t(F32, 16)
        selbT = cv.t(BF16, 512)
        btab = [cv.t(F32, 4, 64) for _ in range(2)]
        cmk = [cv.t(BF16, 2, 512) for _ in range(2)]
        rzn = [cv.t(F32, 8) for _ in range(2)]
        yb_bf = cv.t(BF16, 4, 512)
        ybT_st = [cv.t(BF16, 4, 512) for _ in range(2)]
        R_E, R_acc, R_imp, R_sel, R_selbT, R_ybbf = Res(), Res(), Res(), Res(), Res(), Res()
        btab_r, cmk_r, rzn_r, ybT_st_r = [Res(), Res()], [Res(), Res()], [Res(), Res()], [Res(), Res()]
        R_ybT = Res("ybT_s")
        K.dma("pool", E_sb[0:64], E_d, [], [R_E])
        K.dma("pool", E_sb[64:128], E_d, [], [R_E])
        hh_list = [(hh, hh // 4, hh % 4) for hh in range(8)]
        ecount = [0]

        def evac_branch(bnk, ncols, hh, br, C, with_imp):
            e = ecount[0] % 2
            ecount[0] += 1
            Ov = bk(bnk)[:, 0:4 * ncols].rearrange("p (a b) -> p a b", a=4)
            g = hh // 4
            K.ts("dve", rzn[e][:, 0:4], Ov[:, :, 64], 1e-30, None, ALU.add, None, [bres[bnk]], [rzn_r[e]])
            K.recip(rzn[e][:, 0:4], rzn[e][:, 0:4], [rzn_r[e]], [rzn_r[e]])
            K.tt("dve", rzn[e][:, 4:8], rzn[e][:, 0:4], gate[:, 4 * C:4 * C + 4, br * 8 + hh], ALU.mult,
                 [rzn_r[e], R_gate], [rzn_r[e]])
            for qs in range(4):
                dst = acc[:, qs, hh * 64:(hh + 1) * 64]
                if br == 0:
                    K.ts("dve", dst, Ov[:, qs, 0:64], rzn[e][:, 4 + qs:5 + qs], None, ALU.mult, None,
                         [bres[bnk], rzn_r[e]], [R_acc])
                else:
                    K.stt(dst, Ov[:, qs, 0:64], rzn[e][:, 4 + qs:5 + qs], dst, ALU.mult, ALU.add,
                          [bres[bnk], rzn_r[e], R_acc], [R_acc])
                if with_imp:
                    K.stt(imp[:, qs, g, :], Ov[:, qs, 65:129], rzn[e][:, qs:qs + 1], imp[:, qs, g, :], ALU.mult, ALU.add,
                          [bres[bnk], rzn_r[e], R_imp], [R_imp])

        pipe = AttnPipe([0, 1, 2, 3], pbufs, pres, look=2)
        obank_i = [0]
        pairs = [((j, 0, j), (4 + j, 1, j)) for j in range(4)]
        for C in range(NG):
            cbi = C % 2
            for Cn in ([0, 1] if C == 0 else [C + 1]):
                if Cn < NG:
                    K.dma("sp", btab[Cn % 2], btab_d[Cn * 512:(Cn + 1) * 512, :].rearrange("(a p) n -> p a n", p=128), [], [btab_r[Cn % 2]])
                    K.dma("pool", cmk[Cn % 2], cmask_d[:, :, Cn * 512:(Cn + 1) * 512], [], [cmk_r[Cn % 2]])
            pipe.flush()
            for g in range(2):
                K.cp("dve", imp[:, :, g, :], btab[cbi], [btab_r[cbi]], [R_imp])
            ncts = [ct for ct in range(2) if (ct == 0 or 32 * C + 30 >= 128) and NCB - ct * 128 > 0]
            for pair in pairs:
                for ui_, ct in enumerate(ncts):
                    n = min(128, NCB - ct * 128)
                    units = []
                    for pi_, (hh, g, j) in enumerate(pair):
                        oa = 4 + 2 * pi_
                        ib = oa + 1

                        def qk(bnk, ct=ct, n=n, g=g, j=j, C=C, cbi=cbi):
                            K.mm(bk(bnk)[0:n, :], kcmpT[64 * g:64 * g + 64, ct * 128:ct * 128 + n],
                                 nqT[64 * g:64 * g + 64, j, C * 512:(C + 1) * 512], True, False, [R_kcmp, R_nqT], [bres[bnk]])
                            K.mm(bk(bnk)[0:n, :], ident_bf[:, 0:n], cmk[cbi][:, ct, :], False, True, [R_const, cmk_r[cbi]], [bres[bnk]])

                        def pv(P, Pr, ct=ct, n=n, g=g, oa=oa, ib=ib, first=(ui_ == 0), last=(ui_ == len(ncts) - 1)):
                            for qs in range(4):
                                K.mm(bk(oa)[:, qs * 65:qs * 65 + 65], P[0:n, qs * 128:(qs + 1) * 128], vc_aug[0:n, g, ct, 0:65],
                                     first and qs == 0, last, [Pr, R_vca], [bres[oa]])
                                K.mm(bk(ib)[:, qs * 64:qs * 64 + 64], P[0:n, qs * 128:(qs + 1) * 128], vc_aug[0:n, g, ct, 66:130],
                                     first and qs == 0, last, [Pr, R_vca], [bres[ib]])
                        post = None
                        if ui_ == len(ncts) - 1:
                            def post(oa=oa, ib=ib, hh=hh, C=C, g=g):
                                e = ecount[0] % 2
                                evac_branch(oa, 65, hh, 0, C, False)
                                Iv = bk(ib)[:, 0:256].rearrange("p (a b) -> p a b", a=4)
                                for qs in range(4):
                                    K.stt(imp[:, qs, g, :], Iv[:, qs, :], rzn[e][:, qs:qs + 1], imp[:, qs, g, :], ALU.mult, ALU.add,
                                          [bres[ib], rzn_r[e], R_imp], [R_imp])
                        units.append((qk, n, pv, post))
                    pipe.step(units)
            pipe.flush()
            for qs in range(4):
                for g in range(2):
                    iv = imp[:, qs, g, :]
                    K.S.op("dve", lambda h, iv=iv: h.max(out=m8[:, 0:8], in_=iv), reads=[R_imp], writes=[R_sel])
                    K.S.op("dve", lambda h, iv=iv: h.match_replace(out=imp2, in_to_replace=m8[:, 0:8], in_values=iv, imm_value=-1e9),
                           reads=[R_imp, R_sel], writes=[R_sel])
                    K.S.op("dve", lambda h: h.max(out=m8[:, 8:16], in_=imp2), reads=[R_sel], writes=[R_sel])
                    K.ts("dve", selb, iv, m8[:, 15:16], None, ALU.is_ge, None, [R_imp, R_sel], [R_sel])
                    K.ts("dve", selb_bf[:, qs, g, :], selb, -1.0, -NEG, ALU.add, ALU.mult, [R_sel], [R_selbf])

            def sel_transposes():
                for qs in range(4):
                    K.tr(bkb(0)[:, qs * 128:(qs + 1) * 128], selb_bf[:, qs].rearrange("p g n -> p (g n)"), ident_bf[:],
                         [R_selbf, R_const], [bres[0]])
                K.cp("dve", selbT, bkb(0)[:, 0:512], [bres[0]], [R_selbT])
            for br in (2, 1):
                if br == 1:
                    pipe.flush()
                    sel_transposes()
                    kts = list(range(4 * C + 4))
                else:
                    kts = [kt for kt in range(4 * C - 4, 4 * C + 4) if kt >= 0]
                Vt = vs_aug if br == 1 else vw_aug
                Rv = R_vs if br == 1 else R_vw
                slot = 1 if br == 1 else 2
                for pair in pairs:
                    ob0 = 4 + 2 * (obank_i[0] % 2)
                    obank_i[0] += 1
                    for ui_, kt in enumerate(kts):
                        off = kt - 4 * C
                        units = []
                        for pi_, (hh, g, j) in enumerate(pair):
                            oa = ob0 + pi_

                            def qk(bnk, kt=kt, off=off, g=g, j=j, C=C, br=br, slot=slot):
                                K.mm(bk(bnk), KT[64 * g:64 * g + 64, slot, kt * 128:(kt + 1) * 128],
                                     nqT[64 * g:64 * g + 64, j, C * 512:(C + 1) * 512], True, br == 2, [R_KT, R_nqT], [bres[bnk]])
                                if br == 1:
                                    K.mm(bk(bnk), E_sb[64 * g:64 * g + 64, kt, :], selbT[64 * g:64 * g + 64, :], False, True,
                                         [R_E, R_selbT], [bres[bnk]])
                            qlo, qhi = 0, 4
                            if off >= 0:
                                qk.mask01 = maskcw[:, off, :]
                                qk.mask_cols = (128 * off, 128 * off + 128)
                                qk.exp_cols = (128 * off, 512)
                                qlo = off
                            elif br == 2:
                                jw = off + 4
                                qk.mask01 = maskcw[:, 4 + jw, :]
                                qk.mask_cols = (128 * jw, 128 * jw + 128)
                                qk.exp_cols = (0, 128 * jw + 128)
                                qhi = jw + 1

                            def pv(P, Pr, kt=kt, g=g, oa=oa, Vt=Vt, Rv=Rv, first=(ui_ == 0), last=(ui_ == len(kts) - 1), qlo=qlo, qhi=qhi):
                                for qs in range(qlo, qhi):
                                    K.mm(bk(oa)[:, qs * 65:qs * 65 + 65], P[:, qs * 128:(qs + 1) * 128], Vt[:, kt, g, 0:65],
                                         first and qs == 0, last, [Pr, Rv], [bres[oa]])
                            post = None
                            if ui_ == len(kts) - 1:
                                def post(oa=oa, hh=hh, C=C, br=br):
                                    evac_branch(oa, 65, hh, br, C, False)
                            units.append((qk, 128, pv, post))
                        pipe.step(units)
            pipe.flush()
            K.cp("dve", yb_bf, acc, [R_acc], [R_ybbf])
            for qs in range(4):
                for fc in range(4):
                    K.tr(bkb(1)[:, fc * 128:(fc + 1) * 128], yb_bf[:, qs, fc * 128:(fc + 1) * 128], ident_bf[:],
                         [R_ybbf, R_const], [bres[1]])
                K.cp("dve", ybT_st[cbi][:, :, qs * 128:(qs + 1) * 128], bkb(1)[:, 0:512].rearrange("p (a b) -> p a b", a=4),
                     [bres[1]], [ybT_st_r[cbi]])
            K.dma("pool", ybT_s[C], ybT_st[cbi], [ybT_st_r[cbi]], [R_ybT])
        S.barrier()
        if stop_after == "P3c":
            return _finish()

        cv = Carver()
        Wmg = cv.t(BF16, 8, 2048)
        Wa = cv.t(BF16, 4, D)
        Wb = cv.t(BF16, 4, D)
        Wo = cv.t(BF16, 8, D)
        xg = [cv.t(BF16, 8, 512) for _ in range(2)]
        yag = [cv.t(BF16, 4, 512) for _ in range(2)]
        ybg = [cv.t(BF16, 4, 512) for _ in range(2)]
        xt = [cv.t(F32, D) for _ in range(2)]
        mT = cv.t(BF16, 8, 512)
        sg = [cv.t(F32, 2, 512) for _ in range(2)]
        mm_ = [cv.t(F32, 2, 512) for _ in range(2)]
        ho = [cv.t(F32, D) for _ in range(2)]
        R_W4, R_mT, R_h = Res(), Res(), Res("h_s")
        xg_r, yag_r, ybg_r, xt_r, sg_r, mm_r, ho_r = ([Res(), Res()] for _ in range(7))
        for kq in range(4):
            K.dma("pool", Wmg[:, 2 * kq:2 * kq + 2, :], w_in_d[kq * 256:(kq + 1) * 256, C_MG:C_MG + 2048].rearrange("(k p) c -> p k c", p=128),
                  [], [R_W4])
        K.dma("pool", Wa, wa_d.rearrange("(k p) c -> p k c", p=128), [], [R_W4])
        K.dma("pool", Wb, wb_d.rearrange("(k p) c -> p k c", p=128), [], [R_W4])
        for kq in range(2):
            K.dma("pool", Wo[:, 4 * kq:4 * kq + 4, :], wout_d[kq * 512:(kq + 1) * 512, :].rearrange("(k p) c -> p k c", p=128), [], [R_W4])
        for G in range(NG):
            gb = G % 2
            for Gn in ([0, 1] if G == 0 else [G + 1]):
                if Gn < NG:
                    K.dma("sp", xg[Gn % 2], xTn_s[Gn], [R_xTn], [xg_r[Gn % 2]])
                    K.dma("sp", yag[Gn % 2], yaT_s[Gn], [R_yaT], [yag_r[Gn % 2]])
                    K.dma("sp", ybg[Gn % 2], ybT_s[Gn], [R_ybT], [ybg_r[Gn % 2]])
            for dc in range(8):
                e = dc % 2
                pb4 = (0, 1, 2, 3) if e == 0 else (4, 5, 6, 7)
                for i in range(2):
                    for k in range(8):
                        K.mm(bk(pb4[i]), Wmg[:, k, i * 1024 + dc * 128:i * 1024 + (dc + 1) * 128], xg[gb][:, k, :],
                             k == 0, k == 7, [R_W4, xg_r[gb]], [bres[pb4[i]]])
                for i, (Wx, yg, yr) in enumerate(((Wa, yag, yag_r), (Wb, ybg, ybg_r))):
                    for k in range(4):
                        K.mm(bk(pb4[2 + i]), Wx[:, k, dc * 128:(dc + 1) * 128], yg[gb][:, k, :], k == 0, k == 3,
                             [R_W4, yr[gb]], [bres[pb4[2 + i]]])
                for i in range(2):
                    K.act(sg[e][:, i, :], bk(pb4[i]), AF.Sigmoid, [bres[pb4[i]]], [sg_r[e]])
                for i in range(2):
                    K.tt("dve", mm_[e][:, i, :], sg[e][:, i, :], bk(pb4[2 + i]), ALU.mult, [sg_r[e], bres[pb4[2 + i]]], [mm_r[e]])
                K.tt("dve", mT[:, dc, :], mm_[e][:, 0, :], mm_[e][:, 1, :], ALU.add, [mm_r[e]], [R_mT])
            for qs in range(4):
                t = G * 4 + qs
                b = qs % 2
                K.dma("sp", xt[b], x_d[t * 128:(t + 1) * 128, :], [], [xt_r[b]])
                for n2 in range(2):
                    bnk = 2 * b + n2
                    for dc in range(8):
                        K.mm(bk(bnk), mT[:, dc, qs * 128:(qs + 1) * 128], Wo[:, dc, n2 * 512:(n2 + 1) * 512], dc == 0, dc == 7,
                             [R_mT, R_W4], [bres[bnk]])
                    K.tt("dve", ho[b][:, n2 * 512:(n2 + 1) * 512], bk(bnk), xt[b][:, n2 * 512:(n2 + 1) * 512], ALU.add,
                         [bres[bnk], xt_r[b]], [ho_r[b]])
                K.dma("pool", h_s[t * 128:(t + 1) * 128, :], ho[b], [ho_r[b]], [R_h])
        S.barrier()
        if stop_after == "P4":
            return _finish()

        cv = Carver()
        Wd = cv.t(BF16, 22, D)
        hgs = [cv.t(F32, 4, D) for _ in range(2)]
        xs = [cv.t(F32, D) for _ in range(2)]
        junk = cv.t(F32, D)
        hT = cv.t(BF16, 8, 512)
        actT = cv.t(BF16, 22, 512)
        wr = [cv.t(BF16, 2, 8, 128) for _ in range(4)]
        sgl = [cv.t(F32, 512) for _ in range(2)]
        st1 = [cv.t(F32, 4) for _ in range(2)]
        yo = [cv.t(F32, D) for _ in range(2)]
        R_Wd, R_hT, R_act, R_y = Res(), Res(), Res(), Res("y")
        R_hgs = [Res(), Res()]
        xs_r, wr_r, sgl_r, st1_r, yo_r = [Res(), Res()], [Res() for _ in range(4)], [Res(), Res()], [Res(), Res()], [Res(), Res()]
        junk_r = Res()
        for kq in range(2):
            K.dma("pool", Wd[:, 11 * kq:11 * kq + 11, :], wd_d[kq * 1408:(kq + 1) * 1408, :].rearrange("(k p) c -> p k c", p=128), [], [R_Wd])
        wi = 0
        for G in range(NG):
            for Gn in ([0, 1] if G == 0 else [G + 1]):
                if Gn < NG:
                    K.dma("sp", hgs[Gn % 2], h_s[Gn * 512:(Gn + 1) * 512, :].rearrange("(a p) c -> p a c", p=128), [R_h], [R_hgs[Gn % 2]])
            hg = hgs[G % 2]
            R_hg = R_hgs[G % 2]
            for qs in range(4):
                b = qs % 2
                nt_A(None, qs, b, hg[:, qs, :], R_hg, load=False)
                nt_B(qs, b, g_ffn, hT, R_hT, (2 * b, 2 * b + 1))
            for j in range(22):
                w = wi % 4
                wi += 1
                K.dma("sp", wr[w], wgu_s[j], [R_wgu], [wr_r[w]])
                e = j % 2
                bg, bu = (4, 5) if e == 0 else (6, 7)
                for i, bnk in enumerate((bg, bu)):
                    for k in range(8):
                        K.mm(bk(bnk), wr[w][:, i, k, :], hT[:, k, :], k == 0, k == 7, [wr_r[w], R_hT], [bres[bnk]])
                K.act(sgl[e], bk(bg), AF.Silu, [bres[bg]], [sgl_r[e]])
                K.tt("dve", actT[:, j, :], sgl[e], bk(bu), ALU.mult, [sgl_r[e], bres[bu]], [R_act])
            for qs in range(4):
                t = G * 4 + qs
                b = qs % 2
                for n2 in range(2):
                    bnk = 2 * b + n2
                    for j in range(22):
                        K.mm(bk(bnk), actT[:, j, qs * 128:(qs + 1) * 128], Wd[:, j, n2 * 512:(n2 + 1) * 512], j == 0, j == 21,
                             [R_act, R_Wd], [bres[bnk]])
                    K.tt("dve", yo[b][:, n2 * 512:(n2 + 1) * 512], bk(bnk), hg[:, qs, n2 * 512:(n2 + 1) * 512], ALU.add,
                         [bres[bnk], R_hg], [yo_r[b]])
                K.act(junk, yo[b], AF.Square, [yo_r[b]], [junk_r], accum=st1[b][:, 0:1])
                K.act(st1[b][:, 1:2], st1[b][:, 0:1], AF.Ln, [junk_r, R_const], [st1_r[b]], scale=1.0 / D, bias=eps_ap)
                K.act(st1[b][:, 2:3], st1[b][:, 1:2], AF.Exp, [st1_r[b]], [st1_r[b]], scale=-0.5)
                K.stt(yo[b], yo[b], st1[b][:, 2:3], gfin_b[:], ALU.mult, ALU.mult, [yo_r[b], st1_r[b], R_const], [yo_r[b]])
                K.dma("pool", y_d[t * 128:(t + 1) * 128, :], yo[b], [yo_r[b]], [R_y])
        return _finish()


_PROG = {}


def kernel(**inputs):
    x = np.ascontiguousarray(np.asarray(inputs["x"], dtype=np.float32))
    B, SEQ, _ = x.shape
    pos = np.asarray(inputs["positions"]).astype(np.int32)
    f = lambda k, shp: np.ascontiguousarray(np.asarray(inputs[k], dtype=np.float32).reshape(shp))
    common = {
        "attn_norm_g": f("attn_norm_g", (1, D)), "w_in": f("w_in", (D, INDIM)), "diff_lambda": f("diff_lambda", (1, 256)),
        "diff_subln_g": f("diff_subln_g", (1, 128)), "cmp_pe_k": f("cmp_pe_k", (32, 64)), "cmp_pe_v": f("cmp_pe_v", (32, 64)),
        "cmp_k_w1": f("cmp_k_w1", (2048, 256)), "cmp_k_w2": f("cmp_k_w2", (256, 64)),
        "cmp_v_w1": f("cmp_v_w1", (2048, 256)), "cmp_v_w2": f("cmp_v_w2", (256, 64)),
        "w_branch_a": f("w_branch_a", (512, D)), "w_branch_b": f("w_branch_b", (512, D)), "w_out": f("w_out", (D, D)),
        "ffn_norm_g": f("ffn_norm_g", (1, D)), "w_gate": f("w_gate", (D, DFF)), "w_up": f("w_up", (D, DFF)),
        "w_down": f("w_down", (DFF, D)), "final_norm_g": f("final_norm_g", (1, D)),
    }
    common.update(host_consts(SEQ))
    if SEQ not in _PROG:
        _PROG[SEQ] = build_program(SEQ)
    nc = _PROG[SEQ]
    in_maps = [dict(common, x=x[b], pos=np.ascontiguousarray(pos[b].reshape(SEQ, 1))) for b in range(B)]
    res = run_bass_kernel_spmd(nc, in_maps, core_ids=list(range(B)))
    return np.stack([np.asarray(r["y"], dtype=np.float32) for r in res.results], axis=0)
```

```python
import math
import numpy as np
import concourse.bass as bass
import concourse.mybir as mybir
from concourse.bass_utils import run_bass_kernel_spmd

F32 = mybir.dt.float32
BF16 = mybir.dt.bfloat16
I32 = mybir.dt.int32
AF = mybir.ActivationFunctionType
ALU = mybir.AluOpType
AX = mybir.AxisListType


class Res:
    __slots__ = ("w", "r", "name", "excl")

    def __init__(self, name="", excl=False):
        self.w = None
        self.r = {}
        self.name = name
        self.excl = excl


class Sched:
    ENG = ("pe", "dve", "act", "pool", "sp")
    NDSEM = 12

    def __init__(self, nc, stack):
        self.nc = nc
        self.h = {"pe": nc.tensor, "dve": nc.vector, "act": nc.scalar, "pool": nc.gpsimd, "sp": nc.sync}
        self.sems = {}
        self.cnt = {}
        self.ops = {e: [] for e in self.ENG}
        self.waited = {e: {} for e in self.ENG}
        for e in self.ENG:
            self.sems[e] = stack.enter_context(nc.semaphore("s_" + e))
            self.cnt[e] = 0
        self.dsem = {}
        self.dsem_cnt = {}
        self.dsem_rr = {}
        for q in ("sp", "pool", "act"):
            self.dsem[q] = [stack.enter_context(nc.semaphore(f"d_{q}{i}")) for i in range(self.NDSEM)]
            self.dsem_cnt[q] = [0] * self.NDSEM
            self.dsem_rr[q] = 0
        self.nops = 0
        self.bar = {e: [] for e in self.ENG}

    def _collect(self, eng, reads, writes, extra=()):
        own = ("e", eng)
        waits = {}

        def need(tok, allow_own):
            if tok is None:
                return
            k, v = tok
            if k == own and not allow_own:
                return
            if waits.get(k, 0) < v:
                waits[k] = v

        for r in reads:
            need(r.w, True)
        own_ok = (eng != "pe")
        for r in writes:
            need(r.w, own_ok)
            for k, v in r.r.items():
                need((k, v), own_ok)
        for t in extra:
            need(t, True)
        for t in self.bar[eng]:
            need(t, True)
        self.bar[eng] = []
        wd = self.waited[eng]
        out = []
        for k, v in waits.items():
            if wd.get(k, 0) >= v:
                continue
            wd[k] = v
            out.append((k, v))
        return out

    def _mark(self, tok, reads, writes):
        k, v = tok
        for r in reads:
            if r.r.get(k, 0) < v:
                r.r[k] = v
        for r in writes:
            r.w = tok
            r.r = {}

    def op(self, eng, fn, reads=(), writes=()):
        if any(r.excl for r in reads):
            writes = list(writes) + [r for r in reads if r.excl]
            reads = [r for r in reads if not r.excl]
        waits = self._collect(eng, reads, writes)
        self.cnt[eng] += 1
        tok = (("e", eng), self.cnt[eng])
        self.ops[eng].append((waits, fn, ("e", eng), 1))
        self._mark(tok, reads, writes)
        self.nops += 1
        return tok

    def dma(self, q, fn, reads=(), writes=()):
        i = self.dsem_rr[q]
        self.dsem_rr[q] = (i + 1) % self.NDSEM
        key = ("d", q, i)
        prev = self.dsem_cnt[q][i]
        extra = [(key, prev)] if prev > 0 else []
        waits = self._collect(q, reads, writes, extra)
        self.dsem_cnt[q][i] = prev + 16
        tok = (key, prev + 16)
        self.ops[q].append((waits, fn, key, 16))
        self._mark(tok, reads, writes)
        self.nops += 1
        return tok

    def all_tokens(self):
        toks = [(("e", e), self.cnt[e]) for e in self.ENG if self.cnt[e] > 0]
        for q in self.dsem_cnt:
            for i, v in enumerate(self.dsem_cnt[q]):
                if v > 0:
                    toks.append((("d", q, i), v))
        return toks

    def barrier(self):
        toks = self.all_tokens()
        for e in self.ENG:
            self.bar[e] = list(toks)

    def final_all(self):
        self.ops["sp"].append((self.all_tokens(), None, None, 0))

    def _sem(self, key):
        if key[0] == "e":
            return self.sems[key[1]]
        return self.dsem[key[1]][key[2]]

    def final_wait(self, eng, toks):
        waits = []
        for k, v in toks:
            waits.append((k, v))
        self.ops[eng].append((waits, None, None, 0))

    def emit(self):
        nc = self.nc
        with nc.Block() as block:
            def mk(e):
                def body(h):
                    for waits, fn, key, inc in self.ops[e]:
                        for k, v in waits:
                            h.wait_ge(self._sem(k), v)
                        if fn is not None:
                            ins = fn(h)
                            ins.then_inc(self._sem(key), inc)
                return body
            block.tensor(mk("pe"))
            block.vector(mk("dve"))
            block.scalar(mk("act"))
            block.gpsimd(mk("pool"))
            block.sync(mk("sp"))


D = 1024
HD = 64
DFF = 2816
NEG = -30000.0
EPS = 1e-6
INDIM = 4888
C_DQ, C_DK, C_DV, C_NQ = 0, 512, 1024, 1536
C_NSA0 = 1536
C_MG = 2840


def host_consts(S):
    c = {}
    c["ident"] = np.eye(128, dtype=np.float32)
    kk = np.arange(128)[:, None]
    q = np.arange(512)[None, :]
    m = np.zeros((128, 8, 512), np.float32)
    for j in range(4):
        m[:, j, :] = np.where(128 * j + kk <= q, 1.0, 0.0)
        m[:, 4 + j, :] = np.where(128 * j + kk > q, 1.0, 0.0)
    c["maskcw"] = m
    cc = np.arange(256)[:, None]
    qq = np.arange(S)[None, :]
    cm = np.where((16 * cc + 31 <= qq) & (cc <= 254), 0.0, NEG).astype(np.float32)
    c["cmask"] = np.ascontiguousarray(cm.reshape(2, 128, S).transpose(1, 0, 2))
    nkt = S // 128
    E = np.zeros((64, nkt, 128), np.float32)
    for kt in range(nkt):
        for k2 in range(128):
            n = 2 * kt + k2 // 64
            if n < 64:
                E[n, kt, k2] = 1.0
    c["E"] = E
    ci = np.arange(256)[:, None] * 16
    sj = np.arange(64)[None, :] * 64
    ov = ((ci < sj + 64) & (ci + 32 > sj)).astype(np.float32)
    ov[255] = 0.0
    c["ov"] = np.ascontiguousarray(ov.reshape(2, 128, 64).transpose(1, 0, 2))
    qb = (np.arange(S) // 64)[:, None]
    n = np.arange(64)[None, :]
    forced = (n == 0) | (n == qb) | (n == qb - 1)
    bt = np.where(forced, 100.0 + n, 0.0)
    bt = np.where(n > qb, -1000.0, bt).astype(np.float32)
    c["btab"] = bt
    c["invf"] = (1.0 / (10000.0 ** (np.arange(0, 64, 2, dtype=np.float32) / 64))).astype(np.float32).reshape(1, 32)
    return c


class KB:
    def __init__(self, nc, S_, st):
        self.nc = nc
        self.S = S_
        self.st = st

    def mm(self, out, lhsT, rhs, start, stop, reads, writes):
        return self.S.op("pe", lambda h: h.matmul(out, lhsT, rhs, start=start, stop=stop, skip_group_check=True),
                         reads=reads, writes=writes)

    def tr(self, out, in_, ident, reads, writes):
        return self.S.op("pe", lambda h: h.transpose(out, in_, ident), reads=reads, writes=writes)

    def act(self, out, in_, func, reads, writes, scale=None, bias=None, accum=None):
        kw = {}
        if scale is not None:
            kw["scale"] = scale
        if bias is not None:
            kw["bias"] = bias
        if accum is not None:
            kw["accum_out"] = accum
        return self.S.op("act", lambda h: h.activation(out=out, in_=in_, func=func, **kw), reads=reads, writes=writes)

    def ts(self, eng, out, in0, s1, s2, op0, op1, reads, writes):
        if op1 is None:
            return self.S.op(eng, lambda h: h.tensor_scalar(out=out, in0=in0, scalar1=s1, scalar2=None, op0=op0),
                             reads=reads, writes=writes)
        return self.S.op(eng, lambda h: h.tensor_scalar(out=out, in0=in0, scalar1=s1, scalar2=s2, op0=op0, op1=op1),
                         reads=reads, writes=writes)

    def tt(self, eng, out, in0, in1, op, reads, writes):
        return self.S.op(eng, lambda h: h.tensor_tensor(out=out, in0=in0, in1=in1, op=op), reads=reads, writes=writes)

    def stt(self, out, in0, scalar, in1, op0, op1, reads, writes, accum=None):
        if accum is None:
            return self.S.op("dve", lambda h: h.scalar_tensor_tensor(out=out, in0=in0, scalar=scalar, in1=in1, op0=op0, op1=op1),
                             reads=reads, writes=writes)
        return self.S.op("dve", lambda h: h.scalar_tensor_tensor(out=out, in0=in0, scalar=scalar, in1=in1, op0=op0, op1=op1,
                                                                  accum_out=accum), reads=reads, writes=writes)

    def cp(self, eng, out, in_, reads, writes):
        if eng == "act":
            return self.S.op("act", lambda h: h.copy(out=out, in_=in_), reads=reads, writes=writes)
        return self.S.op(eng, lambda h: h.tensor_copy(out=out, in_=in_), reads=reads, writes=writes)

    def recip(self, out, in_, reads, writes):
        return self.S.op("dve", lambda h: h.reciprocal(out=out, in_=in_), reads=reads, writes=writes)

    def memset(self, eng, ap, val, writes):
        return self.S.op(eng, lambda h: h.memset(ap, val), writes=writes)

    def dma(self, q, out, in_, reads, writes):
        return self.S.dma(q, lambda h: h.dma_start(out=out, in_=in_), reads=reads, writes=writes)


def build_program(SEQ, debug=False, stop_after=None):
    from contextlib import ExitStack
    nc = bass.Bass("TRN2", target_bir_lowering=False)
    NT = SEQ // 128
    NG = SEQ // 512
    ext_in = lambda n, shp, dt=F32: nc.dram_tensor(n, shp, dt, kind="ExternalInput").ap()
    x_d = ext_in("x", [SEQ, D])
    pos_d = ext_in("pos", [SEQ, 1], I32)
    attn_g_d = ext_in("attn_norm_g", [1, D])
    w_in_d = ext_in("w_in", [D, INDIM])
    lam_d = ext_in("diff_lambda", [1, 256])
    subg_d = ext_in("diff_subln_g", [1, 128])
    pek_d = ext_in("cmp_pe_k", [32, 64])
    pev_d = ext_in("cmp_pe_v", [32, 64])
    w1k_d = ext_in("cmp_k_w1", [2048, 256])
    w2k_d = ext_in("cmp_k_w2", [256, 64])
    w1v_d = ext_in("cmp_v_w1", [2048, 256])
    w2v_d = ext_in("cmp_v_w2", [256, 64])
    wa_d = ext_in("w_branch_a", [512, D])
    wb_d = ext_in("w_branch_b", [512, D])
    wout_d = ext_in("w_out", [D, D])
    ffn_g_d = ext_in("ffn_norm_g", [1, D])
    wg_d = ext_in("w_gate", [D, DFF])
    wu_d = ext_in("w_up", [D, DFF])
    wd_d = ext_in("w_down", [DFF, D])
    fin_g_d = ext_in("final_norm_g", [1, D])
    ident_d = ext_in("ident", [128, 128])
    maskcw_d = ext_in("maskcw", [128, 8, 512])
    cmask_d = ext_in("cmask", [128, 2, SEQ])
    E_d = ext_in("E", [64, NT, 128])
    ov_d = ext_in("ov", [128, 2, 64])
    btab_d = ext_in("btab", [SEQ, 64])
    invf_d = ext_in("invf", [1, 32])
    y_d = nc.dram_tensor("y", [SEQ, D], F32, kind="ExternalOutput").ap()
    skind = "ExternalOutput" if debug else "Internal"
    scr = lambda n, shp, dt: nc.dram_tensor(n, shp, dt, kind=skind).ap()
    xTn_s = scr("xTn_s", [NG, 128, 8, 512], BF16)
    yaT_s = scr("yaT_s", [NG, 128, 4, 512], BF16)
    ybT_s = scr("ybT_s", [NG, 128, 4, 512], BF16)
    h_s = scr("h_s", [SEQ, D], F32)
    wgu_s = scr("wgu_s", [22, 128, 2, 8, 128], BF16)

    with ExitStack() as st:
        S = Sched(nc, st)
        K = KB(nc, S, st)

        def _finish():
            S.final_all()
            with nc.allow_non_contiguous_dma(reason="small strided constant loads"):
                S.emit()
            return nc
        sb = lambda n, shp, dt: st.enter_context(nc.sbuf_tensor(n, shp, dt))
        banks = [st.enter_context(nc.psum_tensor(f"bank{i}", [128, 512], F32)) for i in range(8)]
        bres = [Res(f"bank{i}", excl=True) for i in range(8)]
        bk = lambda i: banks[i][:]
        bkb = lambda i: banks[i][:].bitcast(BF16)
        ident_bf = sb("ident_bf", [128, 128], BF16)
        ident_f = sb("ident_f", [128, 128], F32)
        cos_t = sb("cos_t", [128, NT, 32], F32)
        sin_t = sb("sin_t", [128, NT, 32], F32)
        g_attn = sb("g_attn", [128, 8], F32)
        g_ffn = sb("g_ffn", [128, 8], F32)
        gfin_b = sb("gfin_b", [128, D], F32)
        g08 = sb("g08", [128, 128], F32)
        neglam = sb("neglam", [128, 1], F32)
        maskcw = sb("maskcw_sb", [128, 8, 512], BF16)
        gate = sb("gate", [128, NT, 24], F32)
        R_const = Res("const")
        R_gate = Res("gate")
        ARENA_B = 148 * 1024
        arena = sb("arena", [128, ARENA_B // 2], BF16)

        class Carver:
            def __init__(self):
                self.off = 0

            def take(self, nbytes_pp, dt, shape_free, parts=128):
                assert self.off % 4 == 0
                n2 = (nbytes_pp + 3) // 4 * 4
                assert self.off + n2 <= ARENA_B, ("arena overflow", self.off + n2)
                v = arena[0:parts, self.off // 2:(self.off + nbytes_pp) // 2]
                self.off += n2
                if dt == F32:
                    v = v.bitcast(F32)
                return v

            def t(self, dt, *free, parts=128):
                n = 1
                for f in free:
                    n *= f
                esz = 4 if dt == F32 else 2
                v = self.take(n * esz, dt, free, parts)
                if len(free) == 2:
                    v = v.rearrange("p (a b) -> p a b", a=free[0])
                elif len(free) == 3:
                    v = v.rearrange("p (a b c) -> p a b c", a=free[0], b=free[1])
                elif len(free) == 4:
                    v = v.rearrange("p (a b c d) -> p a b c d", a=free[0], b=free[1], c=free[2])
                return v

        cv = Carver()
        K.dma("pool", ident_bf[:], ident_d, [], [R_const])
        K.dma("sp", ident_f[:], ident_d, [], [R_const])
        K.dma("pool", maskcw[:], maskcw_d, [], [R_const])
        K.dma("sp", g_attn[:], attn_g_d.rearrange("o (k p) -> p (o k)", p=128), [], [R_const])
        K.dma("sp", g_ffn[:], ffn_g_d.rearrange("o (k p) -> p (o k)", p=128), [], [R_const])
        K.dma("sp", gfin_b[:], fin_g_d.partition_broadcast(128), [], [R_const])
        K.dma("sp", g08[:], subg_d.partition_broadcast(128), [], [R_const])
        K.ts("dve", g08[:], g08[:], 0.8, None, ALU.mult, None, [R_const], [R_const])
        pos_i = cv.t(I32 if False else F32, NT)
        pos_i32 = pos_i.bitcast(I32)
        pos_f = cv.t(F32, NT)
        invf_b = cv.t(F32, 32)
        ang = cv.t(F32, NT, 32)
        tmpa = cv.t(F32, NT, 32)
        tmpi = cv.t(F32, NT, 32)
        tmpi_i = tmpi.bitcast(I32)
        R_rope = Res("rope")
        K.dma("sp", pos_i32, pos_d.rearrange("(t p) o -> p (t o)", p=128), [], [R_rope])
        K.dma("sp", invf_b, invf_d.partition_broadcast(128), [], [R_rope])
        K.cp("dve", pos_f, pos_i32, [R_rope], [R_rope])
        K.tt("dve", ang, pos_f.unsqueeze(2).broadcast_to([128, NT, 32]), invf_b.unsqueeze(1).broadcast_to([128, NT, 32]),
             ALU.mult, [R_rope], [R_rope])
        TWO_PI = 2.0 * math.pi
        for (dst, shift) in ((sin_t, 0.0), (cos_t, math.pi / 2)):
            K.ts("dve", tmpa, ang, shift, 1.0 / TWO_PI, ALU.add, ALU.mult, [R_rope], [R_rope])
            K.cp("dve", tmpi_i, tmpa, [R_rope], [R_rope])
            K.cp("dve", tmpa, tmpi_i, [R_rope], [R_rope])
            K.ts("dve", tmpa, tmpa, -TWO_PI, shift, ALU.mult, ALU.add, [R_rope], [R_rope])
            K.tt("dve", tmpa, tmpa, ang, ALU.add, [R_rope], [R_rope])
            K.ts("dve", tmpi, tmpa, math.pi, -TWO_PI, ALU.is_gt, ALU.mult, [R_rope], [R_rope])
            K.tt("dve", tmpa, tmpa, tmpi, ALU.add, [R_rope], [R_rope])
            K.ts("dve", tmpi, tmpa, -math.pi, TWO_PI, ALU.is_lt, ALU.mult, [R_rope], [R_rope])
            K.tt("dve", tmpa, tmpa, tmpi, ALU.add, [R_rope], [R_rope])
            K.ts("dve", tmpa, tmpa, math.pi, -math.pi, ALU.min, ALU.max, [R_rope], [R_rope])
            K.act(dst[:], tmpa, AF.Sin, [R_rope], [R_const])
        lam_sb = cv.t(F32, 256, parts=1)
        lam_j = cv.t(F32, 64, parts=1)
        lam_s = cv.t(F32, 4, parts=1)
        ones_row = cv.t(F32, 128, parts=1)
        R_lam = Res("lam")
        K.dma("sp", lam_sb, lam_d, [], [R_lam])
        K.memset("dve", ones_row, 1.0, [R_lam])
        K.stt(lam_j, lam_sb[:, 0:64], 1.0, lam_sb[:, 64:128], ALU.mult, ALU.mult, [R_lam], [R_lam], accum=lam_s[:, 0:1])
        K.stt(lam_j, lam_sb[:, 128:192], 1.0, lam_sb[:, 192:256], ALU.mult, ALU.mult, [R_lam], [R_lam], accum=lam_s[:, 1:2])
        K.act(lam_s[:, 2:4], lam_s[:, 0:2], AF.Exp, [R_lam], [R_lam])
        K.tt("dve", lam_s[:, 0:1], lam_s[:, 3:4], lam_s[:, 2:3], ALU.subtract, [R_lam], [R_lam])
        K.ts("dve", lam_s[:, 0:1], lam_s[:, 0:1], -0.2, None, ALU.add, None, [R_lam], [R_lam])
        K.mm(bk(0)[:, 0:1], ones_row, lam_s[:, 0:1], True, True, [R_lam], [bres[0]])
        K.cp("dve", neglam[:], bk(0)[:, 0:1], [bres[0]], [R_const])
        wst = [cv.t(BF16, 2, 8, 128) for _ in range(2)]
        wst_r = [Res("wst0"), Res("wst1")]
        R_wgu = Res("wgu_s")
        for j in range(22):
            b = j % 2
            K.dma("pool", wst[b][:, 0], wg_d[:, j * 128:(j + 1) * 128].rearrange("(k p) c -> p k c", p=128), [], [wst_r[b]])
            K.dma("pool", wst[b][:, 1], wu_d[:, j * 128:(j + 1) * 128].rearrange("(k p) c -> p k c", p=128), [], [wst_r[b]])
            K.dma("sp", wgu_s[j], wst[b], [wst_r[b]], [R_wgu])
        S.barrier()
        if stop_after == "P0":
            return _finish()

        cv = Carver()
        xt = [cv.t(F32, D) for _ in range(2)]
        xs = [cv.t(F32, D) for _ in range(2)]
        junk = cv.t(F32, D)
        stg = [cv.t(BF16, 8, 512) for _ in range(2)]
        st1 = [cv.t(F32, 4) for _ in range(2)]
        xt_r = [Res(), Res()]
        xs_r = [Res(), Res()]
        stg_r = [Res(), Res()]
        st1_r = [Res(), Res()]
        junk_r = Res()
        R_xTn = Res("xTn_s")

        def nt_A(src_tile_ap, t, b, xt_ap, xt_res, load=True):
            if load:
                K.dma("sp", xt_ap, src_tile_ap, [], [xt_res])
            K.act(junk, xt_ap, AF.Square, [xt_res], [junk_r], accum=st1[b][:, 0:1])
            K.act(st1[b][:, 1:2], st1[b][:, 0:1], AF.Ln, [junk_r, R_const], [st1_r[b]], scale=1.0 / D, bias=eps_ap)
            K.act(st1[b][:, 2:3], st1[b][:, 1:2], AF.Exp, [st1_r[b]], [st1_r[b]], scale=-0.5)
            K.ts("dve", xs[b], xt_ap, st1[b][:, 2:3], None, ALU.mult, None, [xt_res, st1_r[b]], [xs_r[b]])

        def nt_B(t, b, gcol, stage, stage_r, pb):
            for half in range(2):
                pbank = pb[half]
                for kq in range(4):
                    k = half * 4 + kq
                    K.tr(bk(pbank)[:, kq * 128:(kq + 1) * 128], xs[b][:, k * 128:(k + 1) * 128], ident_f[:],
                         [xs_r[b], R_const], [bres[pbank]])
                K.tt("dve", stage[:, half * 4:half * 4 + 4, (t % 4) * 128:(t % 4 + 1) * 128],
                     bk(pbank).rearrange("p (a b) -> p a b", a=4),
                     gcol[:, half * 4:half * 4 + 4].unsqueeze(2).broadcast_to([128, 4, 128]), ALU.mult,
                     [bres[pbank], R_const], [stage_r])

        eps_t = sb("eps_t", [128, 1], F32)
        K.memset("dve", eps_t[:], EPS, [R_const])
        eps_ap = eps_t[:]
        for tt_ in range(NT + 1):
            if tt_ < NT:
                nt_A(x_d[tt_ * 128:(tt_ + 1) * 128, :], tt_, tt_ % 2, xt[tt_ % 2], xt_r[tt_ % 2])
            if tt_ >= 1:
                t = tt_ - 1
                b = t % 2
                g = t // 4
                sgb = g % 2
                nt_B(t, b, g_attn, stg[sgb], stg_r[sgb], (2 * b, 2 * b + 1))
                if t % 4 == 3:
                    K.dma("sp", xTn_s[g], stg[sgb], [stg_r[sgb]], [R_xTn])
        S.barrier()
        if stop_after == "P1":
            return _finish()

        class AttnPipe:
            def __init__(self, st_banks, pbufs, pres, look=1):
                self.stb = st_banks
                self.pb = pbufs
                self.pr = pres
                self.look = look
                self.i = 0
                self.pend = []
                self.deferred = []

            def defer(self, fn, nsteps):
                self.deferred.append([nsteps, fn])

            def _tick(self):
                ready = [d for d in self.deferred if d[0] <= 1]
                self.deferred = [[d[0] - 1, d[1]] for d in self.deferred if d[0] > 1]
                for d in ready:
                    d[1]()

            def step(self, units):
                ent = []
                for (qk, nrows, pv, post) in units:
                    i = self.i
                    self.i += 1
                    bnk = self.stb[i % len(self.stb)]
                    pi = i % len(self.pb)
                    qk(bnk)
                    ent.append((bnk, pi, nrows, pv, post))
                for ui, (bnk, pi, nrows, pv, post) in enumerate(ent):
                    elo, ehi = getattr(units[ui][0], "exp_cols", (0, 512))
                    K.act(self.pb[pi][0:nrows, elo:ehi], bk(bnk)[0:nrows, elo:ehi], AF.Exp, [bres[bnk]], [self.pr[pi]], scale=0.125)
                    mk = getattr(units[ui][0], "mask01", None)
                    if mk is not None:
                        lo, hi = units[ui][0].mask_cols
                        K.tt("pool" if ui == 0 else "dve", self.pb[pi][0:nrows, lo:hi], self.pb[pi][0:nrows, lo:hi], mk[:, lo:hi],
                             ALU.mult, [self.pr[pi], R_const], [self.pr[pi]])
                self.pend.append(ent)
                if len(self.pend) > self.look:
                    self._flush1()
                self._tick()

            def unit(self, qk, nrows, pv, post=None):
                self.step([(qk, nrows, pv, post)])

            def _flush1(self):
                ent = self.pend.pop(0)
                for (bnk, pi, nrows, pv, post) in ent:
                    pv(self.pb[pi], self.pr[pi])
                for (bnk, pi, nrows, pv, post) in ent:
                    if post is not None:
                        post()

            def flush(self):
                while self.pend:
                    self._flush1()
                while self.deferred:
                    self._tick()

        cv = Carver()
        xg = [cv.t(BF16, 8, 512) for _ in range(2)]
        xg_r = [Res(), Res()]
        qkT = cv.t(BF16, 2, SEQ)
        dv_aug = cv.t(BF16, NT, 130)
        W_hs = [cv.t(BF16, 8, 384) for _ in range(2)]
        qk_tok = [cv.t(BF16, 256) for _ in range(2)]
        rtmp = [cv.t(F32, 4, 128) for _ in range(2)]
        pbufs = [cv.t(BF16, 512) for _ in range(6)]
        tb_ = [cv.t(F32, 2, 4, 128) for _ in range(2)]
        ya4 = [cv.t(F32, 4, 128) for _ in range(2)]
        yj = cv.t(F32, 128)
        yan4 = [cv.t(BF16, 4, 128) for _ in range(2)]
        sst = [cv.t(F32, 12) for _ in range(2)]
        rz = [cv.t(F32, 2, 4) for _ in range(2)]
        mhalf = cv.t(F32, 4)
        tb_r = [Res(), Res()]
        gcount = [0]
        yaT_st = [cv.t(BF16, 512) for _ in range(2)]
        R_qkT, R_dv = Res(), Res()
        R_Whs = [Res(), Res()]
        qk_tok_r = [Res(), Res()]
        rtmp_r = [Res(), Res()]
        pres = [Res() for _ in range(6)]
        t0_r, yj_r = Res(), Res()
        ya_r = [Res(), Res()]
        yan_r = [Res(), Res()]
        sst_r = [Res(), Res()]
        rz_r = [Res(), Res()]
        yaT_st_r = [Res(), Res()]
        R_yaT = Res("yaT_s")
        K.memset("dve", dv_aug[:, :, 128:129], 1.0, [R_dv])
        K.memset("dve", mhalf, -0.5, [R_const])

        def rope(eng, out_v, in_v, t, nh, tmp, tmp_r, in_res, out_res):
            shp = [128, nh, 32]
            cb = cos_t[:, t, :].unsqueeze(1).broadcast_to(shp)
            sbb = sin_t[:, t, :].unsqueeze(1).broadcast_to(shp)
            x1 = in_v[:, :, 0, :]
            x2 = in_v[:, :, 1, :]
            tv = [tmp[:, i, 0:nh * 32].rearrange("p (h d) -> p h d", h=nh) for i in range(4)]
            K.tt(eng, tv[0], x1, cb, ALU.mult, in_res + [R_const], [tmp_r])
            K.tt(eng, tv[1], x2, sbb, ALU.mult, in_res + [R_const], [tmp_r])
            K.tt(eng, tv[2], x2, cb, ALU.mult, in_res + [R_const], [tmp_r])
            K.tt(eng, tv[3], x1, sbb, ALU.mult, in_res + [R_const], [tmp_r])
            K.tt(eng, out_v[:, :, 0, :], tv[0], tv[1], ALU.subtract, [tmp_r], out_res)
            K.tt(eng, out_v[:, :, 1, :], tv[2], tv[3], ALU.add, [tmp_r], out_res)

        for hd in range(4):
            for hn in ([0, 1] if hd == 0 else [hd + 1]):
                if hn < 4:
                    for i, c0 in enumerate((C_DQ + hn * 128, C_DK + hn * 128, C_DV + hn * 128)):
                        K.dma("pool", W_hs[hn % 2][:, :, i * 128:(i + 1) * 128],
                              w_in_d[:, c0:c0 + 128].rearrange("(k p) c -> p k c", p=128), [], [R_Whs[hn % 2]])
            W_h = W_hs[hd % 2]
            R_Wh = R_Whs[hd % 2]
            def projA(t):
                g = t // 4
                gb = g % 2
                b = t % 2
                if t % 4 == 0:
                    K.dma("sp", xg[gb], xTn_s[g], [R_xTn], [xg_r[gb]])
                pbank = b
                for k in range(8):
                    K.mm(bk(pbank)[:, 0:384], xg[gb][:, k, (t % 4) * 128:(t % 4 + 1) * 128], W_h[:, k, :],
                         k == 0, k == 7, [xg_r[gb], R_Wh], [bres[pbank]])
                pv = bk(pbank)[:, 0:256].rearrange("p (h c d) -> p h c d", h=4, c=2)
                ov_ = qk_tok[b].rearrange("p (h c d) -> p h c d", h=4, c=2)
                rope("dve", ov_, pv, t, 4, rtmp[b], rtmp_r[b], [bres[pbank]], [qk_tok_r[b]])
                K.cp("act", dv_aug[:, t, 0:128], bk(pbank)[:, 256:384], [bres[pbank]], [R_dv])

            def projB(t):
                b = t % 2
                tb = 2 + b
                for i in range(2):
                    K.tr(bkb(tb)[:, i * 128:(i + 1) * 128], qk_tok[b][:, i * 128:(i + 1) * 128], ident_bf[:],
                         [qk_tok_r[b], R_const], [bres[tb]])
                K.cp("act", qkT[:, :, t * 128:(t + 1) * 128], bkb(tb)[:, 0:256].rearrange("p (a b) -> p a b", a=2),
                     [bres[tb]], [R_qkT])

            for tt_ in range(NT + 1):
                if tt_ < NT:
                    projA(tt_)
                if tt_ >= 1:
                    projB(tt_ - 1)
            if stop_after == "P2a":
                return _finish()
            pipe = AttnPipe([4, 5, 6, 7], pbufs, pres, look=2)
            for C in range(NG):
                nk = 4 * C + 4
                for kt in range(nk):
                    diag = kt - 4 * C
                    units = []
                    for s in range(2):
                        ob = (0, 1) if s == 0 else (2, 3)

                        def qk(bnk, kt=kt, s=s, C=C, diag=diag):
                            K.mm(bk(bnk), qkT[64 * s:64 * s + 64, 1, kt * 128:(kt + 1) * 128],
                                 qkT[64 * s:64 * s + 64, 0, C * 512:(C + 1) * 512], True, True, [R_qkT], [bres[bnk]])
                        if diag >= 0:
                            qk.mask01 = maskcw[:, diag, :]
                            qk.mask_cols = (128 * diag, 128 * diag + 128)
                            qk.exp_cols = (128 * diag, 512)

                        def pv(P, Pr, kt=kt, ob=ob, diag=diag, nk=nk):
                            for qs in range(4):
                                if diag >= 0 and qs < diag:
                                    continue
                                bnk = ob[qs // 2]
                                c0 = (qs % 2) * 129
                                first = (kt == 0 and qs % 2 == 0)
                                K.mm(bk(bnk)[:, c0:c0 + 129], P[:, qs * 128:(qs + 1) * 128], dv_aug[:, kt, 0:129],
                                     first, kt == nk - 1, [Pr, R_dv], [bres[bnk]])

                        post = None
                        if kt == nk - 1:
                            def post(s=s, C=C, ob=ob, hd=hd):
                                gi = gcount[0] % 2
                                for half in range(2):
                                    bnk = ob[half]
                                    K.recip(rz[gi][:, s, 2 * half:2 * half + 2], bk(bnk)[:, 128:258:129], [bres[bnk]], [rz_r[gi]])
                                for qs in range(4):
                                    bnk = ob[qs // 2]
                                    c0 = (qs % 2) * 129
                                    K.ts("dve", tb_[gi][:, s, qs, :], bk(bnk)[:, c0:c0 + 128], rz[gi][:, s, qs:qs + 1], None,
                                         ALU.mult, None, [bres[bnk], rz_r[gi]], [tb_r[gi]])
                                if s == 1:
                                    gcount[0] += 1

                                    def post_b1(gi=gi):
                                        for qs in range(4):
                                            K.stt(ya4[gi][:, qs, :], tb_[gi][:, 1, qs, :], neglam[:], tb_[gi][:, 0, qs, :], ALU.mult, ALU.add,
                                                  [tb_r[gi], R_const], [ya_r[gi]])
                                            K.stt(yj, ya4[gi][:, qs, :], 1.0, ya4[gi][:, qs, :], ALU.mult, ALU.mult, [ya_r[gi]], [yj_r],
                                                  accum=sst[gi][:, qs:qs + 1])
                                        K.ts("pool", sst[gi][:, 4:8], sst[gi][:, 0:4], 1.0 / 128, EPS, ALU.mult, ALU.add, [yj_r], [sst_r[gi]])
                                        K.tt("pool", sst[gi][:, 8:12], sst[gi][:, 4:8], mhalf[:, 0:4], ALU.pow, [sst_r[gi], R_const], [sst_r[gi]])
                                        for qs in range(4):
                                            K.stt(yan4[gi][:, qs, :], ya4[gi][:, qs, :], sst[gi][:, 8 + qs:9 + qs], g08[:], ALU.mult, ALU.mult,
                                                  [ya_r[gi], sst_r[gi], R_const], [yan_r[gi]])

                                    def post_b2(gi=gi, C=C, hd=hd):
                                        for qs in range(4):
                                            K.tr(bkb(7)[:, qs * 128:(qs + 1) * 128], yan4[gi][:, qs, :], ident_bf[:], [yan_r[gi], R_const], [bres[7]])
                                        K.cp("dve", yaT_st[gi], bkb(7)[:, 0:512], [bres[7]], [yaT_st_r[gi]])
                                        K.dma("sp", yaT_s[C, :, hd, :], yaT_st[gi], [yaT_st_r[gi]], [R_yaT])
                                    pipe.defer(post_b1, 2)
                                    pipe.defer(post_b2, 5)
                        units.append((qk, 128, pv, post))
                    pipe.step(units)
            pipe.flush()
        S.barrier()
        if stop_after == "P2":
            return _finish()

        cv = Carver()
        nqT = cv.t(BF16, 4, SEQ)
        KT = cv.t(BF16, 4, SEQ)
        vs_aug = cv.t(BF16, NT, 2, 66)
        vw_aug = cv.t(BF16, NT, 2, 66)
        R_nqT, R_KT, R_vs, R_vw = Res(), Res(), Res(), Res()
        mark = cv.off
        xg = [cv.t(BF16, 8, 512) for _ in range(2)]
        xg_r = [Res(), Res()]
        Wn = cv.t(BF16, 8, 1304)
        R_Wn = Res()
        nq_tok = [cv.t(BF16, 512) for _ in range(2)]
        k_tok = [cv.t(BF16, 4, 128) for _ in range(2)]
        rtmpn = [cv.t(F32, 4, 256) for _ in range(2)]
        rtmpk = [cv.t(F32, 4, 192) for _ in range(2)]
        rtmpk_r = [Res(), Res()]
        gtmp = [cv.t(F32, 24) for _ in range(2)]
        nq_tok_r, k_tok_r, rtmpn_r, gtmp_r = [Res(), Res()], [Res(), Res()], [Res(), Res()], [Res(), Res()]
        for (d0, n_, c0) in ((0, 640, 1536), (640, 128, 2304), (768, 128, 2560), (896, 128, 2176), (1024, 128, 2432), (1152, 152, 2688)):
            K.dma("pool", Wn[:, :, d0:d0 + n_], w_in_d[:, c0:c0 + n_].rearrange("(k p) c -> p k c", p=128), [], [R_Wn])
        K.memset("dve", vs_aug[:, :, :, 64:65], 1.0, [R_vs])
        K.memset("dve", vw_aug[:, :, :, 64:65], 1.0, [R_vw])
        def p3A(t):
            g = t // 4
            gb = g % 2
            b = t % 2
            if t % 4 == 0:
                K.dma("sp", xg[gb], xTn_s[g], [R_xTn], [xg_r[gb]])
            pb3 = (0, 1, 2) if b == 0 else (3, 4, 5)
            for bi, (c0, cn) in enumerate(((0, 512), (512, 512), (1024, 280))):
                for k in range(8):
                    K.mm(bk(pb3[bi])[:, 0:cn], xg[gb][:, k, (t % 4) * 128:(t % 4 + 1) * 128], Wn[:, k, c0:c0 + cn],
                         k == 0, k == 7, [xg_r[gb], R_Wn], [bres[pb3[bi]]])
            A, Bk, Ck = pb3
            shp = [128, 2, 4, 32]
            cb4 = cos_t[:, t, :].unsqueeze(1).unsqueeze(1).broadcast_to(shp)
            sb4 = sin_t[:, t, :].unsqueeze(1).unsqueeze(1).broadcast_to(shp)
            pin = bk(A).rearrange("p (g j c d) -> p g j c d", g=2, j=4, c=2)
            pout = nq_tok[b].rearrange("p (j g c d) -> p g j c d", j=4, g=2, c=2)
            tv = [rtmpn[b][:, i, 0:256].rearrange("p (g j d) -> p g j d", g=2, j=4) for i in range(4)]
            x1, x2 = pin[:, :, :, 0, :], pin[:, :, :, 1, :]
            rr = [bres[A], R_const]
            K.tt("dve", tv[0], x1, cb4, ALU.mult, rr, [rtmpn_r[b]])
            K.tt("dve", tv[1], x2, sb4, ALU.mult, rr, [rtmpn_r[b]])
            K.tt("dve", tv[2], x2, cb4, ALU.mult, rr, [rtmpn_r[b]])
            K.tt("dve", tv[3], x1, sb4, ALU.mult, rr, [rtmpn_r[b]])
            K.tt("dve", pout[:, :, :, 0, :], tv[0], tv[1], ALU.subtract, [rtmpn_r[b]], [nq_tok_r[b]])
            K.tt("dve", pout[:, :, :, 1, :], tv[2], tv[3], ALU.add, [rtmpn_r[b]], [nq_tok_r[b]])
            kin = bk(Bk)[:, 0:384].rearrange("p (h c d) -> p h c d", h=6, c=2)
            rope("dve", k_tok[b][:, 0:3, :].rearrange("p s (h c d) -> p (s h) c d", h=2, c=2), kin, t, 6,
                 rtmpk[b], rtmpk_r[b], [bres[Bk]], [k_tok_r[b]])
            K.cp("act", k_tok[b][:, 3, :], bk(Bk)[:, 384:512], [bres[Bk]], [k_tok_r[b]])
            K.cp("act", vs_aug[:, t, :, 0:64], bk(Ck)[:, 0:128].rearrange("p (g d) -> p g d", g=2), [bres[Ck]], [R_vs])
            K.cp("act", vw_aug[:, t, :, 0:64], bk(Ck)[:, 128:256].rearrange("p (g d) -> p g d", g=2), [bres[Ck]], [R_vw])
            K.act(gtmp[b], bk(Ck)[:, 256:280], AF.Exp, [bres[Ck]], [gtmp_r[b]], scale=-1.0)
            K.ts("dve", gtmp[b], gtmp[b], 1.0, None, ALU.add, None, [gtmp_r[b]], [gtmp_r[b]])
            K.recip(gate[:, t, :], gtmp[b], [gtmp_r[b]], [R_gate])

        def p3B(t):
            b = t % 2
            tb = 6 + b
            for i in range(4):
                K.tr(bkb(tb)[:, i * 128:(i + 1) * 128], nq_tok[b][:, i * 128:(i + 1) * 128], ident_bf[:],
                     [nq_tok_r[b], R_const], [bres[tb]])
            for i in range(4):
                K.tr(bkb(tb)[:, 512 + i * 128:512 + (i + 1) * 128], k_tok[b][:, i, :], ident_bf[:],
                     [k_tok_r[b], R_const], [bres[tb]])
            K.cp("act", nqT[:, :, t * 128:(t + 1) * 128], bkb(tb)[:, 0:512].rearrange("p (a b) -> p a b", a=4), [bres[tb]], [R_nqT])
            K.cp("act", KT[:, :, t * 128:(t + 1) * 128], bkb(tb)[:, 512:1024].rearrange("p (a b) -> p a b", a=4), [bres[tb]], [R_KT])

        for tt_ in range(NT + 1):
            if tt_ < NT:
                p3A(tt_)
            if tt_ >= 1:
                p3B(tt_ - 1)
        S.barrier()
        if stop_after == "P3a":
            return _finish()

        cv.off = mark
        kcmpT = cv.t(BF16, 256)
        vc_aug = cv.t(BF16, 2, 2, 130)
        mark = cv.off
        w1 = [cv.t(BF16, 32, 256) for _ in range(2)]
        w2kd = cv.t(BF16, 2, 128)
        w2v = cv.t(BF16, 2, 64)
        peT = cv.t(BF16, 2, 32)
        cb_sb = cv.t(F32, 2, 2)
        hid_sb = cv.t(BF16, 2, 2, 2, 256)
        R_w1, R_w2, R_pe, R_cb, R_hid, R_kcmp, R_vca = Res(), Res(), Res(), Res(), Res(), Res(), Res()
        for kv, wd_ in enumerate((w1k_d, w1v_d)):
            for half in range(2):
                K.dma("pool", w1[kv][64 * half:64 * half + 64], wd_.rearrange("(l d) h -> d l h", d=64), [], [R_w1])
        for half in range(2):
            K.dma("pool", w2kd[:, :, 64 * half:64 * half + 64], w2k_d.rearrange("(c p) d -> p c d", p=128), [], [R_w2])
        K.dma("pool", w2v, w2v_d.rearrange("(c p) d -> p c d", p=128), [], [R_w2])
        with nc.allow_non_contiguous_dma(reason="tiny pe transpose load"):
            for kv, pd in enumerate((pek_d, pev_d)):
                for half in range(2):
                    K.dma("pool", peT[64 * half:64 * half + 64, kv, :], pd.rearrange("l d -> d l"), [], [R_pe])
        K.memset("dve", vc_aug[:, :, :, 64:65], 1.0, [R_vca])
        for g in range(2):
            K.dma("pool", vc_aug[:, g, :, 66:130], ov_d, [], [R_vca])
        for kv in range(2):
            for hc in range(2):
                col = kv * 2 + hc
                for l in range(32):
                    K.mm(bk(0)[:, col:col + 1], w1[kv][0:64, l, hc * 128:(hc + 1) * 128], peT[0:64, kv, l:l + 1],
                         l == 0 and col == 0, l == 31, [R_w1, R_pe], [bres[0]])
        K.cp("dve", cb_sb.rearrange("p a b -> p (a b)"), bk(0)[:, 0:4], [bres[0]], [R_cb])
        NCB = (SEQ - 32) // 16 + 1
        ui = 0
        for kv in range(2):
            for hc in range(2):
                bb = (1, 2) if ui % 2 == 0 else (3, 6)
                ui += 1
                for l in range(32):
                    for g in range(2):
                        K.mm(bk(bb[g])[:, 0:NCB], w1[kv][64 * g:64 * g + 64, l, hc * 128:(hc + 1) * 128],
                             KT[64 * g:64 * g + 64, 3 * kv, l:l + 16 * (NCB - 1) + 1:16], l == 0, l == 31, [R_w1, R_KT], [bres[bb[g]]])
                for g in range(2):
                    K.act(hid_sb[:, kv, g, hc, 0:NCB], bk(bb[g])[:, 0:NCB], AF.Silu, [bres[bb[g]], R_cb], [R_hid],
                          bias=cb_sb[:, kv, hc:hc + 1])
        for g in range(2):
            for hc in range(2):
                K.mm(bk(4)[:, 0:NCB], w2kd[:, hc, :], hid_sb[:, 0, g, hc, 0:NCB], hc == 0, hc == 1, [R_w2, R_hid], [bres[4]])
            K.cp("dve", kcmpT[64 * g:64 * g + 64, 0:NCB], bk(4)[64 * g:64 * g + 64, 0:NCB], [bres[4]], [R_kcmp])
            for ct in range(2):
                n = min(128, NCB - ct * 128)
                if n <= 0:
                    continue
                for hc in range(2):
                    K.mm(bk(5)[0:n, ct * 64:ct * 64 + 64], hid_sb[:, 1, g, hc, ct * 128:ct * 128 + n], w2v[:, hc, :],
                         hc == 0 and ct == 0, hc == 1, [R_hid, R_w2], [bres[5]])
            for ct in range(2):
                n = min(128, NCB - ct * 128)
                if n <= 0:
                    continue
                K.cp("dve", vc_aug[0:n, g, ct, 0:64], bk(5)[0:n, ct * 64:ct * 64 + 64], [bres[5]], [R_vca])
        S.barrier()
        if stop_after == "P3b":
            return _finish()

        cv.off = mark
        E_sb = cv.t(BF16, NT, 128)
        pbufs = [cv.t(BF16, 512) for _ in range(6)]
        pres = [Res() for _ in range(6)]
        acc = cv.t(F32, 4, 512)
        imp = cv.t(F32, 4, 2, 64)
        imp2 = cv.t(F32, 64)
        selb = cv.t(F32, 64)
        selb_bf = cv.t(BF16, 4, 2, 64)
        R_selbf = Res()
        m8 = cv.t(F32, 16)
        selbT = cv.t(BF16, 512)
        btab = [cv.t(F32, 4, 64) for _ in range(2)]
        cmk = [cv.t(BF16, 2, 512) for _ in range(2)]
        rzn = [cv.t(F32, 8) for _ in range(2)]
        yb_bf = cv.t(BF16, 4, 512)
        ybT_st = [cv.t(BF16, 4, 512) for _ in range(2)]
        R_E, R_acc, R_imp, R_sel, R_selbT, R_ybbf = Res(), Res(), Res(), Res(), Res(), Res()
        btab_r, cmk_r, rzn_r, ybT_st_r = [Res(), Res()], [Res(), Res()], [Res(), Res()], [Res(), Res()]
        R_ybT = Res("ybT_s")
        K.dma("pool", E_sb[0:64], E_d, [], [R_E])
        K.dma("pool", E_sb[64:128], E_d, [], [R_E])
        hh_list = [(hh, hh // 4, hh % 4) for hh in range(8)]
        ecount = [0]

        def evac_branch(bnk, ncols, hh, br, C, with_imp):
            e = ecount[0] % 2
            ecount[0] += 1
            Ov = bk(bnk)[:, 0:4 * ncols].rearrange("p (a b) -> p a b", a=4)
            g = hh // 4
            K.ts("dve", rzn[e][:, 0:4], Ov[:, :, 64], 1e-30, None, ALU.add, None, [bres[bnk]], [rzn_r[e]])
            K.recip(rzn[e][:, 0:4], rzn[e][:, 0:4], [rzn_r[e]], [rzn_r[e]])
            K.tt("dve", rzn[e][:, 4:8], rzn[e][:, 0:4], gate[:, 4 * C:4 * C + 4, br * 8 + hh], ALU.mult,
                 [rzn_r[e], R_gate], [rzn_r[e]])
            for qs in range(4):
                dst = acc[:, qs, hh * 64:(hh + 1) * 64]
                if br == 0:
                    K.ts("dve", dst, Ov[:, qs, 0:64], rzn[e][:, 4 + qs:5 + qs], None, ALU.mult, None,
                         [bres[bnk], rzn_r[e]], [R_acc])
                else:
                    K.stt(dst, Ov[:, qs, 0:64], rzn[e][:, 4 + qs:5 + qs], dst, ALU.mult, ALU.add,
                          [bres[bnk], rzn_r[e], R_acc], [R_acc])
                if with_imp:
                    K.stt(imp[:, qs, g, :], Ov[:, qs, 65:129], rzn[e][:, qs:qs + 1], imp[:, qs, g, :], ALU.mult, ALU.add,
                          [bres[bnk], rzn_r[e], R_imp], [R_imp])

        pipe = AttnPipe([0, 1, 2, 3], pbufs, pres, look=2)
        obank_i = [0]
        pairs = [((j, 0, j), (4 + j, 1, j)) for j in range(4)]
        for C in range(NG):
            cbi = C % 2
            for Cn in ([0, 1] if C == 0 else [C + 1]):
                if Cn < NG:
                    K.dma("sp", btab[Cn % 2], btab_d[Cn * 512:(Cn + 1) * 512, :].rearrange("(a p) n -> p a n", p=128), [], [btab_r[Cn % 2]])
                    K.dma("pool", cmk[Cn % 2], cmask_d[:, :, Cn * 512:(Cn + 1) * 512], [], [cmk_r[Cn % 2]])
            pipe.flush()
            for g in range(2):
                K.cp("dve", imp[:, :, g, :], btab[cbi], [btab_r[cbi]], [R_imp])
            ncts = [ct for ct in range(2) if (ct == 0 or 32 * C + 30 >= 128) and NCB - ct * 128 > 0]
            for pair in pairs:
                for ui_, ct in enumerate(ncts):
                    n = min(128, NCB - ct * 128)
                    units = []
                    for pi_, (hh, g, j) in enumerate(pair):
                        oa = 4 + 2 * pi_
                        ib = oa + 1

                        def qk(bnk, ct=ct, n=n, g=g, j=j, C=C, cbi=cbi):
                            K.mm(bk(bnk)[0:n, :], kcmpT[64 * g:64 * g + 64, ct * 128:ct * 128 + n],
                                 nqT[64 * g:64 * g + 64, j, C * 512:(C + 1) * 512], True, False, [R_kcmp, R_nqT], [bres[bnk]])
                            K.mm(bk(bnk)[0:n, :], ident_bf[:, 0:n], cmk[cbi][:, ct, :], False, True, [R_const, cmk_r[cbi]], [bres[bnk]])

                        def pv(P, Pr, ct=ct, n=n, g=g, oa=oa, ib=ib, first=(ui_ == 0), last=(ui_ == len(ncts) - 1)):
                            for qs in range(4):
                                K.mm(bk(oa)[:, qs * 65:qs * 65 + 65], P[0:n, qs * 128:(qs + 1) * 128], vc_aug[0:n, g, ct, 0:65],
                                     first and qs == 0, last, [Pr, R_vca], [bres[oa]])
                                K.mm(bk(ib)[:, qs * 64:qs * 64 + 64], P[0:n, qs * 128:(qs + 1) * 128], vc_aug[0:n, g, ct, 66:130],
                                     first and qs == 0, last, [Pr, R_vca], [bres[ib]])
                        post = None
                        if ui_ == len(ncts) - 1:
                            def post(oa=oa, ib=ib, hh=hh, C=C, g=g):
                                e = ecount[0] % 2
                                evac_branch(oa, 65, hh, 0, C, False)
                                Iv = bk(ib)[:, 0:256].rearrange("p (a b) -> p a b", a=4)
                                for qs in range(4):
                                    K.stt(imp[:, qs, g, :], Iv[:, qs, :], rzn[e][:, qs:qs + 1], imp[:, qs, g, :], ALU.mult, ALU.add,
                                          [bres[ib], rzn_r[e], R_imp], [R_imp])
                        units.append((qk, n, pv, post))
                    pipe.step(units)
            pipe.flush()
            for qs in range(4):
                for g in range(2):
                    iv = imp[:, qs, g, :]
                    K.S.op("dve", lambda h, iv=iv: h.max(out=m8[:, 0:8], in_=iv), reads=[R_imp], writes=[R_sel])
                    K.S.op("dve", lambda h, iv=iv: h.match_replace(out=imp2, in_to_replace=m8[:, 0:8], in_values=iv, imm_value=-1e9),
                           reads=[R_imp, R_sel], writes=[R_sel])
                    K.S.op("dve", lambda h: h.max(out=m8[:, 8:16], in_=imp2), reads=[R_sel], writes=[R_sel])
                    K.ts("dve", selb, iv, m8[:, 15:16], None, ALU.is_ge, None, [R_imp, R_sel], [R_sel])
                    K.ts("dve", selb_bf[:, qs, g, :], selb, -1.0, -NEG, ALU.add, ALU.mult, [R_sel], [R_selbf])

            def sel_transposes():
                for qs in range(4):
                    K.tr(bkb(0)[:, qs * 128:(qs + 1) * 128], selb_bf[:, qs].rearrange("p g n -> p (g n)"), ident_bf[:],
                         [R_selbf, R_const], [bres[0]])
                K.cp("dve", selbT, bkb(0)[:, 0:512], [bres[0]], [R_selbT])
            for br in (2, 1):
                if br == 1:
                    pipe.flush()
                    sel_transposes()
                    kts = list(range(4 * C + 4))
                else:
                    kts = [kt for kt in range(4 * C - 4, 4 * C + 4) if kt >= 0]
                Vt = vs_aug if br == 1 else vw_aug
                Rv = R_vs if br == 1 else R_vw
                slot = 1 if br == 1 else 2
                for pair in pairs:
                    ob0 = 4 + 2 * (obank_i[0] % 2)
                    obank_i[0] += 1
                    for ui_, kt in enumerate(kts):
                        off = kt - 4 * C
                        units = []
                        for pi_, (hh, g, j) in enumerate(pair):
                            oa = ob0 + pi_

                            def qk(bnk, kt=kt, off=off, g=g, j=j, C=C, br=br, slot=slot):
                                K.mm(bk(bnk), KT[64 * g:64 * g + 64, slot, kt * 128:(kt + 1) * 128],
                                     nqT[64 * g:64 * g + 64, j, C * 512:(C + 1) * 512], True, br == 2, [R_KT, R_nqT], [bres[bnk]])
                                if br == 1:
                                    K.mm(bk(bnk), E_sb[64 * g:64 * g + 64, kt, :], selbT[64 * g:64 * g + 64, :], False, True,
                                         [R_E, R_selbT], [bres[bnk]])
                            qlo, qhi = 0, 4
                            if off >= 0:
                                qk.mask01 = maskcw[:, off, :]
                                qk.mask_cols = (128 * off, 128 * off + 128)
                                qk.exp_cols = (128 * off, 512)
                                qlo = off
                            elif br == 2:
                                jw = off + 4
                                qk.mask01 = maskcw[:, 4 + jw, :]
                                qk.mask_cols = (128 * jw, 128 * jw + 128)
                                qk.exp_cols = (0, 128 * jw + 128)
                                qhi = jw + 1

                            def pv(P, Pr, kt=kt, g=g, oa=oa, Vt=Vt, Rv=Rv, first=(ui_ == 0), last=(ui_ == len(kts) - 1), qlo=qlo, qhi=qhi):
                                for qs in range(qlo, qhi):
                                    K.mm(bk(oa)[:, qs * 65:qs * 65 + 65], P[:, qs * 128:(qs + 1) * 128], Vt[:, kt, g, 0:65],
                                         first and qs == 0, last, [Pr, Rv], [bres[oa]])
                            post = None
                            if ui_ == len(kts) - 1:
                                def post(oa=oa, hh=hh, C=C, br=br):
                                    evac_branch(oa, 65, hh, br, C, False)
                            units.append((qk, 128, pv, post))
                        pipe.step(units)
            pipe.flush()
            K.cp("dve", yb_bf, acc, [R_acc], [R_ybbf])
            for qs in range(4):
                for fc in range(4):
                    K.tr(bkb(1)[:, fc * 128:(fc + 1) * 128], yb_bf[:, qs, fc * 128:(fc + 1) * 128], ident_bf[:],
                         [R_ybbf, R_const], [bres[1]])
                K.cp("dve", ybT_st[cbi][:, :, qs * 128:(qs + 1) * 128], bkb(1)[:, 0:512].rearrange("p (a b) -> p a b", a=4),
                     [bres[1]], [ybT_st_r[cbi]])
            K.dma("pool", ybT_s[C], ybT_st[cbi], [ybT_st_r[cbi]], [R_ybT])
        S.barrier()
        if stop_after == "P3c":
            return _finish()

        cv = Carver()
        Wmg = cv.t(BF16, 8, 2048)
        Wa = cv.t(BF16, 4, D)
        Wb = cv.t(BF16, 4, D)
        Wo = cv.t(BF16, 8, D)
        xg = [cv.t(BF16, 8, 512) for _ in range(2)]
        yag = [cv.t(BF16, 4, 512) for _ in range(2)]
        ybg = [cv.t(BF16, 4, 512) for _ in range(2)]
        xt = [cv.t(F32, D) for _ in range(2)]
        mT = cv.t(BF16, 8, 512)
        sg = [cv.t(F32, 2, 512) for _ in range(2)]
        mm_ = [cv.t(F32, 2, 512) for _ in range(2)]
        ho = [cv.t(F32, D) for _ in range(2)]
        R_W4, R_mT, R_h = Res(), Res(), Res("h_s")
        xg_r, yag_r, ybg_r, xt_r, sg_r, mm_r, ho_r = ([Res(), Res()] for _ in range(7))
        R_Wmg = [Res() for _ in range(4)]
        R_Wab = [Res(), Res()]
        R_Wo = Res()

        def ld_mg(cp):
            for i in range(2):
                c0 = C_MG + i * 1024 + cp * 256
                K.dma("pool", Wmg[:, :, i * 1024 + cp * 256:i * 1024 + cp * 256 + 256],
                      w_in_d[:, c0:c0 + 256].rearrange("(k p) c -> p k c", p=128), [], [R_Wmg[cp]])

        def ld_ab(hf):
            K.dma("pool", Wa[:, :, hf * 512:(hf + 1) * 512], wa_d[:, hf * 512:(hf + 1) * 512].rearrange("(k p) c -> p k c", p=128), [], [R_Wab[hf]])
            K.dma("pool", Wb[:, :, hf * 512:(hf + 1) * 512], wb_d[:, hf * 512:(hf + 1) * 512].rearrange("(k p) c -> p k c", p=128), [], [R_Wab[hf]])

        ld_mg(0)
        ld_ab(0)
        ld_mg(1)
        ld_mg(2)
        ld_ab(1)
        ld_mg(3)
        for kq in range(2):
            K.dma("pool", Wo[:, 4 * kq:4 * kq + 4, :], wout_d[kq * 512:(kq + 1) * 512, :].rearrange("(k p) c -> p k c", p=128), [], [R_Wo])
        for G in range(NG):
            gb = G % 2
            for Gn in ([0, 1] if G == 0 else [G + 1]):
                if Gn < NG:
                    K.dma("sp", xg[Gn % 2], xTn_s[Gn], [R_xTn], [xg_r[Gn % 2]])
                    K.dma("sp", yag[Gn % 2], yaT_s[Gn], [R_yaT], [yag_r[Gn % 2]])
                    K.dma("sp", ybg[Gn % 2], ybT_s[Gn], [R_ybT], [ybg_r[Gn % 2]])
            for dc in range(8):
                e = dc % 2
                pb4 = (0, 1, 2, 3) if e == 0 else (4, 5, 6, 7)
                for i in range(2):
                    for k in range(8):
                        K.mm(bk(pb4[i]), Wmg[:, k, i * 1024 + dc * 128:i * 1024 + (dc + 1) * 128], xg[gb][:, k, :],
                             k == 0, k == 7, [R_Wmg[dc // 2], xg_r[gb]], [bres[pb4[i]]])
                for i, (Wx, yg, yr) in enumerate(((Wa, yag, yag_r), (Wb, ybg, ybg_r))):
                    for k in range(4):
                        K.mm(bk(pb4[2 + i]), Wx[:, k, dc * 128:(dc + 1) * 128], yg[gb][:, k, :], k == 0, k == 3,
                             [R_Wab[dc // 4], yr[gb]], [bres[pb4[2 + i]]])
                for i in range(2):
                    K.act(sg[e][:, i, :], bk(pb4[i]), AF.Sigmoid, [bres[pb4[i]]], [sg_r[e]])
                for i in range(2):
                    K.tt("dve", mm_[e][:, i, :], sg[e][:, i, :], bk(pb4[2 + i]), ALU.mult, [sg_r[e], bres[pb4[2 + i]]], [mm_r[e]])
                K.tt("dve", mT[:, dc, :], mm_[e][:, 0, :], mm_[e][:, 1, :], ALU.add, [mm_r[e]], [R_mT])
            for qs in range(4):
                t = G * 4 + qs
                b = qs % 2
                K.dma("sp", xt[b], x_d[t * 128:(t + 1) * 128, :], [], [xt_r[b]])
                for n2 in range(2):
                    bnk = 2 * b + n2
                    for dc in range(8):
                        K.mm(bk(bnk), mT[:, dc, qs * 128:(qs + 1) * 128], Wo[:, dc, n2 * 512:(n2 + 1) * 512], dc == 0, dc == 7,
                             [R_mT, R_Wo], [bres[bnk]])
                    K.tt("dve", ho[b][:, n2 * 512:(n2 + 1) * 512], bk(bnk), xt[b][:, n2 * 512:(n2 + 1) * 512], ALU.add,
                         [bres[bnk], xt_r[b]], [ho_r[b]])
                K.dma("pool", h_s[t * 128:(t + 1) * 128, :], ho[b], [ho_r[b]], [R_h])
        S.barrier()
        if stop_after == "P4":
            return _finish()

        cv = Carver()
        Wd = cv.t(BF16, 22, D)
        hgs = [cv.t(F32, 4, D) for _ in range(2)]
        xs = [cv.t(F32, D) for _ in range(2)]
        junk = cv.t(F32, D)
        hT = cv.t(BF16, 8, 512)
        actT = cv.t(BF16, 22, 512)
        wr = [cv.t(BF16, 2, 8, 128) for _ in range(4)]
        sgl = [cv.t(F32, 512) for _ in range(2)]
        st1 = [cv.t(F32, 4) for _ in range(2)]
        yo = [cv.t(F32, D) for _ in range(2)]
        R_Wd, R_hT, R_act, R_y = Res(), Res(), Res(), Res("y")
        R_hgs = [Res(), Res()]
        xs_r, wr_r, sgl_r, st1_r, yo_r = [Res(), Res()], [Res() for _ in range(4)], [Res(), Res()], [Res(), Res()], [Res(), Res()]
        junk_r = Res()
        for kq in range(2):
            K.dma("pool", Wd[:, 11 * kq:11 * kq + 11, :], wd_d[kq * 1408:(kq + 1) * 1408, :].rearrange("(k p) c -> p k c", p=128), [], [R_Wd])
        wi = 0
        for G in range(NG):
            for Gn in ([0, 1] if G == 0 else [G + 1]):
                if Gn < NG:
                    K.dma("sp", hgs[Gn % 2], h_s[Gn * 512:(Gn + 1) * 512, :].rearrange("(a p) c -> p a c", p=128), [R_h], [R_hgs[Gn % 2]])
            hg = hgs[G % 2]
            R_hg = R_hgs[G % 2]
            for qs in range(4):
                b = qs % 2
                nt_A(None, qs, b, hg[:, qs, :], R_hg, load=False)
                nt_B(qs, b, g_ffn, hT, R_hT, (2 * b, 2 * b + 1))
            for j in range(22):
                w = wi % 4
                wi += 1
                K.dma("sp", wr[w], wgu_s[j], [R_wgu], [wr_r[w]])
                e = j % 2
                bg, bu = (4, 5) if e == 0 else (6, 7)
                for i, bnk in enumerate((bg, bu)):
                    for k in range(8):
                        K.mm(bk(bnk), wr[w][:, i, k, :], hT[:, k, :], k == 0, k == 7, [wr_r[w], R_hT], [bres[bnk]])
                K.act(sgl[e], bk(bg), AF.Silu, [bres[bg]], [sgl_r[e]])
                K.tt("dve", actT[:, j, :], sgl[e], bk(bu), ALU.mult, [sgl_r[e], bres[bu]], [R_act])
            for qs in range(4):
                t = G * 4 + qs
                b = qs % 2
                for n2 in range(2):
                    bnk = 2 * b + n2
                    for j in range(22):
                        K.mm(bk(bnk), actT[:, j, qs * 128:(qs + 1) * 128], Wd[:, j, n2 * 512:(n2 + 1) * 512], j == 0, j == 21,
                             [R_act, R_Wd], [bres[bnk]])
                    K.tt("dve", yo[b][:, n2 * 512:(n2 + 1) * 512], bk(bnk), hg[:, qs, n2 * 512:(n2 + 1) * 512], ALU.add,
                         [bres[bnk], R_hg], [yo_r[b]])
                K.act(junk, yo[b], AF.Square, [yo_r[b]], [junk_r], accum=st1[b][:, 0:1])
                K.act(st1[b][:, 1:2], st1[b][:, 0:1], AF.Ln, [junk_r, R_const], [st1_r[b]], scale=1.0 / D, bias=eps_ap)
                K.act(st1[b][:, 2:3], st1[b][:, 1:2], AF.Exp, [st1_r[b]], [st1_r[b]], scale=-0.5)
                K.stt(yo[b], yo[b], st1[b][:, 2:3], gfin_b[:], ALU.mult, ALU.mult, [yo_r[b], st1_r[b], R_const], [yo_r[b]])
                K.dma("pool", y_d[t * 128:(t + 1) * 128, :], yo[b], [yo_r[b]], [R_y])
        return _finish()


_PROG = {}


def kernel(**inputs):
    x = np.ascontiguousarray(np.asarray(inputs["x"], dtype=np.float32))
    B, SEQ, _ = x.shape
    pos = np.asarray(inputs["positions"]).astype(np.int32)
    f = lambda k, shp: np.ascontiguousarray(np.asarray(inputs[k], dtype=np.float32).reshape(shp))
    common = {
        "attn_norm_g": f("attn_norm_g", (1, D)), "w_in": f("w_in", (D, INDIM)), "diff_lambda": f("diff_lambda", (1, 256)),
        "diff_subln_g": f("diff_subln_g", (1, 128)), "cmp_pe_k": f("cmp_pe_k", (32, 64)), "cmp_pe_v": f("cmp_pe_v", (32, 64)),
        "cmp_k_w1": f("cmp_k_w1", (2048, 256)), "cmp_k_w2": f("cmp_k_w2", (256, 64)),
        "cmp_v_w1": f("cmp_v_w1", (2048, 256)), "cmp_v_w2": f("cmp_v_w2", (256, 64)),
        "w_branch_a": f("w_branch_a", (512, D)), "w_branch_b": f("w_branch_b", (512, D)), "w_out": f("w_out", (D, D)),
        "ffn_norm_g": f("ffn_norm_g", (1, D)), "w_gate": f("w_gate", (D, DFF)), "w_up": f("w_up", (D, DFF)),
        "w_down": f("w_down", (DFF, D)), "final_norm_g": f("final_norm_g", (1, D)),
    }
    common.update(host_consts(SEQ))
    if SEQ not in _PROG:
        _PROG[SEQ] = build_program(SEQ)
    nc = _PROG[SEQ]
    in_maps = [dict(common, x=x[b], pos=np.ascontiguousarray(pos[b].reshape(SEQ, 1))) for b in range(B)]
    res = run_bass_kernel_spmd(nc, in_maps, core_ids=list(range(B)))
    return np.stack([np.asarray(r["y"], dtype=np.float32) for r in res.results], axis=0)
```

```python
import math
import numpy as np
import concourse.bass as bass
import concourse.mybir as mybir
from concourse.bass_utils import run_bass_kernel_spmd

F32 = mybir.dt.float32
BF16 = mybir.dt.bfloat16
I32 = mybir.dt.int32
AF = mybir.ActivationFunctionType
ALU = mybir.AluOpType
AX = mybir.AxisListType


class Res:
    __slots__ = ("w", "r", "name", "excl")

    def __init__(self, name="", excl=False):
        self.w = None
        self.r = {}
        self.name = name
        self.excl = excl


class Sched:
    ENG = ("pe", "dve", "act", "pool", "sp")
    NDSEM = 12

    def __init__(self, nc, stack):
        self.nc = nc
        self.h = {"pe": nc.tensor, "dve": nc.vector, "act": nc.scalar, "pool": nc.gpsimd, "sp": nc.sync}
        self.sems = {}
        self.cnt = {}
        self.ops = {e: [] for e in self.ENG}
        self.waited = {e: {} for e in self.ENG}
        for e in self.ENG:
            self.sems[e] = stack.enter_context(nc.semaphore("s_" + e))
            self.cnt[e] = 0
        self.dsem = {}
        self.dsem_cnt = {}
        self.dsem_rr = {}
        for q in ("sp", "pool", "act", "bg"):
            self.dsem[q] = [stack.enter_context(nc.semaphore(f"d_{q}{i}")) for i in range(self.NDSEM)]
            self.dsem_cnt[q] = [0] * self.NDSEM
            self.dsem_rr[q] = 0
        self.nops = 0
        self.bar = {e: [] for e in self.ENG}

    def _collect(self, eng, reads, writes, extra=()):
        own = ("e", eng)
        waits = {}

        def need(tok, allow_own):
            if tok is None:
                return
            k, v = tok
            if k == own and not allow_own:
                return
            if waits.get(k, 0) < v:
                waits[k] = v

        for r in reads:
            need(r.w, True)
        own_ok = (eng != "pe")
        for r in writes:
            need(r.w, own_ok)
            for k, v in r.r.items():
                need((k, v), own_ok)
        for t in extra:
            need(t, True)
        for t in self.bar[eng]:
            need(t, True)
        self.bar[eng] = []
        wd = self.waited[eng]
        out = []
        for k, v in waits.items():
            if wd.get(k, 0) >= v:
                continue
            wd[k] = v
            out.append((k, v))
        return out

    def _mark(self, tok, reads, writes):
        k, v = tok
        for r in reads:
            if r.r.get(k, 0) < v:
                r.r[k] = v
        for r in writes:
            r.w = tok
            r.r = {}

    def op(self, eng, fn, reads=(), writes=()):
        if any(r.excl for r in reads):
            writes = list(writes) + [r for r in reads if r.excl]
            reads = [r for r in reads if not r.excl]
        waits = self._collect(eng, reads, writes)
        self.cnt[eng] += 1
        tok = (("e", eng), self.cnt[eng])
        self.ops[eng].append((waits, fn, ("e", eng), 1))
        self._mark(tok, reads, writes)
        self.nops += 1
        return tok

    def dma(self, q, fn, reads=(), writes=(), grp=None):
        grp = grp or q
        i = self.dsem_rr[grp]
        self.dsem_rr[grp] = (i + 1) % self.NDSEM
        key = ("d", grp, i)
        prev = self.dsem_cnt[grp][i]
        extra = [(key, prev)] if prev > 0 else []
        waits = self._collect(q, reads, writes, extra)
        self.dsem_cnt[grp][i] = prev + 16
        tok = (key, prev + 16)
        self.ops[q].append((waits, fn, key, 16))
        self._mark(tok, reads, writes)
        self.nops += 1
        return tok

    def all_tokens(self):
        toks = [(("e", e), self.cnt[e]) for e in self.ENG if self.cnt[e] > 0]
        for q in self.dsem_cnt:
            for i, v in enumerate(self.dsem_cnt[q]):
                if v > 0:
                    toks.append((("d", q, i), v))
        return toks

    def barrier(self, skip_bg=False):
        toks = [t for t in self.all_tokens() if not (skip_bg and t[0][0] == "d" and t[0][1] == "bg")]
        for e in self.ENG:
            self.bar[e] = list(toks)

    def final_all(self):
        self.ops["sp"].append((self.all_tokens(), None, None, 0))

    def _sem(self, key):
        if key[0] == "e":
            return self.sems[key[1]]
        return self.dsem[key[1]][key[2]]

    def final_wait(self, eng, toks):
        waits = []
        for k, v in toks:
            waits.append((k, v))
        self.ops[eng].append((waits, None, None, 0))

    def emit(self):
        nc = self.nc
        with nc.Block() as block:
            def mk(e):
                def body(h):
                    for waits, fn, key, inc in self.ops[e]:
                        for k, v in waits:
                            h.wait_ge(self._sem(k), v)
                        if fn is not None:
                            ins = fn(h)
                            ins.then_inc(self._sem(key), inc)
                return body
            block.tensor(mk("pe"))
            block.vector(mk("dve"))
            block.scalar(mk("act"))
            block.gpsimd(mk("pool"))
            block.sync(mk("sp"))


D = 1024
HD = 64
DFF = 2816
NEG = -30000.0
EPS = 1e-6
INDIM = 4888
C_DQ, C_DK, C_DV, C_NQ = 0, 512, 1024, 1536
C_NSA0 = 1536
C_MG = 2840


def host_consts(S):
    c = {}
    c["ident"] = np.eye(128, dtype=np.float32)
    kk = np.arange(128)[:, None]
    q = np.arange(512)[None, :]
    m = np.zeros((128, 8, 512), np.float32)
    for j in range(4):
        m[:, j, :] = np.where(128 * j + kk <= q, 1.0, 0.0)
        m[:, 4 + j, :] = np.where(128 * j + kk > q, 1.0, 0.0)
    c["maskcw"] = m
    cc = np.arange(256)[:, None]
    qq = np.arange(S)[None, :]
    cm = np.where((16 * cc + 31 <= qq) & (cc <= 254), 0.0, NEG).astype(np.float32)
    c["cmask"] = np.ascontiguousarray(cm.reshape(2, 128, S).transpose(1, 0, 2))
    nkt = S // 128
    E = np.zeros((64, nkt, 128), np.float32)
    for kt in range(nkt):
        for k2 in range(128):
            n = 2 * kt + k2 // 64
            if n < 64:
                E[n, kt, k2] = 1.0
    c["E"] = E
    ci = np.arange(256)[:, None] * 16
    sj = np.arange(64)[None, :] * 64
    ov = ((ci < sj + 64) & (ci + 32 > sj)).astype(np.float32)
    ov[255] = 0.0
    c["ov"] = np.ascontiguousarray(ov.reshape(2, 128, 64).transpose(1, 0, 2))
    qb = (np.arange(S) // 64)[:, None]
    n = np.arange(64)[None, :]
    forced = (n == 0) | (n == qb) | (n == qb - 1)
    bt = np.where(forced, 100.0 + n, 0.0)
    bt = np.where(n > qb, -1000.0, bt).astype(np.float32)
    c["btab"] = bt
    c["invf"] = (1.0 / (10000.0 ** (np.arange(0, 64, 2, dtype=np.float32) / 64))).astype(np.float32).reshape(1, 32)
    return c


class KB:
    def __init__(self, nc, S_, st):
        self.nc = nc
        self.S = S_
        self.st = st

    def mm(self, out, lhsT, rhs, start, stop, reads, writes):
        return self.S.op("pe", lambda h: h.matmul(out, lhsT, rhs, start=start, stop=stop, skip_group_check=True),
                         reads=reads, writes=writes)

    def tr(self, out, in_, ident, reads, writes):
        return self.S.op("pe", lambda h: h.transpose(out, in_, ident), reads=reads, writes=writes)

    def act(self, out, in_, func, reads, writes, scale=None, bias=None, accum=None):
        kw = {}
        if scale is not None:
            kw["scale"] = scale
        if bias is not None:
            kw["bias"] = bias
        if accum is not None:
            kw["accum_out"] = accum
        return self.S.op("act", lambda h: h.activation(out=out, in_=in_, func=func, **kw), reads=reads, writes=writes)

    def ts(self, eng, out, in0, s1, s2, op0, op1, reads, writes):
        if op1 is None:
            return self.S.op(eng, lambda h: h.tensor_scalar(out=out, in0=in0, scalar1=s1, scalar2=None, op0=op0),
                             reads=reads, writes=writes)
        return self.S.op(eng, lambda h: h.tensor_scalar(out=out, in0=in0, scalar1=s1, scalar2=s2, op0=op0, op1=op1),
                         reads=reads, writes=writes)

    def tt(self, eng, out, in0, in1, op, reads, writes):
        return self.S.op(eng, lambda h: h.tensor_tensor(out=out, in0=in0, in1=in1, op=op), reads=reads, writes=writes)

    def stt(self, out, in0, scalar, in1, op0, op1, reads, writes, accum=None):
        if accum is None:
            return self.S.op("dve", lambda h: h.scalar_tensor_tensor(out=out, in0=in0, scalar=scalar, in1=in1, op0=op0, op1=op1),
                             reads=reads, writes=writes)
        return self.S.op("dve", lambda h: h.scalar_tensor_tensor(out=out, in0=in0, scalar=scalar, in1=in1, op0=op0, op1=op1,
                                                                  accum_out=accum), reads=reads, writes=writes)

    def cp(self, eng, out, in_, reads, writes):
        if eng == "act":
            return self.S.op("act", lambda h: h.copy(out=out, in_=in_), reads=reads, writes=writes)
        return self.S.op(eng, lambda h: h.tensor_copy(out=out, in_=in_), reads=reads, writes=writes)

    def recip(self, out, in_, reads, writes):
        return self.S.op("dve", lambda h: h.reciprocal(out=out, in_=in_), reads=reads, writes=writes)

    def memset(self, eng, ap, val, writes):
        return self.S.op(eng, lambda h: h.memset(ap, val), writes=writes)

    def dma(self, q, out, in_, reads, writes, grp=None):
        return self.S.dma(q, lambda h: h.dma_start(out=out, in_=in_), reads=reads, writes=writes, grp=grp)


def build_program(SEQ, debug=False, stop_after=None):
    from contextlib import ExitStack
    nc = bass.Bass("TRN2", target_bir_lowering=False)
    NT = SEQ // 128
    NG = SEQ // 512
    ext_in = lambda n, shp, dt=F32: nc.dram_tensor(n, shp, dt, kind="ExternalInput").ap()
    x_d = ext_in("x", [SEQ, D])
    pos_d = ext_in("pos", [SEQ, 1], I32)
    attn_g_d = ext_in("attn_norm_g", [1, D])
    w_in_d = ext_in("w_in", [D, INDIM])
    lam_d = ext_in("diff_lambda", [1, 256])
    subg_d = ext_in("diff_subln_g", [1, 128])
    pek_d = ext_in("cmp_pe_k", [32, 64])
    pev_d = ext_in("cmp_pe_v", [32, 64])
    w1k_d = ext_in("cmp_k_w1", [2048, 256])
    w2k_d = ext_in("cmp_k_w2", [256, 64])
    w1v_d = ext_in("cmp_v_w1", [2048, 256])
    w2v_d = ext_in("cmp_v_w2", [256, 64])
    wa_d = ext_in("w_branch_a", [512, D])
    wb_d = ext_in("w_branch_b", [512, D])
    wout_d = ext_in("w_out", [D, D])
    ffn_g_d = ext_in("ffn_norm_g", [1, D])
    wg_d = ext_in("w_gate", [D, DFF])
    wu_d = ext_in("w_up", [D, DFF])
    wd_d = ext_in("w_down", [DFF, D])
    fin_g_d = ext_in("final_norm_g", [1, D])
    ident_d = ext_in("ident", [128, 128])
    maskcw_d = ext_in("maskcw", [128, 8, 512])
    cmask_d = ext_in("cmask", [128, 2, SEQ])
    E_d = ext_in("E", [64, NT, 128])
    ov_d = ext_in("ov", [128, 2, 64])
    btab_d = ext_in("btab", [SEQ, 64])
    invf_d = ext_in("invf", [1, 32])
    y_d = nc.dram_tensor("y", [SEQ, D], F32, kind="ExternalOutput").ap()
    skind = "ExternalOutput" if debug else "Internal"
    scr = lambda n, shp, dt: nc.dram_tensor(n, shp, dt, kind=skind).ap()
    xTn_s = scr("xTn_s", [NG, 128, 8, 512], BF16)
    yaT_s = scr("yaT_s", [NG, 128, 4, 512], BF16)
    ybT_s = scr("ybT_s", [NG, 128, 4, 512], BF16)
    h_s = scr("h_s", [SEQ, D], F32)
    wgu_s = scr("wgu_s", [22, 128, 2, 8, 128], BF16)

    with ExitStack() as st:
        S = Sched(nc, st)
        K = KB(nc, S, st)

        def _finish():
            S.final_all()
            with nc.allow_non_contiguous_dma(reason="small strided constant loads"):
                S.emit()
            return nc
        sb = lambda n, shp, dt: st.enter_context(nc.sbuf_tensor(n, shp, dt))
        banks = [st.enter_context(nc.psum_tensor(f"bank{i}", [128, 512], F32)) for i in range(8)]
        bres = [Res(f"bank{i}", excl=True) for i in range(8)]
        bk = lambda i: banks[i][:]
        bkb = lambda i: banks[i][:].bitcast(BF16)
        ident_bf = sb("ident_bf", [128, 128], BF16)
        ident_f = sb("ident_f", [128, 128], F32)
        cos_t = sb("cos_t", [128, NT, 32], F32)
        sin_t = sb("sin_t", [128, NT, 32], F32)
        g_attn = sb("g_attn", [128, 8], F32)
        g_ffn = sb("g_ffn", [128, 8], F32)
        gfin_b = sb("gfin_b", [128, D], F32)
        g08 = sb("g08", [128, 128], F32)
        neglam = sb("neglam", [128, 1], F32)
        maskcw = sb("maskcw_sb", [128, 8, 512], BF16)
        gate = sb("gate", [128, NT, 24], F32)
        R_const = Res("const")
        R_gate = Res("gate")
        ARENA_B = 148 * 1024
        arena = sb("arena", [128, ARENA_B // 2], BF16)

        class Carver:
            def __init__(self):
                self.off = 0

            def take(self, nbytes_pp, dt, shape_free, parts=128):
                assert self.off % 4 == 0
                n2 = (nbytes_pp + 3) // 4 * 4
                assert self.off + n2 <= ARENA_B, ("arena overflow", self.off + n2)
                v = arena[0:parts, self.off // 2:(self.off + nbytes_pp) // 2]
                self.off += n2
                if dt == F32:
                    v = v.bitcast(F32)
                return v

            def t(self, dt, *free, parts=128):
                n = 1
                for f in free:
                    n *= f
                esz = 4 if dt == F32 else 2
                v = self.take(n * esz, dt, free, parts)
                if len(free) == 2:
                    v = v.rearrange("p (a b) -> p a b", a=free[0])
                elif len(free) == 3:
                    v = v.rearrange("p (a b c) -> p a b c", a=free[0], b=free[1])
                elif len(free) == 4:
                    v = v.rearrange("p (a b c d) -> p a b c d", a=free[0], b=free[1], c=free[2])
                return v

        cv = Carver()
        K.dma("pool", ident_bf[:], ident_d, [], [R_const])
        K.dma("sp", ident_f[:], ident_d, [], [R_const])
        K.dma("pool", maskcw[:], maskcw_d, [], [R_const])
        K.dma("sp", g_attn[:], attn_g_d.rearrange("o (k p) -> p (o k)", p=128), [], [R_const])
        K.dma("sp", g_ffn[:], ffn_g_d.rearrange("o (k p) -> p (o k)", p=128), [], [R_const])
        K.dma("sp", gfin_b[:], fin_g_d.partition_broadcast(128), [], [R_const])
        K.dma("sp", g08[:], subg_d.partition_broadcast(128), [], [R_const])
        K.ts("dve", g08[:], g08[:], 0.8, None, ALU.mult, None, [R_const], [R_const])
        pos_i = cv.t(I32 if False else F32, NT)
        pos_i32 = pos_i.bitcast(I32)
        pos_f = cv.t(F32, NT)
        invf_b = cv.t(F32, 32)
        ang = cv.t(F32, NT, 32)
        tmpa = cv.t(F32, NT, 32)
        tmpi = cv.t(F32, NT, 32)
        tmpi_i = tmpi.bitcast(I32)
        R_rope = Res("rope")
        K.dma("sp", pos_i32, pos_d.rearrange("(t p) o -> p (t o)", p=128), [], [R_rope])
        K.dma("sp", invf_b, invf_d.partition_broadcast(128), [], [R_rope])
        K.cp("dve", pos_f, pos_i32, [R_rope], [R_rope])
        K.tt("dve", ang, pos_f.unsqueeze(2).broadcast_to([128, NT, 32]), invf_b.unsqueeze(1).broadcast_to([128, NT, 32]),
             ALU.mult, [R_rope], [R_rope])
        TWO_PI = 2.0 * math.pi
        for (dst, shift) in ((sin_t, 0.0), (cos_t, math.pi / 2)):
            K.ts("dve", tmpa, ang, shift, 1.0 / TWO_PI, ALU.add, ALU.mult, [R_rope], [R_rope])
            K.cp("dve", tmpi_i, tmpa, [R_rope], [R_rope])
            K.cp("dve", tmpa, tmpi_i, [R_rope], [R_rope])
            K.ts("dve", tmpa, tmpa, -TWO_PI, shift, ALU.mult, ALU.add, [R_rope], [R_rope])
            K.tt("dve", tmpa, tmpa, ang, ALU.add, [R_rope], [R_rope])
            K.ts("dve", tmpi, tmpa, math.pi, -TWO_PI, ALU.is_gt, ALU.mult, [R_rope], [R_rope])
            K.tt("dve", tmpa, tmpa, tmpi, ALU.add, [R_rope], [R_rope])
            K.ts("dve", tmpi, tmpa, -math.pi, TWO_PI, ALU.is_lt, ALU.mult, [R_rope], [R_rope])
            K.tt("dve", tmpa, tmpa, tmpi, ALU.add, [R_rope], [R_rope])
            K.ts("dve", tmpa, tmpa, math.pi, -math.pi, ALU.min, ALU.max, [R_rope], [R_rope])
            K.act(dst[:], tmpa, AF.Sin, [R_rope], [R_const])
        lam_sb = cv.t(F32, 256, parts=1)
        lam_j = cv.t(F32, 64, parts=1)
        lam_s = cv.t(F32, 4, parts=1)
        ones_row = cv.t(F32, 128, parts=1)
        R_lam = Res("lam")
        K.dma("sp", lam_sb, lam_d, [], [R_lam])
        K.memset("dve", ones_row, 1.0, [R_lam])
        K.stt(lam_j, lam_sb[:, 0:64], 1.0, lam_sb[:, 64:128], ALU.mult, ALU.mult, [R_lam], [R_lam], accum=lam_s[:, 0:1])
        K.stt(lam_j, lam_sb[:, 128:192], 1.0, lam_sb[:, 192:256], ALU.mult, ALU.mult, [R_lam], [R_lam], accum=lam_s[:, 1:2])
        K.act(lam_s[:, 2:4], lam_s[:, 0:2], AF.Exp, [R_lam], [R_lam])
        K.tt("dve", lam_s[:, 0:1], lam_s[:, 3:4], lam_s[:, 2:3], ALU.subtract, [R_lam], [R_lam])
        K.ts("dve", lam_s[:, 0:1], lam_s[:, 0:1], -0.2, None, ALU.add, None, [R_lam], [R_lam])
        K.mm(bk(0)[:, 0:1], ones_row, lam_s[:, 0:1], True, True, [R_lam], [bres[0]])
        K.cp("dve", neglam[:], bk(0)[:, 0:1], [bres[0]], [R_const])
        _save_off = cv.off
        cv.off = ARENA_B - 2 * 4096
        wst = [cv.t(BF16, 2, 8, 128) for _ in range(2)]
        cv.off = _save_off
        wst_r = [Res("wst0"), Res("wst1")]
        R_wgu = Res("wgu_s")
        for j in range(22):
            b = j % 2
            K.dma("pool", wst[b][:, 0], wg_d[:, j * 128:(j + 1) * 128].rearrange("(k p) c -> p k c", p=128), [], [wst_r[b]], grp="bg")
            K.dma("pool", wst[b][:, 1], wu_d[:, j * 128:(j + 1) * 128].rearrange("(k p) c -> p k c", p=128), [], [wst_r[b]], grp="bg")
            K.dma("pool", wgu_s[j], wst[b], [wst_r[b]], [R_wgu], grp="bg")
        S.barrier(skip_bg=True)
        if stop_after == "P0":
            return _finish()

        cv = Carver()
        xt = [cv.t(F32, D) for _ in range(2)]
        xs = [cv.t(F32, D) for _ in range(2)]
        junk = cv.t(F32, D)
        stg = [cv.t(BF16, 8, 512) for _ in range(2)]
        st1 = [cv.t(F32, 4) for _ in range(2)]
        xt_r = [Res(), Res()]
        xs_r = [Res(), Res()]
        stg_r = [Res(), Res()]
        st1_r = [Res(), Res()]
        junk_r = Res()
        R_xTn = Res("xTn_s")

        def nt_A(src_tile_ap, t, b, xt_ap, xt_res, load=True):
            if load:
                K.dma("sp", xt_ap, src_tile_ap, [], [xt_res])
            K.act(junk, xt_ap, AF.Square, [xt_res], [junk_r], accum=st1[b][:, 0:1])
            K.act(st1[b][:, 1:2], st1[b][:, 0:1], AF.Ln, [junk_r, R_const], [st1_r[b]], scale=1.0 / D, bias=eps_ap)
            K.act(st1[b][:, 2:3], st1[b][:, 1:2], AF.Exp, [st1_r[b]], [st1_r[b]], scale=-0.5)
            K.ts("dve", xs[b], xt_ap, st1[b][:, 2:3], None, ALU.mult, None, [xt_res, st1_r[b]], [xs_r[b]])

        def nt_B(t, b, gcol, stage, stage_r, pb):
            for half in range(2):
                pbank = pb[half]
                for kq in range(4):
                    k = half * 4 + kq
                    K.tr(bk(pbank)[:, kq * 128:(kq + 1) * 128], xs[b][:, k * 128:(k + 1) * 128], ident_f[:],
                         [xs_r[b], R_const], [bres[pbank]])
                K.tt("dve", stage[:, half * 4:half * 4 + 4, (t % 4) * 128:(t % 4 + 1) * 128],
                     bk(pbank).rearrange("p (a b) -> p a b", a=4),
                     gcol[:, half * 4:half * 4 + 4].unsqueeze(2).broadcast_to([128, 4, 128]), ALU.mult,
                     [bres[pbank], R_const], [stage_r])

        eps_t = sb("eps_t", [128, 1], F32)
        K.memset("dve", eps_t[:], EPS, [R_const])
        eps_ap = eps_t[:]
        for tt_ in range(NT + 1):
            if tt_ < NT:
                nt_A(x_d[tt_ * 128:(tt_ + 1) * 128, :], tt_, tt_ % 2, xt[tt_ % 2], xt_r[tt_ % 2])
            if tt_ >= 1:
                t = tt_ - 1
                b = t % 2
                g = t // 4
                sgb = g % 2
                nt_B(t, b, g_attn, stg[sgb], stg_r[sgb], (2 * b, 2 * b + 1))
                if t % 4 == 3:
                    K.dma("sp", xTn_s[g], stg[sgb], [stg_r[sgb]], [R_xTn])
        S.barrier(skip_bg=True)
        if stop_after == "P1":
            return _finish()

        class AttnPipe:
            def __init__(self, st_banks, pbufs, pres, look=1):
                self.stb = st_banks
                self.pb = pbufs
                self.pr = pres
                self.look = look
                self.i = 0
                self.pend = []
                self.deferred = []

            def defer(self, fn, nsteps):
                self.deferred.append([nsteps, fn])

            def _tick(self):
                ready = [d for d in self.deferred if d[0] <= 1]
                self.deferred = [[d[0] - 1, d[1]] for d in self.deferred if d[0] > 1]
                for d in ready:
                    d[1]()

            def step(self, units):
                ent = []
                for (qk, nrows, pv, post) in units:
                    i = self.i
                    self.i += 1
                    bnk = self.stb[i % len(self.stb)]
                    pi = i % len(self.pb)
                    qk(bnk)
                    ent.append((bnk, pi, nrows, pv, post))
                for ui, (bnk, pi, nrows, pv, post) in enumerate(ent):
                    elo, ehi = getattr(units[ui][0], "exp_cols", (0, 512))
                    K.act(self.pb[pi][0:nrows, elo:ehi], bk(bnk)[0:nrows, elo:ehi], AF.Exp, [bres[bnk]], [self.pr[pi]], scale=0.125)
                    mk = getattr(units[ui][0], "mask01", None)
                    if mk is not None:
                        lo, hi = units[ui][0].mask_cols
                        K.tt("pool" if ui == 0 else "dve", self.pb[pi][0:nrows, lo:hi], self.pb[pi][0:nrows, lo:hi], mk[:, lo:hi],
                             ALU.mult, [self.pr[pi], R_const], [self.pr[pi]])
                self.pend.append(ent)
                if len(self.pend) > self.look:
                    self._flush1()
                self._tick()

            def unit(self, qk, nrows, pv, post=None):
                self.step([(qk, nrows, pv, post)])

            def _flush1(self):
                ent = self.pend.pop(0)
                for (bnk, pi, nrows, pv, post) in ent:
                    pv(self.pb[pi], self.pr[pi])
                for (bnk, pi, nrows, pv, post) in ent:
                    if post is not None:
                        post()

            def flush(self):
                while self.pend:
                    self._flush1()
                while self.deferred:
                    self._tick()

        cv = Carver()
        xg = [cv.t(BF16, 8, 512) for _ in range(2)]
        xg_r = [Res(), Res()]
        qkT = cv.t(BF16, 2, SEQ)
        dv_aug = cv.t(BF16, NT, 130)
        W_hs = [cv.t(BF16, 8, 384) for _ in range(2)]
        qk_tok = [cv.t(BF16, 256) for _ in range(2)]
        rtmp = [cv.t(F32, 4, 128) for _ in range(2)]
        pbufs = [cv.t(BF16, 512) for _ in range(6)]
        tb_ = [cv.t(F32, 2, 4, 128) for _ in range(2)]
        ya4 = [cv.t(F32, 4, 128) for _ in range(2)]
        yj = cv.t(F32, 128)
        yan4 = [cv.t(BF16, 4, 128) for _ in range(2)]
        sst = [cv.t(F32, 12) for _ in range(2)]
        rz = [cv.t(F32, 2, 4) for _ in range(2)]
        mhalf = cv.t(F32, 4)
        tb_r = [Res(), Res()]
        gcount = [0]
        yaT_st = [cv.t(BF16, 512) for _ in range(2)]
        R_qkT, R_dv = Res(), Res()
        R_Whs = [Res(), Res()]
        qk_tok_r = [Res(), Res()]
        rtmp_r = [Res(), Res()]
        pres = [Res() for _ in range(6)]
        t0_r, yj_r = Res(), Res()
        ya_r = [Res(), Res()]
        yan_r = [Res(), Res()]
        sst_r = [Res(), Res()]
        rz_r = [Res(), Res()]
        yaT_st_r = [Res(), Res()]
        R_yaT = Res("yaT_s")
        K.memset("dve", dv_aug[:, :, 128:129], 1.0, [R_dv])
        K.memset("dve", mhalf, -0.5, [R_const])

        def rope(eng, out_v, in_v, t, nh, tmp, tmp_r, in_res, out_res):
            shp = [128, nh, 32]
            cb = cos_t[:, t, :].unsqueeze(1).broadcast_to(shp)
            sbb = sin_t[:, t, :].unsqueeze(1).broadcast_to(shp)
            x1 = in_v[:, :, 0, :]
            x2 = in_v[:, :, 1, :]
            tv = [tmp[:, i, 0:nh * 32].rearrange("p (h d) -> p h d", h=nh) for i in range(4)]
            K.tt(eng, tv[0], x1, cb, ALU.mult, in_res + [R_const], [tmp_r])
            K.tt(eng, tv[1], x2, sbb, ALU.mult, in_res + [R_const], [tmp_r])
            K.tt(eng, tv[2], x2, cb, ALU.mult, in_res + [R_const], [tmp_r])
            K.tt(eng, tv[3], x1, sbb, ALU.mult, in_res + [R_const], [tmp_r])
            K.tt(eng, out_v[:, :, 0, :], tv[0], tv[1], ALU.subtract, [tmp_r], out_res)
            K.tt(eng, out_v[:, :, 1, :], tv[2], tv[3], ALU.add, [tmp_r], out_res)

        for hd in range(4):
            for hn in ([0, 1] if hd == 0 else [hd + 1]):
                if hn < 4:
                    for i, c0 in enumerate((C_DQ + hn * 128, C_DK + hn * 128, C_DV + hn * 128)):
                        K.dma("pool", W_hs[hn % 2][:, :, i * 128:(i + 1) * 128],
                              w_in_d[:, c0:c0 + 128].rearrange("(k p) c -> p k c", p=128), [], [R_Whs[hn % 2]])
            W_h = W_hs[hd % 2]
            R_Wh = R_Whs[hd % 2]
            def projA(t):
                g = t // 4
                gb = g % 2
                b = t % 2
                if t % 4 == 0:
                    K.dma("sp", xg[gb], xTn_s[g], [R_xTn], [xg_r[gb]])
                pbank = b
                for k in range(8):
                    K.mm(bk(pbank)[:, 0:384], xg[gb][:, k, (t % 4) * 128:(t % 4 + 1) * 128], W_h[:, k, :],
                         k == 0, k == 7, [xg_r[gb], R_Wh], [bres[pbank]])
                pv = bk(pbank)[:, 0:256].rearrange("p (h c d) -> p h c d", h=4, c=2)
                ov_ = qk_tok[b].rearrange("p (h c d) -> p h c d", h=4, c=2)
                rope("dve", ov_, pv, t, 4, rtmp[b], rtmp_r[b], [bres[pbank]], [qk_tok_r[b]])
                K.cp("act", dv_aug[:, t, 0:128], bk(pbank)[:, 256:384], [bres[pbank]], [R_dv])

            def projB(t):
                b = t % 2
                tb = 2 + b
                for i in range(2):
                    K.tr(bkb(tb)[:, i * 128:(i + 1) * 128], qk_tok[b][:, i * 128:(i + 1) * 128], ident_bf[:],
                         [qk_tok_r[b], R_const], [bres[tb]])
                K.cp("act", qkT[:, :, t * 128:(t + 1) * 128], bkb(tb)[:, 0:256].rearrange("p (a b) -> p a b", a=2),
                     [bres[tb]], [R_qkT])

            for tt_ in range(NT + 1):
                if tt_ < NT:
                    projA(tt_)
                if tt_ >= 1:
                    projB(tt_ - 1)
            if stop_after == "P2a":
                return _finish()
            pipe = AttnPipe([4, 5, 6, 7], pbufs, pres, look=2)
            for C in range(NG):
                nk = 4 * C + 4
                for kt in range(nk):
                    diag = kt - 4 * C
                    units = []
                    for s in range(2):
                        ob = (0, 1) if s == 0 else (2, 3)

                        def qk(bnk, kt=kt, s=s, C=C, diag=diag):
                            K.mm(bk(bnk), qkT[64 * s:64 * s + 64, 1, kt * 128:(kt + 1) * 128],
                                 qkT[64 * s:64 * s + 64, 0, C * 512:(C + 1) * 512], True, True, [R_qkT], [bres[bnk]])
                        if diag >= 0:
                            qk.mask01 = maskcw[:, diag, :]
                            qk.mask_cols = (128 * diag, 128 * diag + 128)
                            qk.exp_cols = (128 * diag, 512)

                        def pv(P, Pr, kt=kt, ob=ob, diag=diag, nk=nk):
                            for qs in range(4):
                                if diag >= 0 and qs < diag:
                                    continue
                                bnk = ob[qs // 2]
                                c0 = (qs % 2) * 129
                                first = (kt == 0 and qs % 2 == 0)
                                K.mm(bk(bnk)[:, c0:c0 + 129], P[:, qs * 128:(qs + 1) * 128], dv_aug[:, kt, 0:129],
                                     first, kt == nk - 1, [Pr, R_dv], [bres[bnk]])

                        post = None
                        if kt == nk - 1:
                            def post(s=s, C=C, ob=ob, hd=hd):
                                gi = gcount[0] % 2
                                for half in range(2):
                                    bnk = ob[half]
                                    K.recip(rz[gi][:, s, 2 * half:2 * half + 2], bk(bnk)[:, 128:258:129], [bres[bnk]], [rz_r[gi]])
                                for qs in range(4):
                                    bnk = ob[qs // 2]
                                    c0 = (qs % 2) * 129
                                    K.ts("dve", tb_[gi][:, s, qs, :], bk(bnk)[:, c0:c0 + 128], rz[gi][:, s, qs:qs + 1], None,
                                         ALU.mult, None, [bres[bnk], rz_r[gi]], [tb_r[gi]])
                                if s == 1:
                                    gcount[0] += 1

                                    def post_b1(gi=gi):
                                        for qs in range(4):
                                            K.stt(ya4[gi][:, qs, :], tb_[gi][:, 1, qs, :], neglam[:], tb_[gi][:, 0, qs, :], ALU.mult, ALU.add,
                                                  [tb_r[gi], R_const], [ya_r[gi]])
                                            K.stt(yj, ya4[gi][:, qs, :], 1.0, ya4[gi][:, qs, :], ALU.mult, ALU.mult, [ya_r[gi]], [yj_r],
                                                  accum=sst[gi][:, qs:qs + 1])
                                        K.ts("pool", sst[gi][:, 4:8], sst[gi][:, 0:4], 1.0 / 128, EPS, ALU.mult, ALU.add, [yj_r], [sst_r[gi]])
                                        K.tt("pool", sst[gi][:, 8:12], sst[gi][:, 4:8], mhalf[:, 0:4], ALU.pow, [sst_r[gi], R_const], [sst_r[gi]])
                                        for qs in range(4):
                                            K.stt(yan4[gi][:, qs, :], ya4[gi][:, qs, :], sst[gi][:, 8 + qs:9 + qs], g08[:], ALU.mult, ALU.mult,
                                                  [ya_r[gi], sst_r[gi], R_const], [yan_r[gi]])

                                    def post_b2(gi=gi, C=C, hd=hd):
                                        for qs in range(4):
                                            K.tr(bkb(7)[:, qs * 128:(qs + 1) * 128], yan4[gi][:, qs, :], ident_bf[:], [yan_r[gi], R_const], [bres[7]])
                                        K.cp("dve", yaT_st[gi], bkb(7)[:, 0:512], [bres[7]], [yaT_st_r[gi]])
                                        K.dma("sp", yaT_s[C, :, hd, :], yaT_st[gi], [yaT_st_r[gi]], [R_yaT])
                                    pipe.defer(post_b1, 2)
                                    pipe.defer(post_b2, 5)
                        units.append((qk, 128, pv, post))
                    pipe.step(units)
            pipe.flush()
        S.barrier()
        if stop_after == "P2":
            return _finish()

        cv = Carver()
        nqT = cv.t(BF16, 4, SEQ)
        KT = cv.t(BF16, 4, SEQ)
        vs_aug = cv.t(BF16, NT, 2, 66)
        vw_aug = cv.t(BF16, NT, 2, 66)
        R_nqT, R_KT, R_vs, R_vw = Res(), Res(), Res(), Res()
        mark = cv.off
        xg = [cv.t(BF16, 8, 512) for _ in range(2)]
        xg_r = [Res(), Res()]
        Wn = cv.t(BF16, 8, 1304)
        R_Wn = Res()
        nq_tok = [cv.t(BF16, 512) for _ in range(2)]
        k_tok = [cv.t(BF16, 4, 128) for _ in range(2)]
        rtmpn = [cv.t(F32, 4, 256) for _ in range(2)]
        rtmpk = [cv.t(F32, 4, 192) for _ in range(2)]
        rtmpk_r = [Res(), Res()]
        gtmp = [cv.t(F32, 24) for _ in range(2)]
        nq_tok_r, k_tok_r, rtmpn_r, gtmp_r = [Res(), Res()], [Res(), Res()], [Res(), Res()], [Res(), Res()]
        for (d0, n_, c0) in ((0, 640, 1536), (640, 128, 2304), (768, 128, 2560), (896, 128, 2176), (1024, 128, 2432), (1152, 152, 2688)):
            K.dma("pool", Wn[:, :, d0:d0 + n_], w_in_d[:, c0:c0 + n_].rearrange("(k p) c -> p k c", p=128), [], [R_Wn])
        K.memset("dve", vs_aug[:, :, :, 64:65], 1.0, [R_vs])
        K.memset("dve", vw_aug[:, :, :, 64:65], 1.0, [R_vw])
        def p3A(t):
            g = t // 4
            gb = g % 2
            b = t % 2
            if t % 4 == 0:
                K.dma("sp", xg[gb], xTn_s[g], [R_xTn], [xg_r[gb]])
            pb3 = (0, 1, 2) if b == 0 else (3, 4, 5)
            for bi, (c0, cn) in enumerate(((0, 512), (512, 512), (1024, 280))):
                for k in range(8):
                    K.mm(bk(pb3[bi])[:, 0:cn], xg[gb][:, k, (t % 4) * 128:(t % 4 + 1) * 128], Wn[:, k, c0:c0 + cn],
                         k == 0, k == 7, [xg_r[gb], R_Wn], [bres[pb3[bi]]])
            A, Bk, Ck = pb3
            shp = [128, 2, 4, 32]
            cb4 = cos_t[:, t, :].unsqueeze(1).unsqueeze(1).broadcast_to(shp)
            sb4 = sin_t[:, t, :].unsqueeze(1).unsqueeze(1).broadcast_to(shp)
            pin = bk(A).rearrange("p (g j c d) -> p g j c d", g=2, j=4, c=2)
            pout = nq_tok[b].rearrange("p (j g c d) -> p g j c d", j=4, g=2, c=2)
            tv = [rtmpn[b][:, i, 0:256].rearrange("p (g j d) -> p g j d", g=2, j=4) for i in range(4)]
            x1, x2 = pin[:, :, :, 0, :], pin[:, :, :, 1, :]
            rr = [bres[A], R_const]
            K.tt("dve", tv[0], x1, cb4, ALU.mult, rr, [rtmpn_r[b]])
            K.tt("dve", tv[1], x2, sb4, ALU.mult, rr, [rtmpn_r[b]])
            K.tt("dve", tv[2], x2, cb4, ALU.mult, rr, [rtmpn_r[b]])
            K.tt("dve", tv[3], x1, sb4, ALU.mult, rr, [rtmpn_r[b]])
            K.tt("dve", pout[:, :, :, 0, :], tv[0], tv[1], ALU.subtract, [rtmpn_r[b]], [nq_tok_r[b]])
            K.tt("dve", pout[:, :, :, 1, :], tv[2], tv[3], ALU.add, [rtmpn_r[b]], [nq_tok_r[b]])
            kin = bk(Bk)[:, 0:384].rearrange("p (h c d) -> p h c d", h=6, c=2)
            rope("dve", k_tok[b][:, 0:3, :].rearrange("p s (h c d) -> p (s h) c d", h=2, c=2), kin, t, 6,
                 rtmpk[b], rtmpk_r[b], [bres[Bk]], [k_tok_r[b]])
            K.cp("act", k_tok[b][:, 3, :], bk(Bk)[:, 384:512], [bres[Bk]], [k_tok_r[b]])
            K.cp("act", vs_aug[:, t, :, 0:64], bk(Ck)[:, 0:128].rearrange("p (g d) -> p g d", g=2), [bres[Ck]], [R_vs])
            K.cp("act", vw_aug[:, t, :, 0:64], bk(Ck)[:, 128:256].rearrange("p (g d) -> p g d", g=2), [bres[Ck]], [R_vw])
            K.act(gtmp[b], bk(Ck)[:, 256:280], AF.Exp, [bres[Ck]], [gtmp_r[b]], scale=-1.0)
            K.ts("dve", gtmp[b], gtmp[b], 1.0, None, ALU.add, None, [gtmp_r[b]], [gtmp_r[b]])
            K.recip(gate[:, t, :], gtmp[b], [gtmp_r[b]], [R_gate])

        def p3B(t):
            b = t % 2
            tb = 6 + b
            for i in range(4):
                K.tr(bkb(tb)[:, i * 128:(i + 1) * 128], nq_tok[b][:, i * 128:(i + 1) * 128], ident_bf[:],
                     [nq_tok_r[b], R_const], [bres[tb]])
            for i in range(4):
                K.tr(bkb(tb)[:, 512 + i * 128:512 + (i + 1) * 128], k_tok[b][:, i, :], ident_bf[:],
                     [k_tok_r[b], R_const], [bres[tb]])
            K.cp("act", nqT[:, :, t * 128:(t + 1) * 128], bkb(tb)[:, 0:512].rearrange("p (a b) -> p a b", a=4), [bres[tb]], [R_nqT])
            K.cp("act", KT[:, :, t * 128:(t + 1) * 128], bkb(tb)[:, 512:1024].rearrange("p (a b) -> p a b", a=4), [bres[tb]], [R_KT])

        for tt_ in range(NT + 1):
            if tt_ < NT:
                p3A(tt_)
            if tt_ >= 1:
                p3B(tt_ - 1)
        S.barrier()
        if stop_after == "P3a":
            return _finish()

        cv.off = mark
        kcmpT = cv.t(BF16, 256)
        vc_aug = cv.t(BF16, 2, 2, 130)
        mark = cv.off
        w1 = [cv.t(BF16, 32, 256) for _ in range(2)]
        w2kd = cv.t(BF16, 2, 128)
        w2v = cv.t(BF16, 2, 64)
        peT = cv.t(BF16, 2, 32)
        cb_sb = cv.t(F32, 2, 2)
        hid_sb = cv.t(BF16, 2, 2, 2, 256)
        R_w1, R_w2, R_pe, R_cb, R_hid, R_kcmp, R_vca = Res(), Res(), Res(), Res(), Res(), Res(), Res()
        for kv, wd_ in enumerate((w1k_d, w1v_d)):
            for half in range(2):
                K.dma("pool", w1[kv][64 * half:64 * half + 64], wd_.rearrange("(l d) h -> d l h", d=64), [], [R_w1])
        for half in range(2):
            K.dma("pool", w2kd[:, :, 64 * half:64 * half + 64], w2k_d.rearrange("(c p) d -> p c d", p=128), [], [R_w2])
        K.dma("pool", w2v, w2v_d.rearrange("(c p) d -> p c d", p=128), [], [R_w2])
        with nc.allow_non_contiguous_dma(reason="tiny pe transpose load"):
            for kv, pd in enumerate((pek_d, pev_d)):
                for half in range(2):
                    K.dma("pool", peT[64 * half:64 * half + 64, kv, :], pd.rearrange("l d -> d l"), [], [R_pe])
        K.memset("dve", vc_aug[:, :, :, 64:65], 1.0, [R_vca])
        for g in range(2):
            K.dma("pool", vc_aug[:, g, :, 66:130], ov_d, [], [R_vca])
        for kv in range(2):
            for hc in range(2):
                col = kv * 2 + hc
                for l in range(32):
                    K.mm(bk(0)[:, col:col + 1], w1[kv][0:64, l, hc * 128:(hc + 1) * 128], peT[0:64, kv, l:l + 1],
                         l == 0 and col == 0, l == 31, [R_w1, R_pe], [bres[0]])
        K.cp("dve", cb_sb.rearrange("p a b -> p (a b)"), bk(0)[:, 0:4], [bres[0]], [R_cb])
        NCB = (SEQ - 32) // 16 + 1
        ui = 0
        for kv in range(2):
            for hc in range(2):
                bb = (1, 2) if ui % 2 == 0 else (3, 6)
                ui += 1
                for l in range(32):
                    for g in range(2):
                        K.mm(bk(bb[g])[:, 0:NCB], w1[kv][64 * g:64 * g + 64, l, hc * 128:(hc + 1) * 128],
                             KT[64 * g:64 * g + 64, 3 * kv, l:l + 16 * (NCB - 1) + 1:16], l == 0, l == 31, [R_w1, R_KT], [bres[bb[g]]])
                for g in range(2):
                    K.act(hid_sb[:, kv, g, hc, 0:NCB], bk(bb[g])[:, 0:NCB], AF.Silu, [bres[bb[g]], R_cb], [R_hid],
                          bias=cb_sb[:, kv, hc:hc + 1])
        for g in range(2):
            for hc in range(2):
                K.mm(bk(4)[:, 0:NCB], w2kd[:, hc, :], hid_sb[:, 0, g, hc, 0:NCB], hc == 0, hc == 1, [R_w2, R_hid], [bres[4]])
            K.cp("dve", kcmpT[64 * g:64 * g + 64, 0:NCB], bk(4)[64 * g:64 * g + 64, 0:NCB], [bres[4]], [R_kcmp])
            for ct in range(2):
                n = min(128, NCB - ct * 128)
                if n <= 0:
                    continue
                for hc in range(2):
                    K.mm(bk(5)[0:n, ct * 64:ct * 64 + 64], hid_sb[:, 1, g, hc, ct * 128:ct * 128 + n], w2v[:, hc, :],
                         hc == 0 and ct == 0, hc == 1, [R_hid, R_w2], [bres[5]])
            for ct in range(2):
                n = min(128, NCB - ct * 128)
                if n <= 0:
                    continue
                K.cp("dve", vc_aug[0:n, g, ct, 0:64], bk(5)[0:n, ct * 64:ct * 64 + 64], [bres[5]], [R_vca])
        S.barrier()
        if stop_after == "P3b":
            return _finish()

        cv.off = mark
        E_sb = cv.t(BF16, NT, 128)
        pbufs = [cv.t(BF16, 512) for _ in range(6)]
        pres = [Res() for _ in range(6)]
        acc = cv.t(F32, 4, 512)
        imp = cv.t(F32, 4, 2, 64)
        imp2 = cv.t(F32, 64)
        selb = cv.t(F32, 64)
        selb_bf = cv.t(BF16, 4, 2, 64)
        R_selbf = Res()
        m8 = cv.t(F32, 16)
        selbT = cv.t(BF16, 512)
        btab = [cv.t(F32, 4, 64) for _ in range(2)]
        cmk = [cv.t(BF16, 2, 512) for _ in range(2)]
        rzn = [cv.t(F32, 8) for _ in range(2)]
        yb_bf = cv.t(BF16, 4, 512)
        ybT_st = [cv.t(BF16, 4, 512) for _ in range(2)]
        R_E, R_acc, R_imp, R_sel, R_selbT, R_ybbf = Res(), Res(), Res(), Res(), Res(), Res()
        btab_r, cmk_r, rzn_r, ybT_st_r = [Res(), Res()], [Res(), Res()], [Res(), Res()], [Res(), Res()]
        R_ybT = Res("ybT_s")
        K.dma("pool", E_sb[0:64], E_d, [], [R_E])
        K.dma("pool", E_sb[64:128], E_d, [], [R_E])
        hh_list = [(hh, hh // 4, hh % 4) for hh in range(8)]
        ecount = [0]

        def evac_branch(bnk, ncols, hh, br, C, with_imp):
            e = ecount[0] % 2
            ecount[0] += 1
            Ov = bk(bnk)[:, 0:4 * ncols].rearrange("p (a b) -> p a b", a=4)
            g = hh // 4
            K.ts("dve", rzn[e][:, 0:4], Ov[:, :, 64], 1e-30, None, ALU.add, None, [bres[bnk]], [rzn_r[e]])
            K.recip(rzn[e][:, 0:4], rzn[e][:, 0:4], [rzn_r[e]], [rzn_r[e]])
            K.tt("dve", rzn[e][:, 4:8], rzn[e][:, 0:4], gate[:, 4 * C:4 * C + 4, br * 8 + hh], ALU.mult,
                 [rzn_r[e], R_gate], [rzn_r[e]])
            for qs in range(4):
                dst = acc[:, qs, hh * 64:(hh + 1) * 64]
                if br == 0:
                    K.ts("dve", dst, Ov[:, qs, 0:64], rzn[e][:, 4 + qs:5 + qs], None, ALU.mult, None,
                         [bres[bnk], rzn_r[e]], [R_acc])
                else:
                    K.stt(dst, Ov[:, qs, 0:64], rzn[e][:, 4 + qs:5 + qs], dst, ALU.mult, ALU.add,
                          [bres[bnk], rzn_r[e], R_acc], [R_acc])
                if with_imp:
                    K.stt(imp[:, qs, g, :], Ov[:, qs, 65:129], rzn[e][:, qs:qs + 1], imp[:, qs, g, :], ALU.mult, ALU.add,
                          [bres[bnk], rzn_r[e], R_imp], [R_imp])

        pipe = AttnPipe([0, 1, 2, 3], pbufs, pres, look=2)
        obank_i = [0]
        pairs = [((j, 0, j), (4 + j, 1, j)) for j in range(4)]
        for C in range(NG):
            cbi = C % 2
            for Cn in ([0, 1] if C == 0 else [C + 1]):
                if Cn < NG:
                    K.dma("sp", btab[Cn % 2], btab_d[Cn * 512:(Cn + 1) * 512, :].rearrange("(a p) n -> p a n", p=128), [], [btab_r[Cn % 2]])
                    K.dma("pool", cmk[Cn % 2], cmask_d[:, :, Cn * 512:(Cn + 1) * 512], [], [cmk_r[Cn % 2]])
            pipe.flush()
            for g in range(2):
                K.cp("dve", imp[:, :, g, :], btab[cbi], [btab_r[cbi]], [R_imp])
            ncts = [ct for ct in range(2) if (ct == 0 or 32 * C + 30 >= 128) and NCB - ct * 128 > 0]
            for pair in pairs:
                for ui_, ct in enumerate(ncts):
                    n = min(128, NCB - ct * 128)
                    units = []
                    for pi_, (hh, g, j) in enumerate(pair):
                        oa = 4 + 2 * pi_
                        ib = oa + 1

                        def qk(bnk, ct=ct, n=n, g=g, j=j, C=C, cbi=cbi):
                            K.mm(bk(bnk)[0:n, :], kcmpT[64 * g:64 * g + 64, ct * 128:ct * 128 + n],
                                 nqT[64 * g:64 * g + 64, j, C * 512:(C + 1) * 512], True, False, [R_kcmp, R_nqT], [bres[bnk]])
                            K.mm(bk(bnk)[0:n, :], ident_bf[:, 0:n], cmk[cbi][:, ct, :], False, True, [R_const, cmk_r[cbi]], [bres[bnk]])

                        def pv(P, Pr, ct=ct, n=n, g=g, oa=oa, ib=ib, first=(ui_ == 0), last=(ui_ == len(ncts) - 1)):
                            for qs in range(4):
                                K.mm(bk(oa)[:, qs * 65:qs * 65 + 65], P[0:n, qs * 128:(qs + 1) * 128], vc_aug[0:n, g, ct, 0:65],
                                     first and qs == 0, last, [Pr, R_vca], [bres[oa]])
                                K.mm(bk(ib)[:, qs * 64:qs * 64 + 64], P[0:n, qs * 128:(qs + 1) * 128], vc_aug[0:n, g, ct, 66:130],
                                     first and qs == 0, last, [Pr, R_vca], [bres[ib]])
                        post = None
                        if ui_ == len(ncts) - 1:
                            def post(oa=oa, ib=ib, hh=hh, C=C, g=g):
                                e = ecount[0] % 2
                                evac_branch(oa, 65, hh, 0, C, False)
                                Iv = bk(ib)[:, 0:256].rearrange("p (a b) -> p a b", a=4)
                                for qs in range(4):
                                    K.stt(imp[:, qs, g, :], Iv[:, qs, :], rzn[e][:, qs:qs + 1], imp[:, qs, g, :], ALU.mult, ALU.add,
                                          [bres[ib], rzn_r[e], R_imp], [R_imp])
                        units.append((qk, n, pv, post))
                    pipe.step(units)
            pipe.flush()
            for qs in range(4):
                for g in range(2):
                    iv = imp[:, qs, g, :]
                    K.S.op("dve", lambda h, iv=iv: h.max(out=m8[:, 0:8], in_=iv), reads=[R_imp], writes=[R_sel])
                    K.S.op("dve", lambda h, iv=iv: h.match_replace(out=imp2, in_to_replace=m8[:, 0:8], in_values=iv, imm_value=-1e9),
                           reads=[R_imp, R_sel], writes=[R_sel])
                    K.S.op("dve", lambda h: h.max(out=m8[:, 8:16], in_=imp2), reads=[R_sel], writes=[R_sel])
                    K.ts("dve", selb, iv, m8[:, 15:16], None, ALU.is_ge, None, [R_imp, R_sel], [R_sel])
                    K.ts("dve", selb_bf[:, qs, g, :], selb, -1.0, -NEG, ALU.add, ALU.mult, [R_sel], [R_selbf])

            def sel_transposes():
                for qs in range(4):
                    K.tr(bkb(0)[:, qs * 128:(qs + 1) * 128], selb_bf[:, qs].rearrange("p g n -> p (g n)"), ident_bf[:],
                         [R_selbf, R_const], [bres[0]])
                K.cp("dve", selbT, bkb(0)[:, 0:512], [bres[0]], [R_selbT])
            for br in (2, 1):
                if br == 1:
                    pipe.flush()
                    sel_transposes()
                    kts = list(range(4 * C + 4))
                else:
                    kts = [kt for kt in range(4 * C - 4, 4 * C + 4) if kt >= 0]
                Vt = vs_aug if br == 1 else vw_aug
                Rv = R_vs if br == 1 else R_vw
                slot = 1 if br == 1 else 2
                for pair in pairs:
                    ob0 = 4 + 2 * (obank_i[0] % 2)
                    obank_i[0] += 1
                    for ui_, kt in enumerate(kts):
                        off = kt - 4 * C
                        units = []
                        for pi_, (hh, g, j) in enumerate(pair):
                            oa = ob0 + pi_

                            def qk(bnk, kt=kt, off=off, g=g, j=j, C=C, br=br, slot=slot):
                                K.mm(bk(bnk), KT[64 * g:64 * g + 64, slot, kt * 128:(kt + 1) * 128],
                                     nqT[64 * g:64 * g + 64, j, C * 512:(C + 1) * 512], True, br == 2, [R_KT, R_nqT], [bres[bnk]])
                                if br == 1:
                                    K.mm(bk(bnk), E_sb[64 * g:64 * g + 64, kt, :], selbT[64 * g:64 * g + 64, :], False, True,
                                         [R_E, R_selbT], [bres[bnk]])
                            qlo, qhi = 0, 4
                            if off >= 0:
                                qk.mask01 = maskcw[:, off, :]
                                qk.mask_cols = (128 * off, 128 * off + 128)
                                qk.exp_cols = (128 * off, 512)
                                qlo = off
                            elif br == 2:
                                jw = off + 4
                                qk.mask01 = maskcw[:, 4 + jw, :]
                                qk.mask_cols = (128 * jw, 128 * jw + 128)
                                qk.exp_cols = (0, 128 * jw + 128)
                                qhi = jw + 1

                            def pv(P, Pr, kt=kt, g=g, oa=oa, Vt=Vt, Rv=Rv, first=(ui_ == 0), last=(ui_ == len(kts) - 1), qlo=qlo, qhi=qhi):
                                for qs in range(qlo, qhi):
                                    K.mm(bk(oa)[:, qs * 65:qs * 65 + 65], P[:, qs * 128:(qs + 1) * 128], Vt[:, kt, g, 0:65],
                                         first and qs == 0, last, [Pr, Rv], [bres[oa]])
                            post = None
                            if ui_ == len(kts) - 1:
                                def post(oa=oa, hh=hh, C=C, br=br):
                                    evac_branch(oa, 65, hh, br, C, False)
                            units.append((qk, 128, pv, post))
                        pipe.step(units)
            pipe.flush()
            K.cp("dve", yb_bf, acc, [R_acc], [R_ybbf])
            for qs in range(4):
                for fc in range(4):
                    K.tr(bkb(1)[:, fc * 128:(fc + 1) * 128], yb_bf[:, qs, fc * 128:(fc + 1) * 128], ident_bf[:],
                         [R_ybbf, R_const], [bres[1]])
                K.cp("dve", ybT_st[cbi][:, :, qs * 128:(qs + 1) * 128], bkb(1)[:, 0:512].rearrange("p (a b) -> p a b", a=4),
                     [bres[1]], [ybT_st_r[cbi]])
            K.dma("pool", ybT_s[C], ybT_st[cbi], [ybT_st_r[cbi]], [R_ybT])
        S.barrier()
        if stop_after == "P3c":
            return _finish()

        cv = Carver()
        Wmg = cv.t(BF16, 8, 2048)
        Wa = cv.t(BF16, 4, D)
        Wb = cv.t(BF16, 4, D)
        Wo = cv.t(BF16, 8, D)
        xg = [cv.t(BF16, 8, 512) for _ in range(2)]
        yag = [cv.t(BF16, 4, 512) for _ in range(2)]
        ybg = [cv.t(BF16, 4, 512) for _ in range(2)]
        xt = [cv.t(F32, D) for _ in range(2)]
        mT = cv.t(BF16, 8, 512)
        sg = [cv.t(F32, 2, 512) for _ in range(2)]
        mm_ = [cv.t(F32, 2, 512) for _ in range(2)]
        ho = [cv.t(F32, D) for _ in range(2)]
        R_W4, R_mT, R_h = Res(), Res(), Res("h_s")
        xg_r, yag_r, ybg_r, xt_r, sg_r, mm_r, ho_r = ([Res(), Res()] for _ in range(7))
        R_Wmg = [Res() for _ in range(4)]
        R_Wab = [Res(), Res()]
        R_Wo = Res()

        def ld_mg(cp):
            for i in range(2):
                c0 = C_MG + i * 1024 + cp * 256
                K.dma("pool", Wmg[:, :, i * 1024 + cp * 256:i * 1024 + cp * 256 + 256],
                      w_in_d[:, c0:c0 + 256].rearrange("(k p) c -> p k c", p=128), [], [R_Wmg[cp]])

        def ld_ab(hf):
            K.dma("pool", Wa[:, :, hf * 512:(hf + 1) * 512], wa_d[:, hf * 512:(hf + 1) * 512].rearrange("(k p) c -> p k c", p=128), [], [R_Wab[hf]])
            K.dma("pool", Wb[:, :, hf * 512:(hf + 1) * 512], wb_d[:, hf * 512:(hf + 1) * 512].rearrange("(k p) c -> p k c", p=128), [], [R_Wab[hf]])

        ld_mg(0)
        ld_ab(0)
        ld_mg(1)
        ld_mg(2)
        ld_ab(1)
        ld_mg(3)
        for kq in range(2):
            K.dma("pool", Wo[:, 4 * kq:4 * kq + 4, :], wout_d[kq * 512:(kq + 1) * 512, :].rearrange("(k p) c -> p k c", p=128), [], [R_Wo])
        for G in range(NG):
            gb = G % 2
            for Gn in ([0, 1] if G == 0 else [G + 1]):
                if Gn < NG:
                    K.dma("sp", xg[Gn % 2], xTn_s[Gn], [R_xTn], [xg_r[Gn % 2]])
                    K.dma("sp", yag[Gn % 2], yaT_s[Gn], [R_yaT], [yag_r[Gn % 2]])
                    K.dma("sp", ybg[Gn % 2], ybT_s[Gn], [R_ybT], [ybg_r[Gn % 2]])
            for dc in range(8):
                e = dc % 2
                pb4 = (0, 1, 2, 3) if e == 0 else (4, 5, 6, 7)
                for i in range(2):
                    for k in range(8):
                        K.mm(bk(pb4[i]), Wmg[:, k, i * 1024 + dc * 128:i * 1024 + (dc + 1) * 128], xg[gb][:, k, :],
                             k == 0, k == 7, [R_Wmg[dc // 2], xg_r[gb]], [bres[pb4[i]]])
                for i, (Wx, yg, yr) in enumerate(((Wa, yag, yag_r), (Wb, ybg, ybg_r))):
                    for k in range(4):
                        K.mm(bk(pb4[2 + i]), Wx[:, k, dc * 128:(dc + 1) * 128], yg[gb][:, k, :], k == 0, k == 3,
                             [R_Wab[dc // 4], yr[gb]], [bres[pb4[2 + i]]])
                for i in range(2):
                    K.act(sg[e][:, i, :], bk(pb4[i]), AF.Sigmoid, [bres[pb4[i]]], [sg_r[e]])
                for i in range(2):
                    K.tt("dve", mm_[e][:, i, :], sg[e][:, i, :], bk(pb4[2 + i]), ALU.mult, [sg_r[e], bres[pb4[2 + i]]], [mm_r[e]])
                K.tt("dve", mT[:, dc, :], mm_[e][:, 0, :], mm_[e][:, 1, :], ALU.add, [mm_r[e]], [R_mT])
            for qs in range(4):
                t = G * 4 + qs
                b = qs % 2
                K.dma("sp", xt[b], x_d[t * 128:(t + 1) * 128, :], [], [xt_r[b]])
                for n2 in range(2):
                    bnk = 2 * b + n2
                    for dc in range(8):
                        K.mm(bk(bnk), mT[:, dc, qs * 128:(qs + 1) * 128], Wo[:, dc, n2 * 512:(n2 + 1) * 512], dc == 0, dc == 7,
                             [R_mT, R_Wo], [bres[bnk]])
                    K.tt("dve", ho[b][:, n2 * 512:(n2 + 1) * 512], bk(bnk), xt[b][:, n2 * 512:(n2 + 1) * 512], ALU.add,
                         [bres[bnk], xt_r[b]], [ho_r[b]])
                K.dma("pool", h_s[t * 128:(t + 1) * 128, :], ho[b], [ho_r[b]], [R_h])
        S.barrier()
        if stop_after == "P4":
            return _finish()

        cv = Carver()
        Wd = cv.t(BF16, 22, D)
        hgs = [cv.t(F32, 4, D) for _ in range(2)]
        xs = [cv.t(F32, D) for _ in range(2)]
        junk = cv.t(F32, D)
        hT = cv.t(BF16, 8, 512)
        actT = cv.t(BF16, 22, 512)
        wr = [cv.t(BF16, 2, 8, 128) for _ in range(4)]
        sgl = [cv.t(F32, 512) for _ in range(2)]
        st1 = [cv.t(F32, 4) for _ in range(2)]
        yo = [cv.t(F32, D) for _ in range(2)]
        R_Wd, R_hT, R_act, R_y = Res(), Res(), Res(), Res("y")
        R_hgs = [Res(), Res()]
        xs_r, wr_r, sgl_r, st1_r, yo_r = [Res(), Res()], [Res() for _ in range(4)], [Res(), Res()], [Res(), Res()], [Res(), Res()]
        junk_r = Res()
        for kq in range(2):
            K.dma("pool", Wd[:, 11 * kq:11 * kq + 11, :], wd_d[kq * 1408:(kq + 1) * 1408, :].rearrange("(k p) c -> p k c", p=128), [], [R_Wd])
        wi = 0
        for G in range(NG):
            for Gn in ([0, 1] if G == 0 else [G + 1]):
                if Gn < NG:
                    K.dma("sp", hgs[Gn % 2], h_s[Gn * 512:(Gn + 1) * 512, :].rearrange("(a p) c -> p a c", p=128), [R_h], [R_hgs[Gn % 2]])
            hg = hgs[G % 2]
            R_hg = R_hgs[G % 2]
            for qs in range(4):
                b = qs % 2
                nt_A(None, qs, b, hg[:, qs, :], R_hg, load=False)
                nt_B(qs, b, g_ffn, hT, R_hT, (2 * b, 2 * b + 1))
            for j in range(22):
                w = wi % 4
                wi += 1
                K.dma("sp", wr[w], wgu_s[j], [R_wgu], [wr_r[w]])
                e = j % 2
                bg, bu = (4, 5) if e == 0 else (6, 7)
                for i, bnk in enumerate((bg, bu)):
                    for k in range(8):
                        K.mm(bk(bnk), wr[w][:, i, k, :], hT[:, k, :], k == 0, k == 7, [wr_r[w], R_hT], [bres[bnk]])
                K.act(sgl[e], bk(bg), AF.Silu, [bres[bg]], [sgl_r[e]])
                K.tt("dve", actT[:, j, :], sgl[e], bk(bu), ALU.mult, [sgl_r[e], bres[bu]], [R_act])
            for qs in range(4):
                t = G * 4 + qs
                b = qs % 2
                for n2 in range(2):
                    bnk = 2 * b + n2
                    for j in range(22):
                        K.mm(bk(bnk), actT[:, j, qs * 128:(qs + 1) * 128], Wd[:, j, n2 * 512:(n2 + 1) * 512], j == 0, j == 21,
                             [R_act, R_Wd], [bres[bnk]])
                    K.tt("dve", yo[b][:, n2 * 512:(n2 + 1) * 512], bk(bnk), hg[:, qs, n2 * 512:(n2 + 1) * 512], ALU.add,
                         [bres[bnk], R_hg], [yo_r[b]])
                K.act(junk, yo[b], AF.Square, [yo_r[b]], [junk_r], accum=st1[b][:, 0:1])
                K.act(st1[b][:, 1:2], st1[b][:, 0:1], AF.Ln, [junk_r, R_const], [st1_r[b]], scale=1.0 / D, bias=eps_ap)
                K.act(st1[b][:, 2:3], st1[b][:, 1:2], AF.Exp, [st1_r[b]], [st1_r[b]], scale=-0.5)
                K.stt(yo[b], yo[b], st1[b][:, 2:3], gfin_b[:], ALU.mult, ALU.mult, [yo_r[b], st1_r[b], R_const], [yo_r[b]])
                K.dma("pool", y_d[t * 128:(t + 1) * 128, :], yo[b], [yo_r[b]], [R_y])
        return _finish()


_PROG = {}


def kernel(**inputs):
    x = np.ascontiguousarray(np.asarray(inputs["x"], dtype=np.float32))
    B, SEQ, _ = x.shape
    pos = np.asarray(inputs["positions"]).astype(np.int32)
    f = lambda k, shp: np.ascontiguousarray(np.asarray(inputs[k], dtype=np.float32).reshape(shp))
    common = {
        "attn_norm_g": f("attn_norm_g", (1, D)), "w_in": f("w_in", (D, INDIM)), "diff_lambda": f("diff_lambda", (1, 256)),
        "diff_subln_g": f("diff_subln_g", (1, 128)), "cmp_pe_k": f("cmp_pe_k", (32, 64)), "cmp_pe_v": f("cmp_pe_v", (32, 64)),
        "cmp_k_w1": f("cmp_k_w1", (2048, 256)), "cmp_k_w2": f("cmp_k_w2", (256, 64)),
        "cmp_v_w1": f("cmp_v_w1", (2048, 256)), "cmp_v_w2": f("cmp_v_w2", (256, 64)),
        "w_branch_a": f("w_branch_a", (512, D)), "w_branch_b": f("w_branch_b", (512, D)), "w_out": f("w_out", (D, D)),
        "ffn_norm_g": f("ffn_norm_g", (1, D)), "w_gate": f("w_gate", (D, DFF)), "w_up": f("w_up", (D, DFF)),
        "w_down": f("w_down", (DFF, D)), "final_norm_g": f("final_norm_g", (1, D)),
    }
    common.update(host_consts(SEQ))
    if SEQ not in _PROG:
        _PROG[SEQ] = build_program(SEQ)
    nc = _PROG[SEQ]
    in_maps = [dict(common, x=x[b], pos=np.ascontiguousarray(pos[b].reshape(SEQ, 1))) for b in range(B)]
    res = run_bass_kernel_spmd(nc, in_maps, core_ids=list(range(B)))
    return np.stack([np.asarray(r["y"], dtype=np.float32) for r in res.results], axis=0)
```

```python
import math
import numpy as np
import concourse.bass as bass
import concourse.mybir as mybir
from concourse.bass_utils import run_bass_kernel_spmd

F32 = mybir.dt.float32
BF16 = mybir.dt.bfloat16
I32 = mybir.dt.int32
AF = mybir.ActivationFunctionType
ALU = mybir.AluOpType
AX = mybir.AxisListType


class Res:
    __slots__ = ("w", "r", "name", "excl")

    def __init__(self, name="", excl=False):
        self.w = None
        self.r = {}
        self.name = name
        self.excl = excl


class Sched:
    ENG = ("pe", "dve", "act", "pool", "sp")
    NDSEM = 12

    def __init__(self, nc, stack):
        self.nc = nc
        self.h = {"pe": nc.tensor, "dve": nc.vector, "act": nc.scalar, "pool": nc.gpsimd, "sp": nc.sync}
        self.sems = {}
        self.cnt = {}
        self.ops = {e: [] for e in self.ENG}
        self.waited = {e: {} for e in self.ENG}
        for e in self.ENG:
            self.sems[e] = stack.enter_context(nc.semaphore("s_" + e))
            self.cnt[e] = 0
        self.dsem = {}
        self.dsem_cnt = {}
        self.dsem_rr = {}
        for q in ("sp", "pool", "act"):
            self.dsem[q] = [stack.enter_context(nc.semaphore(f"d_{q}{i}")) for i in range(self.NDSEM)]
            self.dsem_cnt[q] = [0] * self.NDSEM
            self.dsem_rr[q] = 0
        self.nops = 0
        self.bar = {e: [] for e in self.ENG}

    def _collect(self, eng, reads, writes, extra=()):
        own = ("e", eng)
        waits = {}

        def need(tok, allow_own):
            if tok is None:
                return
            k, v = tok
            if k == own and not allow_own:
                return
            if waits.get(k, 0) < v:
                waits[k] = v

        for r in reads:
            need(r.w, True)
        own_ok = (eng != "pe")
        for r in writes:
            need(r.w, own_ok)
            for k, v in r.r.items():
                need((k, v), own_ok)
        for t in extra:
            need(t, True)
        for t in self.bar[eng]:
            need(t, True)
        self.bar[eng] = []
        wd = self.waited[eng]
        out = []
        for k, v in waits.items():
            if wd.get(k, 0) >= v:
                continue
            wd[k] = v
            out.append((k, v))
        return out

    def _mark(self, tok, reads, writes):
        k, v = tok
        for r in reads:
            if r.r.get(k, 0) < v:
                r.r[k] = v
        for r in writes:
            r.w = tok
            r.r = {}

    def op(self, eng, fn, reads=(), writes=()):
        if any(r.excl for r in reads):
            writes = list(writes) + [r for r in reads if r.excl]
            reads = [r for r in reads if not r.excl]
        waits = self._collect(eng, reads, writes)
        self.cnt[eng] += 1
        tok = (("e", eng), self.cnt[eng])
        self.ops[eng].append((waits, fn, ("e", eng), 1))
        self._mark(tok, reads, writes)
        self.nops += 1
        return tok

    def dma(self, q, fn, reads=(), writes=()):
        i = self.dsem_rr[q]
        self.dsem_rr[q] = (i + 1) % self.NDSEM
        key = ("d", q, i)
        prev = self.dsem_cnt[q][i]
        extra = [(key, prev)] if prev > 0 else []
        waits = self._collect(q, reads, writes, extra)
        self.dsem_cnt[q][i] = prev + 16
        tok = (key, prev + 16)
        self.ops[q].append((waits, fn, key, 16))
        self._mark(tok, reads, writes)
        self.nops += 1
        return tok

    def all_tokens(self):
        toks = [(("e", e), self.cnt[e]) for e in self.ENG if self.cnt[e] > 0]
        for q in self.dsem_cnt:
            for i, v in enumerate(self.dsem_cnt[q]):
                if v > 0:
                    toks.append((("d", q, i), v))
        return toks

    def barrier(self):
        toks = self.all_tokens()
        for e in self.ENG:
            self.bar[e] = list(toks)

    def final_all(self):
        self.ops["sp"].append((self.all_tokens(), None, None, 0))

    def _sem(self, key):
        if key[0] == "e":
            return self.sems[key[1]]
        return self.dsem[key[1]][key[2]]

    def final_wait(self, eng, toks):
        waits = []
        for k, v in toks:
            waits.append((k, v))
        self.ops[eng].append((waits, None, None, 0))

    def emit(self):
        nc = self.nc
        with nc.Block() as block:
            def mk(e):
                def body(h):
                    for waits, fn, key, inc in self.ops[e]:
                        for k, v in waits:
                            h.wait_ge(self._sem(k), v)
                        if fn is not None:
                            ins = fn(h)
                            ins.then_inc(self._sem(key), inc)
                return body
            block.tensor(mk("pe"))
            block.vector(mk("dve"))
            block.scalar(mk("act"))
            block.gpsimd(mk("pool"))
            block.sync(mk("sp"))


D = 1024
HD = 64
DFF = 2816
NEG = -30000.0
EPS = 1e-6
INDIM = 4888
C_DQ, C_DK, C_DV, C_NQ = 0, 512, 1024, 1536
C_NSA0 = 1536
C_MG = 2840


def host_consts(S):
    c = {}
    c["ident"] = np.eye(128, dtype=np.float32)
    kk = np.arange(128)[:, None]
    q = np.arange(512)[None, :]
    m = np.zeros((128, 8, 512), np.float32)
    for j in range(4):
        m[:, j, :] = np.where(128 * j + kk <= q, 1.0, 0.0)
        m[:, 4 + j, :] = np.where(128 * j + kk > q, 1.0, 0.0)
    c["maskcw"] = m
    cc = np.arange(256)[:, None]
    qq = np.arange(S)[None, :]
    cm = np.where((16 * cc + 31 <= qq) & (cc <= 254), 0.0, NEG).astype(np.float32)
    c["cmask"] = np.ascontiguousarray(cm.reshape(2, 128, S).transpose(1, 0, 2))
    nkt = S // 128
    E = np.zeros((64, nkt, 128), np.float32)
    for kt in range(nkt):
        for k2 in range(128):
            n = 2 * kt + k2 // 64
            if n < 64:
                E[n, kt, k2] = 1.0
    c["E"] = E
    ci = np.arange(256)[:, None] * 16
    sj = np.arange(64)[None, :] * 64
    ov = ((ci < sj + 64) & (ci + 32 > sj)).astype(np.float32)
    ov[255] = 0.0
    c["ov"] = np.ascontiguousarray(ov.reshape(2, 128, 64).transpose(1, 0, 2))
    qb = (np.arange(S) // 64)[:, None]
    n = np.arange(64)[None, :]
    forced = (n == 0) | (n == qb) | (n == qb - 1)
    bt = np.where(forced, 100.0 + n, 0.0)
    bt = np.where(n > qb, -1000.0, bt).astype(np.float32)
    c["btab"] = bt
    c["invf"] = (1.0 / (10000.0 ** (np.arange(0, 64, 2, dtype=np.float32) / 64))).astype(np.float32).reshape(1, 32)
    return c


class KB:
    def __init__(self, nc, S_, st):
        self.nc = nc
        self.S = S_
        self.st = st

    def mm(self, out, lhsT, rhs, start, stop, reads, writes):
        return self.S.op("pe", lambda h: h.matmul(out, lhsT, rhs, start=start, stop=stop, skip_group_check=True),
                         reads=reads, writes=writes)

    def tr(self, out, in_, ident, reads, writes):
        return self.S.op("pe", lambda h: h.transpose(out, in_, ident), reads=reads, writes=writes)

    def act(self, out, in_, func, reads, writes, scale=None, bias=None, accum=None):
        kw = {}
        if scale is not None:
            kw["scale"] = scale
        if bias is not None:
            kw["bias"] = bias
        if accum is not None:
            kw["accum_out"] = accum
        return self.S.op("act", lambda h: h.activation(out=out, in_=in_, func=func, **kw), reads=reads, writes=writes)

    def ts(self, eng, out, in0, s1, s2, op0, op1, reads, writes):
        if op1 is None:
            return self.S.op(eng, lambda h: h.tensor_scalar(out=out, in0=in0, scalar1=s1, scalar2=None, op0=op0),
                             reads=reads, writes=writes)
        return self.S.op(eng, lambda h: h.tensor_scalar(out=out, in0=in0, scalar1=s1, scalar2=s2, op0=op0, op1=op1),
                         reads=reads, writes=writes)

    def tt(self, eng, out, in0, in1, op, reads, writes):
        return self.S.op(eng, lambda h: h.tensor_tensor(out=out, in0=in0, in1=in1, op=op), reads=reads, writes=writes)

    def stt(self, out, in0, scalar, in1, op0, op1, reads, writes, accum=None):
        if accum is None:
            return self.S.op("dve", lambda h: h.scalar_tensor_tensor(out=out, in0=in0, scalar=scalar, in1=in1, op0=op0, op1=op1),
                             reads=reads, writes=writes)
        return self.S.op("dve", lambda h: h.scalar_tensor_tensor(out=out, in0=in0, scalar=scalar, in1=in1, op0=op0, op1=op1,
                                                                  accum_out=accum), reads=reads, writes=writes)

    def cp(self, eng, out, in_, reads, writes):
        if eng == "act":
            return self.S.op("act", lambda h: h.copy(out=out, in_=in_), reads=reads, writes=writes)
        return self.S.op(eng, lambda h: h.tensor_copy(out=out, in_=in_), reads=reads, writes=writes)

    def recip(self, out, in_, reads, writes):
        return self.S.op("dve", lambda h: h.reciprocal(out=out, in_=in_), reads=reads, writes=writes)

    def memset(self, eng, ap, val, writes):
        return self.S.op(eng, lambda h: h.memset(ap, val), writes=writes)

    def dma(self, q, out, in_, reads, writes):
        return self.S.dma(q, lambda h: h.dma_start(out=out, in_=in_), reads=reads, writes=writes)


def build_program(SEQ, debug=False, stop_after=None):
    from contextlib import ExitStack
    nc = bass.Bass("TRN2", target_bir_lowering=False)
    NT = SEQ // 128
    NG = SEQ // 512
    ext_in = lambda n, shp, dt=F32: nc.dram_tensor(n, shp, dt, kind="ExternalInput").ap()
    x_d = ext_in("x", [SEQ, D])
    pos_d = ext_in("pos", [SEQ, 1], I32)
    attn_g_d = ext_in("attn_norm_g", [1, D])
    w_in_d = ext_in("w_in", [D, INDIM])
    lam_d = ext_in("diff_lambda", [1, 256])
    subg_d = ext_in("diff_subln_g", [1, 128])
    pek_d = ext_in("cmp_pe_k", [32, 64])
    pev_d = ext_in("cmp_pe_v", [32, 64])
    w1k_d = ext_in("cmp_k_w1", [2048, 256])
    w2k_d = ext_in("cmp_k_w2", [256, 64])
    w1v_d = ext_in("cmp_v_w1", [2048, 256])
    w2v_d = ext_in("cmp_v_w2", [256, 64])
    wa_d = ext_in("w_branch_a", [512, D])
    wb_d = ext_in("w_branch_b", [512, D])
    wout_d = ext_in("w_out", [D, D])
    ffn_g_d = ext_in("ffn_norm_g", [1, D])
    wg_d = ext_in("w_gate", [D, DFF])
    wu_d = ext_in("w_up", [D, DFF])
    wd_d = ext_in("w_down", [DFF, D])
    fin_g_d = ext_in("final_norm_g", [1, D])
    ident_d = ext_in("ident", [128, 128])
    maskcw_d = ext_in("maskcw", [128, 8, 512])
    cmask_d = ext_in("cmask", [128, 2, SEQ])
    E_d = ext_in("E", [64, NT, 128])
    ov_d = ext_in("ov", [128, 2, 64])
    btab_d = ext_in("btab", [SEQ, 64])
    invf_d = ext_in("invf", [1, 32])
    y_d = nc.dram_tensor("y", [SEQ, D], F32, kind="ExternalOutput").ap()
    skind = "ExternalOutput" if debug else "Internal"
    scr = lambda n, shp, dt: nc.dram_tensor(n, shp, dt, kind=skind).ap()
    xTn_s = scr("xTn_s", [NG, 128, 8, 512], BF16)
    yaT_s = scr("yaT_s", [NG, 128, 4, 512], BF16)
    ybT_s = scr("ybT_s", [NG, 128, 4, 512], BF16)
    h_s = scr("h_s", [SEQ, D], F32)
    wgu_s = scr("wgu_s", [22, 128, 2, 8, 128], BF16)

    with ExitStack() as st:
        S = Sched(nc, st)
        K = KB(nc, S, st)

        def _finish():
            S.final_all()
            with nc.allow_non_contiguous_dma(reason="small strided constant loads"):
                S.emit()
            return nc
        sb = lambda n, shp, dt: st.enter_context(nc.sbuf_tensor(n, shp, dt))
        banks = [st.enter_context(nc.psum_tensor(f"bank{i}", [128, 512], F32)) for i in range(8)]
        bres = [Res(f"bank{i}", excl=True) for i in range(8)]
        bk = lambda i: banks[i][:]
        bkb = lambda i: banks[i][:].bitcast(BF16)
        ident_bf = sb("ident_bf", [128, 128], BF16)
        ident_f = sb("ident_f", [128, 128], F32)
        cos_t = sb("cos_t", [128, NT, 32], F32)
        sin_t = sb("sin_t", [128, NT, 32], F32)
        g_attn = sb("g_attn", [128, 8], F32)
        g_ffn = sb("g_ffn", [128, 8], F32)
        gfin_b = sb("gfin_b", [128, D], F32)
        g08 = sb("g08", [128, 128], F32)
        neglam = sb("neglam", [128, 1], F32)
        maskcw = sb("maskcw_sb", [128, 8, 512], BF16)
        gate = sb("gate", [128, NT, 24], F32)
        R_const = Res("const")
        R_gate = Res("gate")
        ARENA_B = 148 * 1024
        arena = sb("arena", [128, ARENA_B // 2], BF16)

        class Carver:
            def __init__(self):
                self.off = 0

            def take(self, nbytes_pp, dt, shape_free, parts=128):
                assert self.off % 4 == 0
                n2 = (nbytes_pp + 3) // 4 * 4
                assert self.off + n2 <= ARENA_B, ("arena overflow", self.off + n2)
                v = arena[0:parts, self.off // 2:(self.off + nbytes_pp) // 2]
                self.off += n2
                if dt == F32:
                    v = v.bitcast(F32)
                return v

            def t(self, dt, *free, parts=128):
                n = 1
                for f in free:
                    n *= f
                esz = 4 if dt == F32 else 2
                v = self.take(n * esz, dt, free, parts)
                if len(free) == 2:
                    v = v.rearrange("p (a b) -> p a b", a=free[0])
                elif len(free) == 3:
                    v = v.rearrange("p (a b c) -> p a b c", a=free[0], b=free[1])
                elif len(free) == 4:
                    v = v.rearrange("p (a b c d) -> p a b c d", a=free[0], b=free[1], c=free[2])
                return v

        cv = Carver()
        K.dma("pool", ident_bf[:], ident_d, [], [R_const])
        K.dma("sp", ident_f[:], ident_d, [], [R_const])
        K.dma("pool", maskcw[:], maskcw_d, [], [R_const])
        K.dma("sp", g_attn[:], attn_g_d.rearrange("o (k p) -> p (o k)", p=128), [], [R_const])
        K.dma("sp", g_ffn[:], ffn_g_d.rearrange("o (k p) -> p (o k)", p=128), [], [R_const])
        K.dma("sp", gfin_b[:], fin_g_d.partition_broadcast(128), [], [R_const])
        K.dma("sp", g08[:], subg_d.partition_broadcast(128), [], [R_const])
        K.ts("dve", g08[:], g08[:], 0.8, None, ALU.mult, None, [R_const], [R_const])
        pos_i = cv.t(I32 if False else F32, NT)
        pos_i32 = pos_i.bitcast(I32)
        pos_f = cv.t(F32, NT)
        invf_b = cv.t(F32, 32)
        ang = cv.t(F32, NT, 32)
        tmpa = cv.t(F32, NT, 32)
        tmpi = cv.t(F32, NT, 32)
        tmpi_i = tmpi.bitcast(I32)
        R_rope = Res("rope")
        K.dma("sp", pos_i32, pos_d.rearrange("(t p) o -> p (t o)", p=128), [], [R_rope])
        K.dma("sp", invf_b, invf_d.partition_broadcast(128), [], [R_rope])
        K.cp("dve", pos_f, pos_i32, [R_rope], [R_rope])
        K.tt("dve", ang, pos_f.unsqueeze(2).broadcast_to([128, NT, 32]), invf_b.unsqueeze(1).broadcast_to([128, NT, 32]),
             ALU.mult, [R_rope], [R_rope])
        TWO_PI = 2.0 * math.pi
        for (dst, shift) in ((sin_t, 0.0), (cos_t, math.pi / 2)):
            K.ts("dve", tmpa, ang, shift, 1.0 / TWO_PI, ALU.add, ALU.mult, [R_rope], [R_rope])
            K.cp("dve", tmpi_i, tmpa, [R_rope], [R_rope])
            K.cp("dve", tmpa, tmpi_i, [R_rope], [R_rope])
            K.ts("dve", tmpa, tmpa, -TWO_PI, shift, ALU.mult, ALU.add, [R_rope], [R_rope])
            K.tt("dve", tmpa, tmpa, ang, ALU.add, [R_rope], [R_rope])
            K.ts("dve", tmpi, tmpa, math.pi, -TWO_PI, ALU.is_gt, ALU.mult, [R_rope], [R_rope])
            K.tt("dve", tmpa, tmpa, tmpi, ALU.add, [R_rope], [R_rope])
            K.ts("dve", tmpi, tmpa, -math.pi, TWO_PI, ALU.is_lt, ALU.mult, [R_rope], [R_rope])
            K.tt("dve", tmpa, tmpa, tmpi, ALU.add, [R_rope], [R_rope])
            K.ts("dve", tmpa, tmpa, math.pi, -math.pi, ALU.min, ALU.max, [R_rope], [R_rope])
            K.act(dst[:], tmpa, AF.Sin, [R_rope], [R_const])
        lam_sb = cv.t(F32, 256, parts=1)
        lam_j = cv.t(F32, 64, parts=1)
        lam_s = cv.t(F32, 4, parts=1)
        ones_row = cv.t(F32, 128, parts=1)
        R_lam = Res("lam")
        K.dma("sp", lam_sb, lam_d, [], [R_lam])
        K.memset("dve", ones_row, 1.0, [R_lam])
        K.stt(lam_j, lam_sb[:, 0:64], 1.0, lam_sb[:, 64:128], ALU.mult, ALU.mult, [R_lam], [R_lam], accum=lam_s[:, 0:1])
        K.stt(lam_j, lam_sb[:, 128:192], 1.0, lam_sb[:, 192:256], ALU.mult, ALU.mult, [R_lam], [R_lam], accum=lam_s[:, 1:2])
        K.act(lam_s[:, 2:4], lam_s[:, 0:2], AF.Exp, [R_lam], [R_lam])
        K.tt("dve", lam_s[:, 0:1], lam_s[:, 3:4], lam_s[:, 2:3], ALU.subtract, [R_lam], [R_lam])
        K.ts("dve", lam_s[:, 0:1], lam_s[:, 0:1], -0.2, None, ALU.add, None, [R_lam], [R_lam])
        K.mm(bk(0)[:, 0:1], ones_row, lam_s[:, 0:1], True, True, [R_lam], [bres[0]])
        K.cp("dve", neglam[:], bk(0)[:, 0:1], [bres[0]], [R_const])
        wst = [cv.t(BF16, 2, 8, 128) for _ in range(2)]
        wst_r = [Res("wst0"), Res("wst1")]
        R_wgu = Res("wgu_s")
        for j in range(22):
            b = j % 2
            K.dma("pool", wst[b][:, 0], wg_d[:, j * 128:(j + 1) * 128].rearrange("(k p) c -> p k c", p=128), [], [wst_r[b]])
            K.dma("pool", wst[b][:, 1], wu_d[:, j * 128:(j + 1) * 128].rearrange("(k p) c -> p k c", p=128), [], [wst_r[b]])
            K.dma("sp", wgu_s[j], wst[b], [wst_r[b]], [R_wgu])
        S.barrier()
        if stop_after == "P0":
            return _finish()

        cv = Carver()
        xt = [cv.t(F32, D) for _ in range(2)]
        xs = [cv.t(F32, D) for _ in range(2)]
        junk = cv.t(F32, D)
        stg = [cv.t(BF16, 8, 512) for _ in range(2)]
        st1 = [cv.t(F32, 4) for _ in range(2)]
        xt_r = [Res(), Res()]
        xs_r = [Res(), Res()]
        stg_r = [Res(), Res()]
        st1_r = [Res(), Res()]
        junk_r = Res()
        R_xTn = Res("xTn_s")

        def nt_A(src_tile_ap, t, b, xt_ap, xt_res, load=True):
            if load:
                K.dma("sp", xt_ap, src_tile_ap, [], [xt_res])
            K.act(junk, xt_ap, AF.Square, [xt_res], [junk_r], accum=st1[b][:, 0:1])
            K.act(st1[b][:, 1:2], st1[b][:, 0:1], AF.Ln, [junk_r, R_const], [st1_r[b]], scale=1.0 / D, bias=eps_ap)
            K.act(st1[b][:, 2:3], st1[b][:, 1:2], AF.Exp, [st1_r[b]], [st1_r[b]], scale=-0.5)
            K.ts("dve", xs[b], xt_ap, st1[b][:, 2:3], None, ALU.mult, None, [xt_res, st1_r[b]], [xs_r[b]])

        def nt_B(t, b, gcol, stage, stage_r, pb):
            for half in range(2):
                pbank = pb[half]
                for kq in range(4):
                    k = half * 4 + kq
                    K.tr(bk(pbank)[:, kq * 128:(kq + 1) * 128], xs[b][:, k * 128:(k + 1) * 128], ident_f[:],
                         [xs_r[b], R_const], [bres[pbank]])
                K.tt("dve", stage[:, half * 4:half * 4 + 4, (t % 4) * 128:(t % 4 + 1) * 128],
                     bk(pbank).rearrange("p (a b) -> p a b", a=4),
                     gcol[:, half * 4:half * 4 + 4].unsqueeze(2).broadcast_to([128, 4, 128]), ALU.mult,
                     [bres[pbank], R_const], [stage_r])

        eps_t = sb("eps_t", [128, 1], F32)
        K.memset("dve", eps_t[:], EPS, [R_const])
        eps_ap = eps_t[:]
        for tt_ in range(NT + 1):
            if tt_ < NT:
                nt_A(x_d[tt_ * 128:(tt_ + 1) * 128, :], tt_, tt_ % 2, xt[tt_ % 2], xt_r[tt_ % 2])
            if tt_ >= 1:
                t = tt_ - 1
                b = t % 2
                g = t // 4
                sgb = g % 2
                nt_B(t, b, g_attn, stg[sgb], stg_r[sgb], (2 * b, 2 * b + 1))
                if t % 4 == 3:
                    K.dma("sp", xTn_s[g], stg[sgb], [stg_r[sgb]], [R_xTn])
        S.barrier()
        if stop_after == "P1":
            return _finish()

        class AttnPipe:
            def __init__(self, st_banks, pbufs, pres, look=1):
                self.stb = st_banks
                self.pb = pbufs
                self.pr = pres
                self.look = look
                self.i = 0
                self.pend = []
                self.deferred = []

            def defer(self, fn, nsteps):
                self.deferred.append([nsteps, fn])

            def _tick(self):
                ready = [d for d in self.deferred if d[0] <= 1]
                self.deferred = [[d[0] - 1, d[1]] for d in self.deferred if d[0] > 1]
                for d in ready:
                    d[1]()

            def step(self, units):
                ent = []
                for (qk, nrows, pv, post) in units:
                    i = self.i
                    self.i += 1
                    bnk = self.stb[i % len(self.stb)]
                    pi = i % len(self.pb)
                    qk(bnk)
                    ent.append((bnk, pi, nrows, pv, post))
                for ui, (bnk, pi, nrows, pv, post) in enumerate(ent):
                    elo, ehi = getattr(units[ui][0], "exp_cols", (0, 512))
                    K.act(self.pb[pi][0:nrows, elo:ehi], bk(bnk)[0:nrows, elo:ehi], AF.Exp, [bres[bnk]], [self.pr[pi]], scale=0.125)
                    mk = getattr(units[ui][0], "mask01", None)
                    if mk is not None:
                        lo, hi = units[ui][0].mask_cols
                        K.tt("pool" if ui == 0 else "dve", self.pb[pi][0:nrows, lo:hi], self.pb[pi][0:nrows, lo:hi], mk[:, lo:hi],
                             ALU.mult, [self.pr[pi], R_const], [self.pr[pi]])
                self.pend.append(ent)
                if len(self.pend) > self.look:
                    self._flush1()
                self._tick()

            def unit(self, qk, nrows, pv, post=None):
                self.step([(qk, nrows, pv, post)])

            def _flush1(self):
                ent = self.pend.pop(0)
                for (bnk, pi, nrows, pv, post) in ent:
                    pv(self.pb[pi], self.pr[pi])
                for (bnk, pi, nrows, pv, post) in ent:
                    if post is not None:
                        post()

            def flush(self):
                while self.pend:
                    self._flush1()
                while self.deferred:
                    self._tick()

        cv = Carver()
        xg = [cv.t(BF16, 8, 512) for _ in range(2)]
        xg_r = [Res(), Res()]
        qkT = cv.t(BF16, 2, SEQ)
        dv_aug = cv.t(BF16, NT, 130)
        W_hs = [cv.t(BF16, 8, 384) for _ in range(2)]
        qk_tok = [cv.t(BF16, 256) for _ in range(2)]
        rtmp = [cv.t(F32, 4, 128) for _ in range(2)]
        pbufs = [cv.t(BF16, 512) for _ in range(6)]
        tb_ = [cv.t(F32, 2, 4, 128) for _ in range(2)]
        ya4 = [cv.t(F32, 4, 128) for _ in range(2)]
        yj = cv.t(F32, 128)
        yan4 = [cv.t(BF16, 4, 128) for _ in range(2)]
        sst = [cv.t(F32, 12) for _ in range(2)]
        rz = [cv.t(F32, 2, 4) for _ in range(2)]
        mhalf = cv.t(F32, 4)
        tb_r = [Res(), Res()]
        gcount = [0]
        yaT_st = [cv.t(BF16, 512) for _ in range(2)]
        R_qkT, R_dv = Res(), Res()
        R_Whs = [Res(), Res()]
        qk_tok_r = [Res(), Res()]
        rtmp_r = [Res(), Res()]
        pres = [Res() for _ in range(6)]
        t0_r, yj_r = Res(), Res()
        ya_r = [Res(), Res()]
        yan_r = [Res(), Res()]
        sst_r = [Res(), Res()]
        rz_r = [Res(), Res()]
        yaT_st_r = [Res(), Res()]
        R_yaT = Res("yaT_s")
        K.memset("dve", dv_aug[:, :, 128:129], 1.0, [R_dv])
        K.memset("dve", mhalf, -0.5, [R_const])

        def rope(eng, out_v, in_v, t, nh, tmp, tmp_r, in_res, out_res):
            shp = [128, nh, 32]
            cb = cos_t[:, t, :].unsqueeze(1).broadcast_to(shp)
            sbb = sin_t[:, t, :].unsqueeze(1).broadcast_to(shp)
            x1 = in_v[:, :, 0, :]
            x2 = in_v[:, :, 1, :]
            tv = [tmp[:, i, 0:nh * 32].rearrange("p (h d) -> p h d", h=nh) for i in range(4)]
            K.tt(eng, tv[0], x1, cb, ALU.mult, in_res + [R_const], [tmp_r])
            K.tt(eng, tv[1], x2, sbb, ALU.mult, in_res + [R_const], [tmp_r])
            K.tt(eng, tv[2], x2, cb, ALU.mult, in_res + [R_const], [tmp_r])
            K.tt(eng, tv[3], x1, sbb, ALU.mult, in_res + [R_const], [tmp_r])
            K.tt(eng, out_v[:, :, 0, :], tv[0], tv[1], ALU.subtract, [tmp_r], out_res)
            K.tt(eng, out_v[:, :, 1, :], tv[2], tv[3], ALU.add, [tmp_r], out_res)

        for hd in range(4):
            for hn in ([0, 1] if hd == 0 else [hd + 1]):
                if hn < 4:
                    for i, c0 in enumerate((C_DQ + hn * 128, C_DK + hn * 128, C_DV + hn * 128)):
                        K.dma("pool", W_hs[hn % 2][:, :, i * 128:(i + 1) * 128],
                              w_in_d[:, c0:c0 + 128].rearrange("(k p) c -> p k c", p=128), [], [R_Whs[hn % 2]])
            W_h = W_hs[hd % 2]
            R_Wh = R_Whs[hd % 2]
            def projA(t):
                g = t // 4
                gb = g % 2
                b = t % 2
                if t % 4 == 0:
                    K.dma("sp", xg[gb], xTn_s[g], [R_xTn], [xg_r[gb]])
                pbank = b
                for k in range(8):
                    K.mm(bk(pbank)[:, 0:384], xg[gb][:, k, (t % 4) * 128:(t % 4 + 1) * 128], W_h[:, k, :],
                         k == 0, k == 7, [xg_r[gb], R_Wh], [bres[pbank]])
                pv = bk(pbank)[:, 0:256].rearrange("p (h c d) -> p h c d", h=4, c=2)
                ov_ = qk_tok[b].rearrange("p (h c d) -> p h c d", h=4, c=2)
                rope("dve", ov_, pv, t, 4, rtmp[b], rtmp_r[b], [bres[pbank]], [qk_tok_r[b]])
                K.cp("act", dv_aug[:, t, 0:128], bk(pbank)[:, 256:384], [bres[pbank]], [R_dv])

            def projB(t):
                b = t % 2
                tb = 2 + b
                for i in range(2):
                    K.tr(bkb(tb)[:, i * 128:(i + 1) * 128], qk_tok[b][:, i * 128:(i + 1) * 128], ident_bf[:],
                         [qk_tok_r[b], R_const], [bres[tb]])
                K.cp("act", qkT[:, :, t * 128:(t + 1) * 128], bkb(tb)[:, 0:256].rearrange("p (a b) -> p a b", a=2),
                     [bres[tb]], [R_qkT])

            for tt_ in range(NT + 1):
                if tt_ < NT:
                    projA(tt_)
                if tt_ >= 1:
                    projB(tt_ - 1)
            if stop_after == "P2a":
                return _finish()
            pipe = AttnPipe([4, 5, 6, 7], pbufs, pres, look=2)
            for C in range(NG):
                nk = 4 * C + 4
                for kt in range(nk):
                    diag = kt - 4 * C
                    units = []
                    for s in range(2):
                        ob = (0, 1) if s == 0 else (2, 3)

                        def qk(bnk, kt=kt, s=s, C=C, diag=diag):
                            K.mm(bk(bnk), qkT[64 * s:64 * s + 64, 1, kt * 128:(kt + 1) * 128],
                                 qkT[64 * s:64 * s + 64, 0, C * 512:(C + 1) * 512], True, True, [R_qkT], [bres[bnk]])
                        if diag >= 0:
                            qk.mask01 = maskcw[:, diag, :]
                            qk.mask_cols = (128 * diag, 128 * diag + 128)
                            qk.exp_cols = (128 * diag, 512)

                        def pv(P, Pr, kt=kt, ob=ob, diag=diag, nk=nk):
                            for qs in range(4):
                                if diag >= 0 and qs < diag:
                                    continue
                                bnk = ob[qs // 2]
                                c0 = (qs % 2) * 129
                                first = (kt == 0 and qs % 2 == 0)
                                K.mm(bk(bnk)[:, c0:c0 + 129], P[:, qs * 128:(qs + 1) * 128], dv_aug[:, kt, 0:129],
                                     first, kt == nk - 1, [Pr, R_dv], [bres[bnk]])

                        post = None
                        if kt == nk - 1:
                            def post(s=s, C=C, ob=ob, hd=hd):
                                gi = gcount[0] % 2
                                for half in range(2):
                                    bnk = ob[half]
                                    K.recip(rz[gi][:, s, 2 * half:2 * half + 2], bk(bnk)[:, 128:258:129], [bres[bnk]], [rz_r[gi]])
                                for qs in range(4):
                                    bnk = ob[qs // 2]
                                    c0 = (qs % 2) * 129
                                    K.ts("dve", tb_[gi][:, s, qs, :], bk(bnk)[:, c0:c0 + 128], rz[gi][:, s, qs:qs + 1], None,
                                         ALU.mult, None, [bres[bnk], rz_r[gi]], [tb_r[gi]])
                                if s == 1:
                                    gcount[0] += 1

                                    def post_b1(gi=gi):
                                        for qs in range(4):
                                            K.stt(ya4[gi][:, qs, :], tb_[gi][:, 1, qs, :], neglam[:], tb_[gi][:, 0, qs, :], ALU.mult, ALU.add,
                                                  [tb_r[gi], R_const], [ya_r[gi]])
                                            K.stt(yj, ya4[gi][:, qs, :], 1.0, ya4[gi][:, qs, :], ALU.mult, ALU.mult, [ya_r[gi]], [yj_r],
                                                  accum=sst[gi][:, qs:qs + 1])
                                        K.ts("pool", sst[gi][:, 4:8], sst[gi][:, 0:4], 1.0 / 128, EPS, ALU.mult, ALU.add, [yj_r], [sst_r[gi]])
                                        K.tt("pool", sst[gi][:, 8:12], sst[gi][:, 4:8], mhalf[:, 0:4], ALU.pow, [sst_r[gi], R_const], [sst_r[gi]])
                                        for qs in range(4):
                                            K.stt(yan4[gi][:, qs, :], ya4[gi][:, qs, :], sst[gi][:, 8 + qs:9 + qs], g08[:], ALU.mult, ALU.mult,
                                                  [ya_r[gi], sst_r[gi], R_const], [yan_r[gi]])

                                    def post_b2(gi=gi, C=C, hd=hd):
                                        for qs in range(4):
                                            K.tr(bkb(7)[:, qs * 128:(qs + 1) * 128], yan4[gi][:, qs, :], ident_bf[:], [yan_r[gi], R_const], [bres[7]])
                                        K.cp("dve", yaT_st[gi], bkb(7)[:, 0:512], [bres[7]], [yaT_st_r[gi]])
                                        K.dma("sp", yaT_s[C, :, hd, :], yaT_st[gi], [yaT_st_r[gi]], [R_yaT])
                                    pipe.defer(post_b1, 2)
                                    pipe.defer(post_b2, 5)
                        units.append((qk, 128, pv, post))
                    pipe.step(units)
            pipe.flush()
        S.barrier()
        if stop_after == "P2":
            return _finish()

        cv = Carver()
        nqT = cv.t(BF16, 4, SEQ)
        KT = cv.t(BF16, 4, SEQ)
        vs_aug = cv.t(BF16, NT, 2, 66)
        vw_aug = cv.t(BF16, NT, 2, 66)
        R_nqT, R_KT, R_vs, R_vw = Res(), Res(), Res(), Res()
        mark = cv.off
        xg = [cv.t(BF16, 8, 512) for _ in range(2)]
        xg_r = [Res(), Res()]
        Wn = cv.t(BF16, 8, 1304)
        R_Wn = Res()
        nq_tok = [cv.t(BF16, 512) for _ in range(2)]
        k_tok = [cv.t(BF16, 4, 128) for _ in range(2)]
        rtmpn = [cv.t(F32, 4, 256) for _ in range(2)]
        rtmpk = [cv.t(F32, 4, 192) for _ in range(2)]
        rtmpk_r = [Res(), Res()]
        gtmp = [cv.t(F32, 24) for _ in range(2)]
        nq_tok_r, k_tok_r, rtmpn_r, gtmp_r = [Res(), Res()], [Res(), Res()], [Res(), Res()], [Res(), Res()]
        for (d0, n_, c0) in ((0, 640, 1536), (640, 128, 2304), (768, 128, 2560), (896, 128, 2176), (1024, 128, 2432), (1152, 152, 2688)):
            K.dma("pool", Wn[:, :, d0:d0 + n_], w_in_d[:, c0:c0 + n_].rearrange("(k p) c -> p k c", p=128), [], [R_Wn])
        K.memset("dve", vs_aug[:, :, :, 64:65], 1.0, [R_vs])
        K.memset("dve", vw_aug[:, :, :, 64:65], 1.0, [R_vw])
        def p3A(t):
            g = t // 4
            gb = g % 2
            b = t % 2
            if t % 4 == 0:
                K.dma("sp", xg[gb], xTn_s[g], [R_xTn], [xg_r[gb]])
            pb3 = (0, 1, 2) if b == 0 else (3, 4, 5)
            for bi, (c0, cn) in enumerate(((0, 512), (512, 512), (1024, 280))):
                for k in range(8):
                    K.mm(bk(pb3[bi])[:, 0:cn], xg[gb][:, k, (t % 4) * 128:(t % 4 + 1) * 128], Wn[:, k, c0:c0 + cn],
                         k == 0, k == 7, [xg_r[gb], R_Wn], [bres[pb3[bi]]])
            A, Bk, Ck = pb3
            shp = [128, 2, 4, 32]
            cb4 = cos_t[:, t, :].unsqueeze(1).unsqueeze(1).broadcast_to(shp)
            sb4 = sin_t[:, t, :].unsqueeze(1).unsqueeze(1).broadcast_to(shp)
            pin = bk(A).rearrange("p (g j c d) -> p g j c d", g=2, j=4, c=2)
            pout = nq_tok[b].rearrange("p (j g c d) -> p g j c d", j=4, g=2, c=2)
            tv = [rtmpn[b][:, i, 0:256].rearrange("p (g j d) -> p g j d", g=2, j=4) for i in range(4)]
            x1, x2 = pin[:, :, :, 0, :], pin[:, :, :, 1, :]
            rr = [bres[A], R_const]
            K.tt("dve", tv[0], x1, cb4, ALU.mult, rr, [rtmpn_r[b]])
            K.tt("dve", tv[1], x2, sb4, ALU.mult, rr, [rtmpn_r[b]])
            K.tt("dve", tv[2], x2, cb4, ALU.mult, rr, [rtmpn_r[b]])
            K.tt("dve", tv[3], x1, sb4, ALU.mult, rr, [rtmpn_r[b]])
            K.tt("dve", pout[:, :, :, 0, :], tv[0], tv[1], ALU.subtract, [rtmpn_r[b]], [nq_tok_r[b]])
            K.tt("dve", pout[:, :, :, 1, :], tv[2], tv[3], ALU.add, [rtmpn_r[b]], [nq_tok_r[b]])
            kin = bk(Bk)[:, 0:384].rearrange("p (h c d) -> p h c d", h=6, c=2)
            rope("dve", k_tok[b][:, 0:3, :].rearrange("p s (h c d) -> p (s h) c d", h=2, c=2), kin, t, 6,
                 rtmpk[b], rtmpk_r[b], [bres[Bk]], [k_tok_r[b]])
            K.cp("act", k_tok[b][:, 3, :], bk(Bk)[:, 384:512], [bres[Bk]], [k_tok_r[b]])
            K.cp("act", vs_aug[:, t, :, 0:64], bk(Ck)[:, 0:128].rearrange("p (g d) -> p g d", g=2), [bres[Ck]], [R_vs])
            K.cp("act", vw_aug[:, t, :, 0:64], bk(Ck)[:, 128:256].rearrange("p (g d) -> p g d", g=2), [bres[Ck]], [R_vw])
            K.act(gtmp[b], bk(Ck)[:, 256:280], AF.Exp, [bres[Ck]], [gtmp_r[b]], scale=-1.0)
            K.ts("dve", gtmp[b], gtmp[b], 1.0, None, ALU.add, None, [gtmp_r[b]], [gtmp_r[b]])
            K.recip(gate[:, t, :], gtmp[b], [gtmp_r[b]], [R_gate])

        def p3B(t):
            b = t % 2
            tb = 6 + b
            for i in range(4):
                K.tr(bkb(tb)[:, i * 128:(i + 1) * 128], nq_tok[b][:, i * 128:(i + 1) * 128], ident_bf[:],
                     [nq_tok_r[b], R_const], [bres[tb]])
            for i in range(4):
                K.tr(bkb(tb)[:, 512 + i * 128:512 + (i + 1) * 128], k_tok[b][:, i, :], ident_bf[:],
                     [k_tok_r[b], R_const], [bres[tb]])
            K.cp("act", nqT[:, :, t * 128:(t + 1) * 128], bkb(tb)[:, 0:512].rearrange("p (a b) -> p a b", a=4), [bres[tb]], [R_nqT])
            K.cp("act", KT[:, :, t * 128:(t + 1) * 128], bkb(tb)[:, 512:1024].rearrange("p (a b) -> p a b", a=4), [bres[tb]], [R_KT])

        for tt_ in range(NT + 1):
            if tt_ < NT:
                p3A(tt_)
            if tt_ >= 1:
                p3B(tt_ - 1)
        S.barrier()
        if stop_after == "P3a":
            return _finish()

        cv.off = mark
        kcmpT = cv.t(BF16, 256)
        vc_aug = cv.t(BF16, 2, 2, 130)
        mark = cv.off
        w1 = [cv.t(BF16, 32, 256) for _ in range(2)]
        w2kd = cv.t(BF16, 2, 128)
        w2v = cv.t(BF16, 2, 64)
        peT = cv.t(BF16, 2, 32)
        cb_sb = cv.t(F32, 2, 2)
        hid_sb = cv.t(BF16, 2, 2, 2, 256)
        R_w1, R_w2, R_pe, R_cb, R_hid, R_kcmp, R_vca = Res(), Res(), Res(), Res(), Res(), Res(), Res()
        for kv, wd_ in enumerate((w1k_d, w1v_d)):
            for half in range(2):
                K.dma("pool", w1[kv][64 * half:64 * half + 64], wd_.rearrange("(l d) h -> d l h", d=64), [], [R_w1])
        for half in range(2):
            K.dma("pool", w2kd[:, :, 64 * half:64 * half + 64], w2k_d.rearrange("(c p) d -> p c d", p=128), [], [R_w2])
        K.dma("pool", w2v, w2v_d.rearrange("(c p) d -> p c d", p=128), [], [R_w2])
        with nc.allow_non_contiguous_dma(reason="tiny pe transpose load"):
            for kv, pd in enumerate((pek_d, pev_d)):
                for half in range(2):
                    K.dma("pool", peT[64 * half:64 * half + 64, kv, :], pd.rearrange("l d -> d l"), [], [R_pe])
        K.memset("dve", vc_aug[:, :, :, 64:65], 1.0, [R_vca])
        for g in range(2):
            K.dma("pool", vc_aug[:, g, :, 66:130], ov_d, [], [R_vca])
        for kv in range(2):
            for hc in range(2):
                col = kv * 2 + hc
                for l in range(32):
                    K.mm(bk(0)[:, col:col + 1], w1[kv][0:64, l, hc * 128:(hc + 1) * 128], peT[0:64, kv, l:l + 1],
                         l == 0 and col == 0, l == 31, [R_w1, R_pe], [bres[0]])
        K.cp("dve", cb_sb.rearrange("p a b -> p (a b)"), bk(0)[:, 0:4], [bres[0]], [R_cb])
        NCB = (SEQ - 32) // 16 + 1
        ui = 0
        for kv in range(2):
            for hc in range(2):
                bb = (1, 2) if ui % 2 == 0 else (3, 6)
                ui += 1
                for l in range(32):
                    for g in range(2):
                        K.mm(bk(bb[g])[:, 0:NCB], w1[kv][64 * g:64 * g + 64, l, hc * 128:(hc + 1) * 128],
                             KT[64 * g:64 * g + 64, 3 * kv, l:l + 16 * (NCB - 1) + 1:16], l == 0, l == 31, [R_w1, R_KT], [bres[bb[g]]])
                for g in range(2):
                    K.act(hid_sb[:, kv, g, hc, 0:NCB], bk(bb[g])[:, 0:NCB], AF.Silu, [bres[bb[g]], R_cb], [R_hid],
                          bias=cb_sb[:, kv, hc:hc + 1])
        for g in range(2):
            for hc in range(2):
                K.mm(bk(4)[:, 0:NCB], w2kd[:, hc, :], hid_sb[:, 0, g, hc, 0:NCB], hc == 0, hc == 1, [R_w2, R_hid], [bres[4]])
            K.cp("dve", kcmpT[64 * g:64 * g + 64, 0:NCB], bk(4)[64 * g:64 * g + 64, 0:NCB], [bres[4]], [R_kcmp])
            for ct in range(2):
                n = min(128, NCB - ct * 128)
                if n <= 0:
                    continue
                for hc in range(2):
                    K.mm(bk(5)[0:n, ct * 64:ct * 64 + 64], hid_sb[:, 1, g, hc, ct * 128:ct * 128 + n], w2v[:, hc, :],
                         hc == 0 and ct == 0, hc == 1, [R_hid, R_w2], [bres[5]])
            for ct in range(2):
                n = min(128, NCB - ct * 128)
                if n <= 0:
                    continue
                K.cp("dve", vc_aug[0:n, g, ct, 0:64], bk(5)[0:n, ct * 64:ct * 64 + 64], [bres[5]], [R_vca])
        S.barrier()
        if stop_after == "P3b":
            return _finish()

        cv.off = mark
        E_sb = cv.t(BF16, NT, 128)
        pbufs = [cv.t(BF16, 512) for _ in range(6)]
        pres = [Res() for _ in range(6)]
        acc = cv.t(F32, 4, 512)
        imp = cv.t(F32, 4, 2, 64)
        imp2 = cv.t(F32, 64)
        selb = cv.t(F32, 64)
        selb_bf = cv.t(BF16, 4, 2, 64)
        R_selbf = Res()
        m8 = cv.t(F32, 16)
        selbT = cv.t(BF16, 512)
        btab = [cv.t(F32, 4, 64) for _ in range(2)]
        cmk = [cv.t(BF16, 2, 512) for _ in range(2)]
        rzn = [cv.t(F32, 8) for _ in range(2)]
        yb_bf = cv.t(BF16, 4, 512)
        ybT_st = [cv.t(BF16, 4, 512) for _ in range(2)]
        R_E, R_acc, R_imp, R_sel, R_selbT, R_ybbf = Res(), Res(), Res(), Res(), Res(), Res()
        btab_r, cmk_r, rzn_r, ybT_st_r = [Res(), Res()], [Res(), Res()], [Res(), Res()], [Res(), Res()]
        R_ybT = Res("ybT_s")
        K.dma("pool", E_sb[0:64], E_d, [], [R_E])
        K.dma("pool", E_sb[64:128], E_d, [], [R_E])
        hh_list = [(hh, hh // 4, hh % 4) for hh in range(8)]
        ecount = [0]

        def evac_branch(bnk, ncols, hh, br, C, with_imp):
            e = ecount[0] % 2
            ecount[0] += 1
            Ov = bk(bnk)[:, 0:4 * ncols].rearrange("p (a b) -> p a b", a=4)
            g = hh // 4
            K.ts("dve", rzn[e][:, 0:4], Ov[:, :, 64], 1e-30, None, ALU.add, None, [bres[bnk]], [rzn_r[e]])
            K.recip(rzn[e][:, 0:4], rzn[e][:, 0:4], [rzn_r[e]], [rzn_r[e]])
            K.tt("dve", rzn[e][:, 4:8], rzn[e][:, 0:4], gate[:, 4 * C:4 * C + 4, br * 8 + hh], ALU.mult,
                 [rzn_r[e], R_gate], [rzn_r[e]])
            for qs in range(4):
                dst = acc[:, qs, hh * 64:(hh + 1) * 64]
                if br == 0:
                    K.ts("dve", dst, Ov[:, qs, 0:64], rzn[e][:, 4 + qs:5 + qs], None, ALU.mult, None,
                         [bres[bnk], rzn_r[e]], [R_acc])
                else:
                    K.stt(dst, Ov[:, qs, 0:64], rzn[e][:, 4 + qs:5 + qs], dst, ALU.mult, ALU.add,
                          [bres[bnk], rzn_r[e], R_acc], [R_acc])
                if with_imp:
                    K.stt(imp[:, qs, g, :], Ov[:, qs, 65:129], rzn[e][:, qs:qs + 1], imp[:, qs, g, :], ALU.mult, ALU.add,
                          [bres[bnk], rzn_r[e], R_imp], [R_imp])

        pipe = AttnPipe([0, 1, 2, 3], pbufs, pres, look=2)
        obank_i = [0]
        pairs = [((j, 0, j), (4 + j, 1, j)) for j in range(4)]
        for C in range(NG):
            cbi = C % 2
            for Cn in ([0, 1] if C == 0 else [C + 1]):
                if Cn < NG:
                    K.dma("sp", btab[Cn % 2], btab_d[Cn * 512:(Cn + 1) * 512, :].rearrange("(a p) n -> p a n", p=128), [], [btab_r[Cn % 2]])
                    K.dma("pool", cmk[Cn % 2], cmask_d[:, :, Cn * 512:(Cn + 1) * 512], [], [cmk_r[Cn % 2]])
            pipe.flush()
            for g in range(2):
                K.cp("dve", imp[:, :, g, :], btab[cbi], [btab_r[cbi]], [R_imp])
            ncts = [ct for ct in range(2) if (ct == 0 or 32 * C + 30 >= 128) and NCB - ct * 128 > 0]
            for pair in pairs:
                for ui_, ct in enumerate(ncts):
                    n = min(128, NCB - ct * 128)
                    units = []
                    for pi_, (hh, g, j) in enumerate(pair):
                        oa = 4 + 2 * pi_
                        ib = oa + 1

                        def qk(bnk, ct=ct, n=n, g=g, j=j, C=C, cbi=cbi):
                            K.mm(bk(bnk)[0:n, :], kcmpT[64 * g:64 * g + 64, ct * 128:ct * 128 + n],
                                 nqT[64 * g:64 * g + 64, j, C * 512:(C + 1) * 512], True, False, [R_kcmp, R_nqT], [bres[bnk]])
                            K.mm(bk(bnk)[0:n, :], ident_bf[:, 0:n], cmk[cbi][:, ct, :], False, True, [R_const, cmk_r[cbi]], [bres[bnk]])

                        def pv(P, Pr, ct=ct, n=n, g=g, oa=oa, ib=ib, first=(ui_ == 0), last=(ui_ == len(ncts) - 1)):
                            for qs in range(4):
                                K.mm(bk(oa)[:, qs * 65:qs * 65 + 65], P[0:n, qs * 128:(qs + 1) * 128], vc_aug[0:n, g, ct, 0:65],
                                     first and qs == 0, last, [Pr, R_vca], [bres[oa]])
                                K.mm(bk(ib)[:, qs * 64:qs * 64 + 64], P[0:n, qs * 128:(qs + 1) * 128], vc_aug[0:n, g, ct, 66:130],
                                     first and qs == 0, last, [Pr, R_vca], [bres[ib]])
                        post = None
                        if ui_ == len(ncts) - 1:
                            def post(oa=oa, ib=ib, hh=hh, C=C, g=g):
                                e = ecount[0] % 2
                                evac_branch(oa, 65, hh, 0, C, False)
                                Iv = bk(ib)[:, 0:256].rearrange("p (a b) -> p a b", a=4)
                                for qs in range(4):
                                    K.stt(imp[:, qs, g, :], Iv[:, qs, :], rzn[e][:, qs:qs + 1], imp[:, qs, g, :], ALU.mult, ALU.add,
                                          [bres[ib], rzn_r[e], R_imp], [R_imp])
                        units.append((qk, n, pv, post))
                    pipe.step(units)
            def selection_dve(C=C):
                for qs in range(4):
                    for g in range(2):
                        iv = imp[:, qs, g, :]
                        K.S.op("dve", lambda h, iv=iv: h.max(out=m8[:, 0:8], in_=iv), reads=[R_imp], writes=[R_sel])
                        K.S.op("dve", lambda h, iv=iv: h.match_replace(out=imp2, in_to_replace=m8[:, 0:8], in_values=iv, imm_value=-1e9),
                               reads=[R_imp, R_sel], writes=[R_sel])
                        K.S.op("dve", lambda h: h.max(out=m8[:, 8:16], in_=imp2), reads=[R_sel], writes=[R_sel])
                        K.ts("dve", selb, iv, m8[:, 15:16], None, ALU.is_ge, None, [R_imp, R_sel], [R_sel])
                        K.ts("dve", selb_bf[:, qs, g, :], selb, -1.0, -NEG, ALU.add, ALU.mult, [R_sel], [R_selbf])

            pipe.defer(selection_dve, 3)

            def sel_transposes():
                for qs in range(4):
                    K.tr(bkb(0)[:, qs * 128:(qs + 1) * 128], selb_bf[:, qs].rearrange("p g n -> p (g n)"), ident_bf[:],
                         [R_selbf, R_const], [bres[0]])
                K.cp("dve", selbT, bkb(0)[:, 0:512], [bres[0]], [R_selbT])
            for br in (2, 1):
                if br == 1:
                    sel_transposes()
                    kts = list(range(4 * C + 4))
                else:
                    kts = [kt for kt in range(4 * C - 4, 4 * C + 4) if kt >= 0]
                Vt = vs_aug if br == 1 else vw_aug
                Rv = R_vs if br == 1 else R_vw
                slot = 1 if br == 1 else 2
                for pair in pairs:
                    ob0 = 4 + 2 * (obank_i[0] % 2)
                    obank_i[0] += 1
                    for ui_, kt in enumerate(kts):
                        off = kt - 4 * C
                        units = []
                        for pi_, (hh, g, j) in enumerate(pair):
                            oa = ob0 + pi_

                            def qk(bnk, kt=kt, off=off, g=g, j=j, C=C, br=br, slot=slot):
                                K.mm(bk(bnk), KT[64 * g:64 * g + 64, slot, kt * 128:(kt + 1) * 128],
                                     nqT[64 * g:64 * g + 64, j, C * 512:(C + 1) * 512], True, br == 2, [R_KT, R_nqT], [bres[bnk]])
                                if br == 1:
                                    K.mm(bk(bnk), E_sb[64 * g:64 * g + 64, kt, :], selbT[64 * g:64 * g + 64, :], False, True,
                                         [R_E, R_selbT], [bres[bnk]])
                            qlo, qhi = 0, 4
                            if off >= 0:
                                qk.mask01 = maskcw[:, off, :]
                                qk.mask_cols = (128 * off, 128 * off + 128)
                                qk.exp_cols = (128 * off, 512)
                                qlo = off
                            elif br == 2:
                                jw = off + 4
                                qk.mask01 = maskcw[:, 4 + jw, :]
                                qk.mask_cols = (128 * jw, 128 * jw + 128)
                                qk.exp_cols = (0, 128 * jw + 128)
                                qhi = jw + 1

                            def pv(P, Pr, kt=kt, g=g, oa=oa, Vt=Vt, Rv=Rv, first=(ui_ == 0), last=(ui_ == len(kts) - 1), qlo=qlo, qhi=qhi):
                                for qs in range(qlo, qhi):
                                    K.mm(bk(oa)[:, qs * 65:qs * 65 + 65], P[:, qs * 128:(qs + 1) * 128], Vt[:, kt, g, 0:65],
                                         first and qs == 0, last, [Pr, Rv], [bres[oa]])
                            post = None
                            if ui_ == len(kts) - 1:
                                def post(oa=oa, hh=hh, C=C, br=br):
                                    evac_branch(oa, 65, hh, br, C, False)
                            units.append((qk, 128, pv, post))
                        pipe.step(units)
            pipe.flush()
            K.cp("dve", yb_bf, acc, [R_acc], [R_ybbf])
            for qs in range(4):
                for fc in range(4):
                    K.tr(bkb(1)[:, fc * 128:(fc + 1) * 128], yb_bf[:, qs, fc * 128:(fc + 1) * 128], ident_bf[:],
                         [R_ybbf, R_const], [bres[1]])
                K.cp("dve", ybT_st[cbi][:, :, qs * 128:(qs + 1) * 128], bkb(1)[:, 0:512].rearrange("p (a b) -> p a b", a=4),
                     [bres[1]], [ybT_st_r[cbi]])
            K.dma("pool", ybT_s[C], ybT_st[cbi], [ybT_st_r[cbi]], [R_ybT])
        S.barrier()
        if stop_after == "P3c":
            return _finish()

        cv = Carver()
        Wmg = cv.t(BF16, 8, 2048)
        Wa = cv.t(BF16, 4, D)
        Wb = cv.t(BF16, 4, D)
        Wo = cv.t(BF16, 8, D)
        xg = [cv.t(BF16, 8, 512) for _ in range(2)]
        yag = [cv.t(BF16, 4, 512) for _ in range(2)]
        ybg = [cv.t(BF16, 4, 512) for _ in range(2)]
        xt = [cv.t(F32, D) for _ in range(2)]
        mT = cv.t(BF16, 8, 512)
        sg = [cv.t(F32, 2, 512) for _ in range(2)]
        mm_ = [cv.t(F32, 2, 512) for _ in range(2)]
        ho = [cv.t(F32, D) for _ in range(2)]
        R_W4, R_mT, R_h = Res(), Res(), Res("h_s")
        xg_r, yag_r, ybg_r, xt_r, sg_r, mm_r, ho_r = ([Res(), Res()] for _ in range(7))
        R_Wmg = [Res() for _ in range(4)]
        R_Wab = [Res(), Res()]
        R_Wo = Res()

        def ld_mg(cp):
            for i in range(2):
                c0 = C_MG + i * 1024 + cp * 256
                K.dma("pool", Wmg[:, :, i * 1024 + cp * 256:i * 1024 + cp * 256 + 256],
                      w_in_d[:, c0:c0 + 256].rearrange("(k p) c -> p k c", p=128), [], [R_Wmg[cp]])

        def ld_ab(hf):
            K.dma("pool", Wa[:, :, hf * 512:(hf + 1) * 512], wa_d[:, hf * 512:(hf + 1) * 512].rearrange("(k p) c -> p k c", p=128), [], [R_Wab[hf]])
            K.dma("pool", Wb[:, :, hf * 512:(hf + 1) * 512], wb_d[:, hf * 512:(hf + 1) * 512].rearrange("(k p) c -> p k c", p=128), [], [R_Wab[hf]])

        ld_mg(0)
        ld_ab(0)
        ld_mg(1)
        ld_mg(2)
        ld_ab(1)
        ld_mg(3)
        for kq in range(2):
            K.dma("pool", Wo[:, 4 * kq:4 * kq + 4, :], wout_d[kq * 512:(kq + 1) * 512, :].rearrange("(k p) c -> p k c", p=128), [], [R_Wo])
        for G in range(NG):
            gb = G % 2
            for Gn in ([0, 1] if G == 0 else [G + 1]):
                if Gn < NG:
                    K.dma("sp", xg[Gn % 2], xTn_s[Gn], [R_xTn], [xg_r[Gn % 2]])
                    K.dma("sp", yag[Gn % 2], yaT_s[Gn], [R_yaT], [yag_r[Gn % 2]])
                    K.dma("sp", ybg[Gn % 2], ybT_s[Gn], [R_ybT], [ybg_r[Gn % 2]])
            for dc in range(8):
                e = dc % 2
                pb4 = (0, 1, 2, 3) if e == 0 else (4, 5, 6, 7)
                for i in range(2):
                    for k in range(8):
                        K.mm(bk(pb4[i]), Wmg[:, k, i * 1024 + dc * 128:i * 1024 + (dc + 1) * 128], xg[gb][:, k, :],
                             k == 0, k == 7, [R_Wmg[dc // 2], xg_r[gb]], [bres[pb4[i]]])
                for i, (Wx, yg, yr) in enumerate(((Wa, yag, yag_r), (Wb, ybg, ybg_r))):
                    for k in range(4):
                        K.mm(bk(pb4[2 + i]), Wx[:, k, dc * 128:(dc + 1) * 128], yg[gb][:, k, :], k == 0, k == 3,
                             [R_Wab[dc // 4], yr[gb]], [bres[pb4[2 + i]]])
                for i in range(2):
                    K.act(sg[e][:, i, :], bk(pb4[i]), AF.Sigmoid, [bres[pb4[i]]], [sg_r[e]])
                for i in range(2):
                    K.tt("dve", mm_[e][:, i, :], sg[e][:, i, :], bk(pb4[2 + i]), ALU.mult, [sg_r[e], bres[pb4[2 + i]]], [mm_r[e]])
                K.tt("dve", mT[:, dc, :], mm_[e][:, 0, :], mm_[e][:, 1, :], ALU.add, [mm_r[e]], [R_mT])
            for qs in range(4):
                t = G * 4 + qs
                b = qs % 2
                K.dma("sp", xt[b], x_d[t * 128:(t + 1) * 128, :], [], [xt_r[b]])
                for n2 in range(2):
                    bnk = 2 * b + n2
                    for dc in range(8):
                        K.mm(bk(bnk), mT[:, dc, qs * 128:(qs + 1) * 128], Wo[:, dc, n2 * 512:(n2 + 1) * 512], dc == 0, dc == 7,
                             [R_mT, R_Wo], [bres[bnk]])
                    K.tt("dve", ho[b][:, n2 * 512:(n2 + 1) * 512], bk(bnk), xt[b][:, n2 * 512:(n2 + 1) * 512], ALU.add,
                         [bres[bnk], xt_r[b]], [ho_r[b]])
                K.dma("pool", h_s[t * 128:(t + 1) * 128, :], ho[b], [ho_r[b]], [R_h])
        S.barrier()
        if stop_after == "P4":
            return _finish()

        cv = Carver()
        Wd = cv.t(BF16, 22, D)
        hgs = [cv.t(F32, 4, D) for _ in range(2)]
        xs = [cv.t(F32, D) for _ in range(2)]
        junk = cv.t(F32, D)
        hT = cv.t(BF16, 8, 512)
        actT = cv.t(BF16, 22, 512)
        wr = [cv.t(BF16, 2, 8, 128) for _ in range(4)]
        sgl = [cv.t(F32, 512) for _ in range(2)]
        st1 = [cv.t(F32, 4) for _ in range(2)]
        yo = [cv.t(F32, D) for _ in range(2)]
        R_Wd, R_hT, R_act, R_y = Res(), Res(), Res(), Res("y")
        R_hgs = [Res(), Res()]
        xs_r, wr_r, sgl_r, st1_r, yo_r = [Res(), Res()], [Res() for _ in range(4)], [Res(), Res()], [Res(), Res()], [Res(), Res()]
        junk_r = Res()
        for kq in range(2):
            K.dma("pool", Wd[:, 11 * kq:11 * kq + 11, :], wd_d[kq * 1408:(kq + 1) * 1408, :].rearrange("(k p) c -> p k c", p=128), [], [R_Wd])
        wi = 0
        for G in range(NG):
            for Gn in ([0, 1] if G == 0 else [G + 1]):
                if Gn < NG:
                    K.dma("sp", hgs[Gn % 2], h_s[Gn * 512:(Gn + 1) * 512, :].rearrange("(a p) c -> p a c", p=128), [R_h], [R_hgs[Gn % 2]])
            hg = hgs[G % 2]
            R_hg = R_hgs[G % 2]
            for qs in range(4):
                b = qs % 2
                nt_A(None, qs, b, hg[:, qs, :], R_hg, load=False)
                nt_B(qs, b, g_ffn, hT, R_hT, (2 * b, 2 * b + 1))
            for j in range(22):
                w = wi % 4
                wi += 1
                K.dma("sp", wr[w], wgu_s[j], [R_wgu], [wr_r[w]])
                e = j % 2
                bg, bu = (4, 5) if e == 0 else (6, 7)
                for i, bnk in enumerate((bg, bu)):
                    for k in range(8):
                        K.mm(bk(bnk), wr[w][:, i, k, :], hT[:, k, :], k == 0, k == 7, [wr_r[w], R_hT], [bres[bnk]])
                K.act(sgl[e], bk(bg), AF.Silu, [bres[bg]], [sgl_r[e]])
                K.tt("dve", actT[:, j, :], sgl[e], bk(bu), ALU.mult, [sgl_r[e], bres[bu]], [R_act])
            for qs in range(4):
                t = G * 4 + qs
                b = qs % 2
                for n2 in range(2):
                    bnk = 2 * b + n2
                    for j in range(22):
                        K.mm(bk(bnk), actT[:, j, qs * 128:(qs + 1) * 128], Wd[:, j, n2 * 512:(n2 + 1) * 512], j == 0, j == 21,
                             [R_act, R_Wd], [bres[bnk]])
                    K.tt("dve", yo[b][:, n2 * 512:(n2 + 1) * 512], bk(bnk), hg[:, qs, n2 * 512:(n2 + 1) * 512], ALU.add,
                         [bres[bnk], R_hg], [yo_r[b]])
                K.act(junk, yo[b], AF.Square, [yo_r[b]], [junk_r], accum=st1[b][:, 0:1])
                K.act(st1[b][:, 1:2], st1[b][:, 0:1], AF.Ln, [junk_r, R_const], [st1_r[b]], scale=1.0 / D, bias=eps_ap)
                K.act(st1[b][:, 2:3], st1[b][:, 1:2], AF.Exp, [st1_r[b]], [st1_r[b]], scale=-0.5)
                K.stt(yo[b], yo[b], st1[b][:, 2:3], gfin_b[:], ALU.mult, ALU.mult, [yo_r[b], st1_r[b], R_const], [yo_r[b]])
                K.dma("pool", y_d[t * 128:(t + 1) * 128, :], yo[b], [yo_r[b]], [R_y])
        return _finish()


_PROG = {}


def kernel(**inputs):
    x = np.ascontiguousarray(np.asarray(inputs["x"], dtype=np.float32))
    B, SEQ, _ = x.shape
    pos = np.asarray(inputs["positions"]).astype(np.int32)
    f = lambda k, shp: np.ascontiguousarray(np.asarray(inputs[k], dtype=np.float32).reshape(shp))
    common = {
        "attn_norm_g": f("attn_norm_g", (1, D)), "w_in": f("w_in", (D, INDIM)), "diff_lambda": f("diff_lambda", (1, 256)),
        "diff_subln_g": f("diff_subln_g", (1, 128)), "cmp_pe_k": f("cmp_pe_k", (32, 64)), "cmp_pe_v": f("cmp_pe_v", (32, 64)),
        "cmp_k_w1": f("cmp_k_w1", (2048, 256)), "cmp_k_w2": f("cmp_k_w2", (256, 64)),
        "cmp_v_w1": f("cmp_v_w1", (2048, 256)), "cmp_v_w2": f("cmp_v_w2", (256, 64)),
        "w_branch_a": f("w_branch_a", (512, D)), "w_branch_b": f("w_branch_b", (512, D)), "w_out": f("w_out", (D, D)),
        "ffn_norm_g": f("ffn_norm_g", (1, D)), "w_gate": f("w_gate", (D, DFF)), "w_up": f("w_up", (D, DFF)),
        "w_down": f("w_down", (DFF, D)), "final_norm_g": f("final_norm_g", (1, D)),
    }
    common.update(host_consts(SEQ))
    if SEQ not in _PROG:
        _PROG[SEQ] = build_program(SEQ)
    nc = _PROG[SEQ]
    in_maps = [dict(common, x=x[b], pos=np.ascontiguousarray(pos[b].reshape(SEQ, 1))) for b in range(B)]
    res = run_bass_kernel_spmd(nc, in_maps, core_ids=list(range(B)))
    return np.stack([np.asarray(r["y"], dtype=np.float32) for r in res.results], axis=0)
```
